# Optimizing a Trainium2 kernel written in Bass

```python
import math
import jax
import jax.numpy as jnp
from jax import lax
import numpy as np

D_MODEL = 2048
BATCH = 4
SEQ = 2048
DEPTH = 2

HEAD_DIM = 128
N_HEADS = D_MODEL // HEAD_DIM
GLA_HEADS = N_HEADS // 4
DIFF_HEADS = (N_HEADS - GLA_HEADS) // 2
MOBA_HEADS = N_HEADS - GLA_HEADS - DIFF_HEADS
DIFF_QK_DIM = HEAD_DIM // 2
DIFF_Q_BLOCK = 128
GLA_DK = HEAD_DIM // 2
GLA_DV = HEAD_DIM
GLA_GATE_RANK = 16
GLA_TAU = 16.0
GLA_CHUNK = 64
MOBA_BLOCK = 256
MOBA_TOPK = 3
MOBA_Q_CHUNK = 32
ROPE_THETA = 10000.0
D_FF = ((8 * D_MODEL // 3 + 255) // 256) * 256
PLE_DIM = 256
LN_EPS = 1e-5
DEEPNORM_ALPHA = (2 * DEPTH) ** 0.25
DEEPNORM_BETA = (8 * DEPTH) ** -0.25

_IN_WIDTHS = (
    DIFF_HEADS * 2 * DIFF_QK_DIM,
    DIFF_HEADS * 2 * DIFF_QK_DIM,
    DIFF_HEADS * HEAD_DIM,
    GLA_HEADS * GLA_DK,
    GLA_HEADS * GLA_DK,
    GLA_HEADS * GLA_DV,
    GLA_HEADS * GLA_DV,
    GLA_GATE_RANK,
    MOBA_HEADS * HEAD_DIM,
    MOBA_HEADS * HEAD_DIM,
    MOBA_HEADS * HEAD_DIM,
)
D_IN = int(sum(_IN_WIDTHS))
IN_SPLITS = tuple(int(v) for v in np.cumsum(_IN_WIDTHS)[:-1])
MIX_WIDTH = (DIFF_HEADS + GLA_HEADS + MOBA_HEADS) * HEAD_DIM

kernel_name = "hymba_style_diff_gla_moba_macaron_deepnorm"


def layer_norm(x, g, b):
    xf = x.astype(jnp.float32)
    mu = jnp.mean(xf, axis=-1, keepdims=True)
    var = jnp.mean(jnp.square(xf - mu), axis=-1, keepdims=True)
    return ((xf - mu) * lax.rsqrt(var + LN_EPS) * g + b).astype(x.dtype)


def rms_norm(x, g):
    xf = x.astype(jnp.float32)
    return (xf * lax.rsqrt(jnp.mean(xf * xf, axis=-1, keepdims=True) + LN_EPS) * g).astype(x.dtype)


def swiglu(h, w_gate, w_up, w_down):
    return (jax.nn.silu(h @ w_gate) * (h @ w_up)) @ w_down


def to_heads(t, n, d):
    b, s, _ = t.shape
    return t.reshape(b, s, n, d).transpose(0, 2, 1, 3)


def rope(t, positions):
    d = t.shape[-1]
    inv = ROPE_THETA ** (-jnp.arange(0, d, 2, dtype=jnp.float32) / d)
    ang = positions.astype(jnp.float32)[:, None, :, None] * inv
    cos, sin = jnp.cos(ang), jnp.sin(ang)
    tf = t.astype(jnp.float32)
    t1, t2 = tf[..., : d // 2], tf[..., d // 2:]
    return jnp.concatenate([t1 * cos - t2 * sin, t2 * cos + t1 * sin], axis=-1).astype(t.dtype)


def diff_attention(q1, q2, k1, k2, v, lam):
    b, h, s, dqk = q1.shape
    scale = dqk ** -0.5
    key_pos = jnp.arange(s)

    def block(i):
        qs = i * DIFF_Q_BLOCK
        q1b = lax.dynamic_slice_in_dim(q1, qs, DIFF_Q_BLOCK, axis=2)
        q2b = lax.dynamic_slice_in_dim(q2, qs, DIFF_Q_BLOCK, axis=2)
        mask = (qs + jnp.arange(DIFF_Q_BLOCK))[:, None] >= key_pos[None, :]
        s1 = jnp.einsum('bhqd,bhkd->bhqk', q1b, k1).astype(jnp.float32) * scale
        s2 = jnp.einsum('bhqd,bhkd->bhqk', q2b, k2).astype(jnp.float32) * scale
        a1 = jax.nn.softmax(jnp.where(mask, s1, -jnp.inf), axis=-1)
        a2 = jax.nn.softmax(jnp.where(mask, s2, -jnp.inf), axis=-1)
        a = (a1 - lam * a2).astype(v.dtype)
        return jnp.einsum('bhqk,bhkd->bhqd', a, v)

    out = lax.map(block, jnp.arange(s // DIFF_Q_BLOCK))
    return jnp.moveaxis(out, 0, 2).reshape(b, h, s, v.shape[-1])


def gla_chunked(q, k, v, log_a):
    dtype = v.dtype
    q, k, v, log_a = (t.astype(jnp.float32) for t in (q, k, v, log_a))
    b, h, s, dk = q.shape
    dv = v.shape[-1]
    n = s // GLA_CHUNK
    chunk = lambda t: jnp.moveaxis(t.reshape(b, h, n, GLA_CHUNK, t.shape[-1]), 2, 0)
    causal = jnp.tril(jnp.ones((GLA_CHUNK, GLA_CHUNK), dtype=bool))

    def step(state, inp):
        qc, kc, vc, gc = inp
        cum = jnp.cumsum(gc, axis=-2)
        inter = jnp.einsum('bhik,bhkv->bhiv', qc * jnp.exp(cum), state)
        rel = cum[:, :, :, None, :] - cum[:, :, None, :, :]
        decay = jnp.exp(jnp.where(causal[:, :, None], rel, -jnp.inf))
        att = jnp.einsum('bhik,bhjk,bhijk->bhij', qc, kc, decay)
        out = inter + jnp.einsum('bhij,bhjv->bhiv', att, vc)
        last = cum[:, :, -1:, :]
        new_state = jnp.exp(last[:, :, 0, :])[..., None] * state + jnp.einsum(
            'bhjk,bhjv->bhkv', kc * jnp.exp(last - cum), vc)
        return new_state, out

    state0 = jnp.zeros((b, h, dk, dv), jnp.float32)
    _, out = lax.scan(step, state0, (chunk(q), chunk(k), chunk(v), chunk(log_a)))
    return jnp.moveaxis(out, 0, 2).reshape(b, h, s, dv).astype(dtype)


def moba_attention(q, k, v):
    b, h, s, d = q.shape
    scale = d ** -0.5
    nb = -(-s // MOBA_BLOCK)
    pad = nb * MOBA_BLOCK - s
    kp = jnp.pad(k, ((0, 0), (0, 0), (0, pad), (0, 0)))
    vp = jnp.pad(v, ((0, 0), (0, 0), (0, pad), (0, 0)))
    kb = kp.reshape(b, h, nb, MOBA_BLOCK, d)
    vb = vp.reshape(b, h, nb, MOBA_BLOCK, d)
    kmean = jnp.mean(kb.astype(jnp.float32), axis=3)
    kt = min(MOBA_TOPK, nb)
    bi = jnp.arange(b)[:, None, None, None]
    hi = jnp.arange(h)[None, :, None, None]

    def chunk(c):
        qs = c * MOBA_Q_CHUNK
        own = qs // MOBA_BLOCK
        qc = lax.dynamic_slice_in_dim(q, qs, MOBA_Q_CHUNK, axis=2)
        gate = jnp.einsum('bhqd,bhnd->bhqn', qc.astype(jnp.float32), kmean)
        gate = jnp.where(jnp.arange(nb) < own, gate, -jnp.inf)
        _, idx = lax.top_k(gate, kt)
        valid = jnp.repeat(jnp.arange(kt) < own, MOBA_BLOCK)
        ksel = kb[bi, hi, idx]
        vsel = vb[bi, hi, idx]
        s_sel = jnp.einsum('bhqd,bhqrkd->bhqrk', qc, ksel).astype(jnp.float32) * scale
        s_sel = jnp.where(valid, s_sel.reshape(b, h, MOBA_Q_CHUNK, kt * MOBA_BLOCK), -jnp.inf)
        kown = lax.dynamic_slice_in_dim(kp, own * MOBA_BLOCK, MOBA_BLOCK, axis=2)
        vown = lax.dynamic_slice_in_dim(vp, own * MOBA_BLOCK, MOBA_BLOCK, axis=2)
        own_mask = (own * MOBA_BLOCK + jnp.arange(MOBA_BLOCK))[None, :] <= (qs + jnp.arange(MOBA_Q_CHUNK))[:, None]
        s_own = jnp.einsum('bhqd,bhkd->bhqk', qc, kown).astype(jnp.float32) * scale
        s_own = jnp.where(own_mask, s_own, -jnp.inf)
        probs = jax.nn.softmax(jnp.concatenate([s_sel, s_own], axis=-1), axis=-1).astype(v.dtype)
        p_sel = probs[..., : kt * MOBA_BLOCK].reshape(b, h, MOBA_Q_CHUNK, kt, MOBA_BLOCK)
        p_own = probs[..., kt * MOBA_BLOCK:]
        return (jnp.einsum('bhqrk,bhqrkd->bhqd', p_sel, vsel)
                + jnp.einsum('bhqk,bhkd->bhqd', p_own, vown))

    out = lax.map(chunk, jnp.arange(s // MOBA_Q_CHUNK))
    return jnp.moveaxis(out, 0, 2).reshape(b, h, s, d)


def token_mix(h, positions, w_in, w_out, diff_lambda, diff_norm_g, gla_gate_up, gla_gate_b,
              gla_norm_g, layer):
    b, s, _ = h.shape
    proj = h @ w_in
    dq, dk, dv, gq, gk, gv, gr, gg, mq, mk, mv = jnp.split(proj, IN_SPLITS, axis=-1)

    lam_init = 0.8 - 0.6 * math.exp(-0.3 * layer)
    lf = diff_lambda.astype(jnp.float32)
    lam = jnp.exp(jnp.sum(lf[0] * lf[1])) - jnp.exp(jnp.sum(lf[2] * lf[3])) + lam_init
    dq = dq.reshape(b, s, DIFF_HEADS, 2, DIFF_QK_DIM).transpose(0, 2, 3, 1, 4)
    dk = dk.reshape(b, s, DIFF_HEADS, 2, DIFF_QK_DIM).transpose(0, 2, 3, 1, 4)
    q1, q2 = rope(dq[:, :, 0], positions), rope(dq[:, :, 1], positions)
    k1, k2 = rope(dk[:, :, 0], positions), rope(dk[:, :, 1], positions)
    o_diff = diff_attention(q1, q2, k1, k2, to_heads(dv, DIFF_HEADS, HEAD_DIM), lam)
    o_diff = rms_norm(o_diff.transpose(0, 2, 1, 3), diff_norm_g) * (1.0 - lam_init)
    o_diff = o_diff.reshape(b, s, DIFF_HEADS * HEAD_DIM)

    log_a = jax.nn.log_sigmoid(gg @ gla_gate_up + gla_gate_b) / GLA_TAU
    o_gla = gla_chunked(to_heads(gq, GLA_HEADS, GLA_DK) * (GLA_DK ** -0.5),
                        to_heads(gk, GLA_HEADS, GLA_DK),
                        to_heads(gv, GLA_HEADS, GLA_DV),
                        to_heads(log_a, GLA_HEADS, GLA_DK))
    o_gla = rms_norm(o_gla.transpose(0, 2, 1, 3), gla_norm_g).reshape(b, s, GLA_HEADS * GLA_DV)
    o_gla = o_gla * jax.nn.silu(gr)

    o_moba = moba_attention(rope(to_heads(mq, MOBA_HEADS, HEAD_DIM), positions),
                            rope(to_heads(mk, MOBA_HEADS, HEAD_DIM), positions),
                            to_heads(mv, MOBA_HEADS, HEAD_DIM))
    o_moba = o_moba.transpose(0, 2, 1, 3).reshape(b, s, MOBA_HEADS * HEAD_DIM)

    return jnp.concatenate([o_diff, o_gla, o_moba], axis=-1) @ w_out


def setup_inputs(seed: int = 0) -> dict:
    key = jax.random.key(seed)
    ks = jax.random.split(key, 21)
    nrm = lambda k, shape, scale: jax.random.normal(k, shape, jnp.float32) * scale
    beta = DEEPNORM_BETA
    return {
        "x": nrm(ks[0], (BATCH, SEQ, D_MODEL), 1.0),
        "p": nrm(ks[1], (DEPTH, BATCH, SEQ, PLE_DIM), 1.0),
        "positions": jnp.broadcast_to(jnp.arange(SEQ, dtype=jnp.int32)[None, :], (BATCH, SEQ)),
        "w_in": nrm(ks[2], (DEPTH, D_MODEL, D_IN), D_MODEL ** -0.5),
        "w_out": nrm(ks[3], (DEPTH, MIX_WIDTH, D_MODEL), beta * MIX_WIDTH ** -0.5),
        "diff_lambda": nrm(ks[4], (DEPTH, 4, DIFF_QK_DIM), 0.1),
        "diff_norm_g": 1.0 + nrm(ks[5], (DEPTH, HEAD_DIM), 0.02),
        "gla_gate_up": nrm(ks[6], (DEPTH, GLA_GATE_RANK, GLA_HEADS * GLA_DK), GLA_GATE_RANK ** -0.5),
        "gla_gate_b": nrm(ks[7], (DEPTH, GLA_HEADS * GLA_DK), 0.1),
        "gla_norm_g": 1.0 + nrm(ks[8], (DEPTH, GLA_DV), 0.02),
        "ffn1_gate": nrm(ks[9], (DEPTH, D_MODEL, D_FF), D_MODEL ** -0.5),
        "ffn1_up": nrm(ks[10], (DEPTH, D_MODEL, D_FF), D_MODEL ** -0.5),
        "ffn1_down": nrm(ks[11], (DEPTH, D_FF, D_MODEL), beta * D_FF ** -0.5),
        "ffn2_gate": nrm(ks[12], (DEPTH, D_MODEL, D_FF), D_MODEL ** -0.5),
        "ffn2_up": nrm(ks[13], (DEPTH, D_MODEL, D_FF), D_MODEL ** -0.5),
        "ffn2_down": nrm(ks[14], (DEPTH, D_FF, D_MODEL), beta * D_FF ** -0.5),
        "w_pe": nrm(ks[15], (DEPTH, PLE_DIM, D_MODEL), beta * PLE_DIM ** -0.5),
        "w_pg": nrm(ks[16], (DEPTH, D_MODEL, D_MODEL), D_MODEL ** -0.5),
        "ln_g": 1.0 + nrm(ks[17], (DEPTH, 4, D_MODEL), 0.02),
        "ln_b": nrm(ks[18], (DEPTH, 4, D_MODEL), 0.02),
    }


def reference(x, p, positions, w_in, w_out, diff_lambda, diff_norm_g, gla_gate_up, gla_gate_b,
              gla_norm_g, ffn1_gate, ffn1_up, ffn1_down, ffn2_gate, ffn2_up, ffn2_down,
              w_pe, w_pg, ln_g, ln_b):
    a = DEEPNORM_ALPHA
    for i in range(DEPTH):
        x = layer_norm(a * x + 0.5 * swiglu(x, ffn1_gate[i], ffn1_up[i], ffn1_down[i]), ln_g[i, 0], ln_b[i, 0])
        x = layer_norm(a * x + token_mix(x, positions, w_in[i], w_out[i], diff_lambda[i], diff_norm_g[i],
                                         gla_gate_up[i], gla_gate_b[i], gla_norm_g[i], i),
                       ln_g[i, 1], ln_b[i, 1])
        x = layer_norm(a * x + 0.5 * swiglu(x, ffn2_gate[i], ffn2_up[i], ffn2_down[i]), ln_g[i, 2], ln_b[i, 2])
        e = (p[i] @ w_pe[i]) * jax.nn.sigmoid(x @ w_pg[i])
        x = layer_norm(a * x + e, ln_g[i, 3], ln_b[i, 3])
    return x
```

```python
import math
import types
from contextlib import ExitStack
import numpy as np
import concourse.bass as bass
import concourse.mybir as mybir
from concourse.bass_utils import run_bass_kernel_spmd

F32 = mybir.dt.float32
BF16 = mybir.dt.bfloat16
I32 = mybir.dt.int32
AF = mybir.ActivationFunctionType
ALU = mybir.AluOpType
AX = mybir.AxisListType

D = 2048
DFF = 5632
SEQ = 2048
NB = 4
DEPTH = 2
NCORES = 8
TOK = 1024
ALPHA = (2 * DEPTH) ** 0.25
EPS = 1e-5
DIN = 6160


class Buf:
    __slots__ = ("name", "lw", "rd", "excl")

    def __init__(self, name, excl=False):
        self.name = name
        self.lw = None
        self.rd = {}
        self.excl = excl


def _freeze(fn):
    if getattr(fn, "__closure__", None) is None:
        return fn
    cells = []
    for c in fn.__closure__:
        try:
            cells.append(types.CellType(c.cell_contents))
        except ValueError:
            cells.append(c)
    return types.FunctionType(fn.__code__, fn.__globals__, fn.__name__, fn.__defaults__, tuple(cells))


def _flat(x):
    for b in x:
        if isinstance(b, (tuple, list)):
            yield from _flat(b)
        else:
            yield b


class Op:
    __slots__ = ("eng", "fn", "deps", "sig", "cnt", "dma", "key", "ndma", "chan", "cc")


class Prog:
    ENGS = ("pe", "dve", "act", "pool", "sp")

    def __init__(self):
        self.ops = []
        self.bar = []

    def barrier(self):
        last = {}
        for op in self.ops:
            last[op.chan] = op
        self.bar = list(last.values())

    def add(self, eng, fn, reads=(), writes=(), dma=False, key=None, ndma=1, cc=False):
        op = Op()
        op.cc = cc
        op.eng = eng
        op.fn = _freeze(fn)
        op.sig = False
        op.cnt = 0
        op.dma = dma
        op.key = key
        op.ndma = ndma
        op.chan = ("dma", key) if dma else eng
        deps = {}

        def need(d):
            if d is None or d is op:
                return
            c = d.chan
            if c not in deps or deps[c][0] < d.cnt:
                deps[c] = (d.cnt, d)

        op.cnt = len(self.ops)
        reads = tuple(_flat(reads))
        writes = tuple(_flat(writes))
        for d in self.bar:
            need(d)
        writes = tuple(writes) + tuple(b for b in reads if b.excl)
        for b in reads:
            need(b.lw)
        for b in writes:
            need(b.lw)
            for r in b.rd.values():
                need(r)
        for b in reads:
            b.rd[op.chan] = op
        for b in writes:
            b.lw = op
            b.rd = {}
        op.deps = [d for (_, d) in deps.values()]
        if not dma and eng == "pe":
            op.deps = [d for d in op.deps if d.chan != "pe"]
        for d in op.deps:
            d.sig = True
        self.ops.append(op)
        return op

    def emit(self, nc, stack):
        cnt = {}
        for op in self.ops:
            if op.dma:
                cnt[op.chan] = cnt.get(op.chan, 0) + (1 if op.cc else 16 * op.ndma)
                op.cnt = cnt[op.chan]
            elif op.sig:
                cnt[op.chan] = cnt.get(op.chan, 0) + 1
                op.cnt = cnt[op.chan]
            else:
                op.cnt = None
        sems = {}
        for c in cnt:
            nm = ("s_" + (c if isinstance(c, str) else "d_" + str(c[1]))).replace(":", "_")
            sems[c] = stack.enter_context(nc.semaphore(nm))
        self.nsems = len(sems)
        self.final_cnt = dict(cnt)
        block = stack.enter_context(nc.Block())
        streams = {e: [op for op in self.ops if op.eng == e] for e in self.ENGS}

        def run(e, h):
            waited = {}
            for op in streams[e]:
                for d in op.deps:
                    if waited.get(d.chan, 0) < d.cnt:
                        h.wait_ge(sems[d.chan], d.cnt)
                        waited[d.chan] = d.cnt
                ins = op.fn(h)
                if op.cc:
                    ins.then_inc(sems[op.chan])
                elif op.dma:
                    if not isinstance(ins, (list, tuple)):
                        ins = [ins]
                    assert len(ins) == op.ndma
                    for i in ins:
                        i.then_inc(sems[op.chan], 16)
                elif op.sig:
                    ins.then_inc(sems[op.chan], 1)
            if e == "sp":
                for c, v in cnt.items():
                    if waited.get(c, 0) < v:
                        h.wait_ge(sems[c], v)

        @block.tensor
        def _(h):
            run("pe", h)

        @block.vector
        def _(h):
            run("dve", h)

        @block.scalar
        def _(h):
            run("act", h)

        @block.gpsimd
        def _(h):
            run("pool", h)

        @block.sync
        def _(h):
            run("sp", h)


class Ctx:
    def __init__(self):
        self.nc = bass.Bass("TRN2", target_bir_lowering=False)
        self.P = Prog()
        self.stack = ExitStack()
        self.stage = None
        self.n = 0
        self.nstage = 0

    def begin_stage(self):
        self.stage = ExitStack()
        self.nstage += 1

    def end_stage(self):
        self.stage.close()
        self.stage = None
        self.P.barrier()

    def sb(self, shape, dt, name=None):
        self.n += 1
        st = self.stage if self.stage is not None else self.stack
        return st.enter_context(self.nc.sbuf_tensor(f"sb{self.nstage}_" + (name or f"t{self.n}"), list(shape), dt))

    def ps(self, shape, dt=F32, name=None):
        self.n += 1
        st = self.stage if self.stage is not None else self.stack
        return st.enter_context(self.nc.psum_tensor(f"ps{self.nstage}_" + (name or f"p{self.n}"), list(shape), dt))

    def din(self, name, shape, dt=F32):
        return self.nc.dram_tensor(name, list(shape), dt, kind="ExternalInput").ap()

    def dout(self, name, shape, dt=F32):
        return self.nc.dram_tensor(name, list(shape), dt, kind="ExternalOutput").ap()

    def dscratch(self, name, shape, dt=F32):
        return self.nc.dram_tensor(name, list(shape), dt).ap()

    def finish(self):
        if self.stage is not None:
            self.end_stage()
        self.P.emit(self.nc, self.stack)
        self.stack.close()
        self.nc._prog = self.P
        return self.nc


def pipeline(units, ph1, ph2, lag=2):
    n = len(units)
    for i in range(n + lag):
        if i < n:
            ph1(units[i])
        if i >= lag:
            ph2(units[i - lag])


class RowStage:
    def __init__(self, cx, ntok):
        self.cx = cx
        self.ntok = ntok
        self.NT = ntok // 128
        self.xacc = cx.sb([128, self.NT, D], F32, "xacc")
        self.xacc_cb = [[Buf(f"xacc{t}_{c}") for c in range(4)] for t in range(self.NT)]
        self.xacc_b = [tuple(self.xacc_cb[t]) for t in range(self.NT)]
        self.acc_tmp = [cx.sb([128, 512], F32, f"acct{k}") for k in range(2)]
        self.acc_tmp_b = [Buf(f"acct{k}") for k in range(2)]
        self.xT = cx.sb([128, 16, ntok], BF16, "xT")
        self.xT_b = [Buf(f"xT{t}") for t in range(self.NT)]
        self.ident = cx.sb([128, 128], F32, "ident")
        self.ident_b = Buf("ident")
        self.pbank = [cx.ps([128, 512], F32, f"bank{i}") for i in range(8)]
        self.pbank_b = [Buf(f"bank{i}", excl=True) for i in range(8)]
        self.rr = 0
        self.wbuf = [cx.sb([128, 8192], BF16, f"wbuf{s}") for s in range(2)]
        self.wbuf_b = [Buf(f"wbuf{s}") for s in range(2)]
        self.wdb = [cx.sb([128, 2, D], BF16, f"wdb{s}") for s in range(2)]
        self.wdb_b = [Buf(f"wdb{s}") for s in range(2)]
        self.actT = [cx.sb([128, 2, ntok], BF16, f"actT{s}") for s in range(2)]
        self.actT_b = [Buf(f"actT{s}") for s in range(2)]
        self.sg = [cx.sb([128, 512], F32, f"sg{s}") for s in range(2)]
        self.sg_b = [Buf(f"sg{s}") for s in range(2)]
        self.gbc = cx.sb([128, 2, D], F32, "gbc")
        self.gb_b = Buf("gbc")
        self.lntmps = [(cx.sb([128, 4, 6], F32, f"stats{k}"), cx.sb([128, 2], F32, f"mv{k}"), cx.sb([128, 1], F32, f"sc{k}"),
                        cx.sb([128, 1], F32, f"nb{k}"), Buf(f"stat{k}")) for k in range(3)]
        self.xin = [cx.sb([128, D], F32, f"xin{s}") for s in range(2)]
        self.xin_b = [Buf(f"xin{s}") for s in range(2)]
        self.wslot = 0

    def load_ln(self, lng, lnb, idx):
        gbc = self.gbc
        self.cx.P.add("sp", lambda h: [h.dma_start(out=gbc[:, 0, :], in_=lng[idx:idx + 1, :].partition_broadcast(128)),
                                       h.dma_start(out=gbc[:, 1, :], in_=lnb[idx:idx + 1, :].partition_broadcast(128))],
                      writes=(self.gb_b,), dma=True, key="gbc", ndma=2)

    def ln_all(self, zscale, out_fn=None):
        def ph1(tt):
            ln_stats(self.cx, self, tt, zscale, self.lntmps[tt % 3])

        def ph2(tt):
            if out_fn is None:
                o, ob = self.xacc[:, tt, :], self.xacc_b[tt]
            else:
                o, ob = out_fn(tt)
            ln_apply(self.cx, self, tt, self.gbc[:, 0, :], self.gbc[:, 1, :], self.gb_b, o, ob, self.lntmps[tt % 3])
            if out_fn is not None:
                out_fn(tt, done=True)

        pipeline(list(range(self.NT)), ph1, ph2, lag=2)


def load_const(cx, dst, dst_b, src_ap, key):
    cx.P.add("sp", lambda h: h.dma_start(out=dst, in_=src_ap), reads=(), writes=(dst_b,), dma=True, key=key)


def build_xT(cx, st, tt, src, src_b, banks=(4, 5, 6, 7)):
    P = cx.P
    for g in range(4):
        bk = banks[(st.rr) % len(banks)]
        st.rr += 1
        pb = st.pbank[bk]
        for j in range(4):
            kc = g * 4 + j
            P.add("pe", lambda h, pb=pb, j=j, kc=kc: h.transpose(out=pb[:, j * 128:(j + 1) * 128], in_=src[:, kc * 128:(kc + 1) * 128], identity=st.ident[:, :]),
                  reads=(src_b, st.ident_b), writes=(st.pbank_b[bk],))
        eng = "act" if g % 2 == 0 else "dve"
        o = st.xT[:, g * 4:(g + 1) * 4, tt * 128:(tt + 1) * 128]
        i = pb[:, :].rearrange("p (j t) -> p j t", j=4)
        if eng == "act":
            P.add("act", lambda h, o=o, i=i: h.copy(out=o, in_=i), reads=(st.pbank_b[bk],), writes=(st.xT_b[tt],))
        else:
            P.add("dve", lambda h, o=o, i=i: h.tensor_copy(out=o, in_=i), reads=(st.pbank_b[bk],), writes=(st.xT_b[tt],))


def ln_stats(cx, st, tt, zscale, tmp):
    P = cx.P
    stats, mv, sc, nb, stat_b = tmp
    P.add("dve", lambda h: [h.bn_stats(out=stats[:, c, :], in_=st.xacc[:, tt, c * 512:(c + 1) * 512]) for c in range(4)][-1],
          reads=(st.xacc_b[tt],), writes=(stat_b,))
    P.add("dve", lambda h: h.bn_aggr(out=mv[:, :], in_=stats[:, :, :]), reads=(stat_b,), writes=(stat_b,))
    P.add("dve", lambda h: h.tensor_scalar_add(out=sc[:, :], in0=mv[:, 1:2], scalar1=EPS / (zscale * zscale)),
          reads=(stat_b,), writes=(stat_b,))
    P.add("act", lambda h: h.sqrt(out=sc[:, :], in_=sc[:, :]), reads=(stat_b,), writes=(stat_b,))
    P.add("dve", lambda h: h.reciprocal(out=sc[:, :], in_=sc[:, :]), reads=(stat_b,), writes=(stat_b,))
    P.add("dve", lambda h: h.scalar_tensor_tensor(out=nb[:, :], in0=mv[:, 0:1], scalar=-1.0, in1=sc[:, :], op0=ALU.mult, op1=ALU.mult),
          reads=(stat_b,), writes=(stat_b,))


def ln_apply(cx, st, tt, g_bc, b_bc, gb_b, out_ap, out_b, tmp):
    P = cx.P
    xa = st.xacc[:, tt, :]
    stats, mv, sc, nb, stat_b = tmp
    P.add("act", lambda h: h.activation(out=xa, in_=xa, func=AF.Identity, bias=nb[:, 0:1], scale=sc[:, 0:1]),
          reads=(stat_b, st.xacc_b[tt]), writes=(st.xacc_b[tt],))
    P.add("pool", lambda h: h.tensor_tensor(out=xa, in0=xa, in1=g_bc, op=ALU.mult), reads=(st.xacc_b[tt], gb_b), writes=(st.xacc_b[tt],))
    P.add("dve", lambda h: h.tensor_tensor(out=out_ap, in0=xa, in1=b_bc, op=ALU.add), reads=(st.xacc_b[tt], gb_b), writes=(out_b,))


def ffn_stage(cx, st, wg, wu, wd, FB=256):
    P = cx.P
    nblk = DFF // FB
    CPB = FB // 128
    wgu = [w[:, :].rearrange("p (w k f) -> p w k f", w=2, k=16) for w in st.wbuf]
    wgu_b = st.wbuf_b
    wdb, wdb_b = st.wdb, st.wdb_b
    actT, actT_b = st.actT, st.actT_b
    sg, sg_b = st.sg, st.sg_b
    wgv = wg.rearrange("(kc p) f -> p kc f", p=128)
    wuv = wu.rearrange("(kc p) f -> p kc f", p=128)
    wdv = wd.rearrange("(c p) n -> p c n", p=128)
    NTG = st.ntok // 512
    allxT = tuple(st.xT_b)
    gi = 0
    di = 0

    def load_gu(b):
        s = b % 2
        P.add("pool", lambda h: [h.dma_start(out=wgu[s][:, 0, :, :], in_=wgv[:, :, b * FB:(b + 1) * FB]),
                                 h.dma_start(out=wgu[s][:, 1, :, :], in_=wuv[:, :, b * FB:(b + 1) * FB])],
              writes=(wgu_b[s],), dma=True, key=f"wbuf{s}", ndma=2)

    def load_d(b):
        s = b % 2
        P.add("pool", lambda h: h.dma_start(out=wdb[s][:, :, :], in_=wdv[:, b * CPB:(b + 1) * CPB, :]),
              writes=(wdb_b[s],), dma=True, key=f"wdb{s}")

    def gateup(b):
        nonlocal gi
        s = b % 2
        for c in range(CPB):
            for tg in range(NTG):
                bg = gi % 2
                bu = 2 + gi % 2
                gi += 1
                for (w, bk) in ((0, bg), (1, bu)):
                    for kc in range(16):
                        P.add("pe", lambda h, w=w, bk=bk, kc=kc, c=c, tg=tg: h.matmul(
                            st.pbank[bk][:, :], lhsT=wgu[s][:, w, kc, c * 128:(c + 1) * 128],
                            rhs=st.xT[:, kc, tg * 512:(tg + 1) * 512], start=(kc == 0), stop=(kc == 15)),
                            reads=(wgu_b[s],) + allxT, writes=(st.pbank_b[bk],))
                sgi = gi % 2
                P.add("act", lambda h, bg=bg, sgi=sgi: h.activation(out=sg[sgi][:, :], in_=st.pbank[bg][:, :], func=AF.Silu),
                      reads=(st.pbank_b[bg],), writes=(sg_b[sgi],))
                P.add("dve", lambda h, bu=bu, sgi=sgi, c=c, tg=tg: h.tensor_tensor(
                    out=actT[s][:, c, tg * 512:(tg + 1) * 512], in0=sg[sgi][:, :], in1=st.pbank[bu][:, :], op=ALU.mult),
                    reads=(sg_b[sgi], st.pbank_b[bu]), writes=(actT_b[s],))

    def down(b):
        nonlocal di
        s = b % 2
        for tt in range(st.NT):
            for cg in range(4):
                bk = 4 + di % 4
                di += 1
                for c in range(CPB):
                    P.add("pe", lambda h, bk=bk, c=c, tt=tt, cg=cg: h.matmul(
                        st.pbank[bk][:, :], lhsT=actT[s][:, c, tt * 128:(tt + 1) * 128],
                        rhs=wdb[s][:, c, cg * 512:(cg + 1) * 512], start=(c == 0), stop=(c == CPB - 1)),
                        reads=(actT_b[s], wdb_b[s]), writes=(st.pbank_b[bk],))
                xa = st.xacc[:, tt, cg * 512:(cg + 1) * 512]
                xb = st.xacc_cb[tt][cg]
                if di % 3 == 0:
                    k = (di // 3) % 2
                    P.add("act", lambda h, bk=bk, k=k: h.copy(out=st.acc_tmp[k][:, :], in_=st.pbank[bk][:, :]),
                          reads=(st.pbank_b[bk],), writes=(st.acc_tmp_b[k],))
                    P.add("pool", lambda h, xa=xa, k=k: h.tensor_tensor(out=xa, in0=xa, in1=st.acc_tmp[k][:, :], op=ALU.add),
                          reads=(st.acc_tmp_b[k], xb), writes=(xb,))
                else:
                    P.add("dve", lambda h, xa=xa, bk=bk: h.tensor_tensor(out=xa, in0=xa, in1=st.pbank[bk][:, :], op=ALU.add),
                          reads=(st.pbank_b[bk], xb), writes=(xb,))

    load_gu(0)
    load_gu(1)
    load_d(0)
    for b in range(nblk):
        gateup(b)
        if b + 2 < nblk:
            load_gu(b + 2)
        if b >= 1:
            down(b - 1)
        if b + 1 < nblk:
            load_d(b + 1)
    down(nblk - 1)


def load_x_tiles(cx, st, x, scale, with_xT=True):
    P = cx.P
    for tt in range(st.NT):
        s = tt % 2
        P.add("sp", lambda h, s=s, tt=tt: h.dma_start(out=st.xin[s][:, :], in_=x[tt * 128:(tt + 1) * 128, :]),
              writes=(st.xin_b[s],), dma=True, key=f"xin{s}")
        if with_xT:
            build_xT(cx, st, tt, st.xin[s], st.xin_b[s])
        P.add("act", lambda h, s=s, tt=tt: h.mul(out=st.xacc[:, tt, :], in_=st.xin[s][:, :], mul=scale),
              reads=(st.xin_b[s],), writes=(st.xacc_b[tt],))


def refresh_xT(cx, st, scale):
    for tt in range(st.NT):
        build_xT(cx, st, tt, st.xacc[:, tt, :], st.xacc_b[tt])
        if scale != 1.0:
            cx.P.add("act", lambda h, tt=tt: h.mul(out=st.xacc[:, tt, :], in_=st.xacc[:, tt, :], mul=scale),
                     reads=(st.xacc_b[tt],), writes=(st.xacc_b[tt],))


def store_out(cx, st, y, yb=None):
    def out_fn(tt, done=False):
        s = tt % 2
        if done:
            cx.P.add("sp", lambda h: h.dma_start(out=y[tt * 128:(tt + 1) * 128, :], in_=st.xin[s][:, :]),
                     reads=(st.xin_b[s],), dma=True, key=f"yo{s}")
            if yb is not None:
                cx.P.add("pool", lambda h: h.dma_start(out=yb[tt * 128:(tt + 1) * 128, :], in_=st.xin[s][:, :]),
                         reads=(st.xin_b[s],), dma=True, key=f"yb{s}")
            return None
        return st.xin[s][:, :], st.xin_b[s]
    return out_fn


def dense_acc(cx, st, w, lhs_fn, nk, lhs_bufs, post):
    P = cx.P
    wv = w.rearrange("(kc p) n -> p kc n", p=128)
    for cg in range(4):
        s = st.wslot % 2
        st.wslot += 1
        wt = st.wbuf[s][:, 0:nk * 512].rearrange("p (k f) -> p k f", k=nk)
        P.add("pool", lambda h, wt=wt, cg=cg: h.dma_start(out=wt, in_=wv[:, :, cg * 512:(cg + 1) * 512]),
              writes=(st.wbuf_b[s],), dma=True, key=f"wbuf{s}")
        for tt in range(st.NT):
            bk = 4 + st.rr % 4
            st.rr += 1
            for kc in range(nk):
                P.add("pe", lambda h, bk=bk, kc=kc, tt=tt, wt=wt: h.matmul(st.pbank[bk][:, :], lhsT=lhs_fn(kc, tt), rhs=wt[:, kc, :],
                                                                        start=(kc == 0), stop=(kc == nk - 1)),
                      reads=(st.wbuf_b[s],) + tuple(lhs_bufs), writes=(st.pbank_b[bk],))
            post(tt, cg, bk)


def emit_A(cx, x, y, wg, wu, wd, lng, lnb, identd, ntok=TOK, yb=None):
    st = RowStage(cx, ntok)
    load_const(cx, st.ident[:, :], st.ident_b, identd[:, :], "ident")
    st.load_ln(lng, lnb, 0)
    load_x_tiles(cx, st, x, 2.0 * ALPHA)
    ffn_stage(cx, st, wg, wu, wd)
    st.ln_all(0.5, store_out(cx, st, y, yb))


def emit_C(cx, x, oT, pp, wo, wg, wu, wd, wpe, wpg, lng, lnb, identd, y, nxt=None, ntok=TOK, yb=None, pre=None):
    P = cx.P
    with_next = nxt is not None
    odeps = tuple(pre()) if pre else ()
    st = RowStage(cx, ntok)
    load_const(cx, st.ident[:, :], st.ident_b, identd[:, :], "ident")
    st.load_ln(lng, lnb, 1)
    load_x_tiles(cx, st, x, ALPHA, with_xT=False)
    if callable(oT):
        for kc in range(16):
            sl = kc % 2
            P.add("sp", lambda h, kc=kc, sl=sl: h.dma_start(out=st.xin[sl][:, 0:ntok], in_=oT(h, kc)), writes=(st.xin_b[sl],), dma=True, key=f"xin{sl}")
            if kc % 2 == 0:
                P.add("act", lambda h, kc=kc, sl=sl: h.copy(out=st.xT[:, kc, :], in_=st.xin[sl][:, 0:ntok]), reads=(st.xin_b[sl],), writes=tuple(st.xT_b))
            else:
                P.add("dve", lambda h, kc=kc, sl=sl: h.tensor_copy(out=st.xT[:, kc, :], in_=st.xin[sl][:, 0:ntok]), reads=(st.xin_b[sl],), writes=tuple(st.xT_b))
    else:
        oTv = oT.rearrange("(kc p) t -> p kc t", p=128)
        P.add("pool", lambda h: h.dma_start(out=st.xT[:, :, :], in_=oTv), reads=odeps, writes=tuple(st.xT_b), dma=True, key="oT")

    def post_add(tt, cg, bk):
        xa = st.xacc[:, tt, cg * 512:(cg + 1) * 512]
        P.add("dve", lambda h: h.tensor_tensor(out=xa, in0=xa, in1=st.pbank[bk][:, :], op=ALU.add),
              reads=(st.pbank_b[bk], st.xacc_cb[tt][cg]), writes=(st.xacc_cb[tt][cg],))

    dense_acc(cx, st, wo, lambda kc, tt: st.xT[:, kc, tt * 128:(tt + 1) * 128], 16, st.xT_b, post_add)
    st.ln_all(1.0)
    st.load_ln(lng, lnb, 2)
    refresh_xT(cx, st, 2.0 * ALPHA)
    ffn_stage(cx, st, wg, wu, wd)
    st.ln_all(0.5)
    st.load_ln(lng, lnb, 3)
    refresh_xT(cx, st, 1.0)
    pT = cx.sb([128, 2, ntok], BF16, "pT")
    pT_b = Buf("pT")
    pin = cx.sb([128, 256], F32, "pin")
    pin_b = Buf("pin")
    for tt in range(st.NT):
        P.add("sp", lambda h, tt=tt: h.dma_start(out=pin[:, :], in_=pp[tt * 128:(tt + 1) * 128, :]), writes=(pin_b,), dma=True, key="pin")
        bk = 4 + st.rr % 4
        st.rr += 1
        for j in range(2):
            P.add("pe", lambda h, j=j, bk=bk: h.transpose(out=st.pbank[bk][:, j * 128:(j + 1) * 128], in_=pin[:, j * 128:(j + 1) * 128], identity=st.ident[:, :]),
                  reads=(pin_b, st.ident_b), writes=(st.pbank_b[bk],))
        P.add("act", lambda h, tt=tt, bk=bk: h.copy(out=pT[:, :, tt * 128:(tt + 1) * 128], in_=st.pbank[bk][:, 0:256].rearrange("p (j t) -> p j t", j=2)),
              reads=(st.pbank_b[bk],), writes=(pT_b,))
    wpet = cx.sb([128, 2, D], BF16, "wpe")
    wpe_b = Buf("wpe")
    P.add("pool", lambda h: h.dma_start(out=wpet[:, :, :], in_=wpe.rearrange("(c p) n -> p c n", p=128)), writes=(wpe_b,), dma=True, key="wpe")
    et, et_b = st.acc_tmp, st.acc_tmp_b
    ei = [0]

    def post_ple(tt, cg, bk):
        be = ei[0] % 2
        s = ei[0] % 2
        ei[0] += 1
        for c in range(2):
            P.add("pe", lambda h, c=c: h.matmul(st.pbank[be][:, :], lhsT=pT[:, c, tt * 128:(tt + 1) * 128], rhs=wpet[:, c, cg * 512:(cg + 1) * 512],
                                               start=(c == 0), stop=(c == 1)),
                  reads=(pT_b, wpe_b), writes=(st.pbank_b[be],))
        P.add("act", lambda h: h.activation(out=st.sg[s][:, :], in_=st.pbank[bk][:, :], func=AF.Sigmoid),
              reads=(st.pbank_b[bk],), writes=(st.sg_b[s],))
        P.add("dve", lambda h: h.tensor_tensor(out=et[s][:, :], in0=st.sg[s][:, :], in1=st.pbank[be][:, :], op=ALU.mult),
              reads=(st.sg_b[s], st.pbank_b[be]), writes=(et_b[s],))
        xa = st.xacc[:, tt, cg * 512:(cg + 1) * 512]
        P.add("dve", lambda h: h.scalar_tensor_tensor(out=xa, in0=xa, scalar=ALPHA, in1=et[s][:, :], op0=ALU.mult, op1=ALU.add),
              reads=(et_b[s], st.xacc_cb[tt][cg]), writes=(st.xacc_cb[tt][cg],))

    dense_acc(cx, st, wpg, lambda kc, tt: st.xT[:, kc, tt * 128:(tt + 1) * 128], 16, st.xT_b, post_ple)
    if not with_next:
        st.ln_all(1.0, store_out(cx, st, y))
        return
    wg2, wu2, wd2, lng2, lnb2 = nxt
    st.ln_all(1.0)
    st.load_ln(lng2, lnb2, 0)
    refresh_xT(cx, st, 2.0 * ALPHA)
    ffn_stage(cx, st, wg2, wu2, wd2)
    st.ln_all(0.5, store_out(cx, st, y, yb))


NCOL = 3088
PW = 400


def mix_consts():
    c = {}
    c["ident"] = np.eye(128, dtype=np.float32)
    rc = np.zeros((128, 8), np.float32)
    p = np.arange(128)
    rc[:, 0] = 10000.0 ** (-(2.0 * (p % 32)) / 64.0)
    rc[:, 1] = 10000.0 ** (-(2.0 * (p % 64)) / 128.0)
    rc[:, 2] = np.where((p % 64) < 32, -2.0, 2.0)
    rc[:, 3] = np.where(p < 64, -2.0, 2.0)
    rc[:, 6] = math.pi / 2.0
    rc[:, 4] = -math.pi
    rc[:, 5] = EPS
    c["rc"] = rc
    perm = np.zeros((2, 128, 128), np.float32)
    for m in range(128):
        perm[0, (m // 64) * 64 + ((m % 64) + 32) % 64, m] = 1.0
        perm[1, (m + 64) % 128, m] = 1.0
    c["perm"] = perm
    k = np.arange(128)[:, None]
    q = np.arange(512)[None, :]
    c["cmask"] = np.stack([(j * 128 + k <= q).astype(np.float32) for j in range(4)])
    tri = (np.arange(128)[:, None] <= np.arange(128)[None, :]).astype(np.float32)
    c["gcst"] = np.stack([tri, 1.0 - tri, np.ones((128, 128), np.float32)])
    own = np.arange(8)[:, None]
    n = np.arange(8)[None, :]
    c["nmask"] = np.where(n < own, 0.0, -1e30).astype(np.float32).reshape(1, 64)
    c["pmask"] = (n < own).astype(np.float32).reshape(1, 64)
    es = np.zeros((8, 8, 128), np.float32)
    for j in range(8):
        es[j, j, :] = 1.0
    c["esel"] = es
    return c


def emit_mix(cx, A):
    P = cx.P
    S = SEQ
    NT = S // 128
    x, pos, wsel2, dlam, lamc, dng, ggu2, ggb2, gng = (A[k] for k in ("x", "pos", "wsel2", "dlam", "lamc", "dng", "ggu2", "ggb2", "gng"))
    d_ident, d_rc, d_perm, d_cmask, d_gcst, d_nmask, d_pmask, d_esel = (A[k] for k in ("ident", "rc", "perm", "cmask", "gcst", "nmask", "pmask", "esel"))
    out = A["out"]
    phases = ("diff", "gla", "moba")
    cur = {"hh": 0}
    xdeps = tuple(A["pre"]()) if A.get("pre") else ()
    HH = tuple(range(A.get("nhh", 2)))
    xsrc = A.get("x_dyn") or (lambda h, tt: x[tt * 128:(tt + 1) * 128, :])
    row_of = A.get("row_of") or (lambda kind, hh, j: {"d": (3 * hh + j) * 128, "g": 768 + (2 * hh + j) * 128, "m": 1280 + (3 * hh + j) * 128}[kind])

    def V(eng, fn, r=(), w=()):
        return P.add(eng, fn, reads=r, writes=w)

    def LD(dst, src, name, cast=False, n=1):
        b = Buf(name)
        P.add("pool" if cast else "sp", lambda h: h.dma_start(out=dst, in_=src), writes=(b,), dma=True, key=name)
        return b

    ident = cx.sb([128, 128], F32, "ident")
    ident_b = LD(ident[:, :], d_ident[:, :], "ident")
    rc = cx.sb([128, 8], F32, "rc")
    rc_b = LD(rc[:, :], d_rc[:, :], "rc")
    perm = cx.sb([128, 2, 128], BF16, "perm")
    perm_b = LD(perm[:, :, :], d_perm.rearrange("a p m -> p a m"), "perm", cast=True)
    cmask = cx.sb([128, 4, 512], BF16, "cmask")
    cmask_b = LD(cmask[:, :, :], d_cmask.rearrange("a p m -> p a m"), "cmask", cast=True)
    gcst = cx.sb([128, 3, 128], F32, "gcst")
    gcst_b = LD(gcst[:, :, :], d_gcst.rearrange("a p m -> p a m"), "gcst")
    ones_bf = cx.sb([128, 128], BF16, "ones_bf")
    onesbf_b = LD(ones_bf[:, :], d_gcst[2, :, :], "ones_bf", cast=True)
    tri_bf = cmask[:, 0, 0:128]
    nmask = cx.sb([128, 64], F32, "nmask")
    nmask_b = LD(nmask[:, :], d_nmask[0:1, :].partition_broadcast(128), "nmask")
    pmask = cx.sb([128, 64], F32, "pmask")
    pmask_b = LD(pmask[:, :], d_pmask[0:1, :].partition_broadcast(128), "pmask")
    esel = cx.sb([8, 8, 128], BF16, "esel")
    esel_b = LD(esel[:, :, :], d_esel[:, :, :], "esel", cast=True)
    dl = cx.sb([128, 256], F32, "dl")
    dl_b = LD(dl[:, :], dlam[0:1, :].partition_broadcast(128), "dl")
    lc = cx.sb([128, 2], F32, "lc")
    lc_b = LD(lc[:, :], lamc[0:1, :].partition_broadcast(128), "lc")
    dngt = cx.sb([128, 1], F32, "dngt")
    dng_b = LD(dngt[:, :], dng[:, :], "dng")
    gngt = cx.sb([128, 1], F32, "gngt")
    gng_b = LD(gngt[:, :], gng[:, :], "gng")
    ggut_l = [cx.sb([16, 128], F32, f"ggut{k}") for k in HH]
    ggu_bl = [LD(ggut_l[k][:, :], ggu2[k, :, :], f"ggu{k}") for k in HH]
    ggbt_l = [cx.sb([128, 128], F32, f"ggbt{k}") for k in HH]
    ggb_bl = [LD(ggbt_l[k][:, :], ggb2[k, 0:1, :].partition_broadcast(128), f"ggb{k}") for k in HH]

    pbank = [cx.ps([128, 512], F32, f"bank{i}") for i in range(8)]
    pb_b = [Buf(f"bank{i}", excl=True) for i in range(8)]
    rr = {"s": 0, "a": 0, "g": 0}

    def sbank():
        rr["s"] += 1
        return rr["s"] % 2

    def abank():
        rr["a"] += 1
        return 2 + rr["a"] % 2

    def gbank():
        rr["g"] += 1
        return rr["g"] % 8

    xT = cx.sb([128, 16, S], BF16, "xT")
    xT_b = [Buf(f"xT{t}") for t in range(NT)]
    stg = [cx.sb([128, S], F32, f"stg{s}") for s in range(2)]
    stg_b = [Buf(f"stg{s}") for s in range(2)]
    k4 = 0
    for tt in range(NT):
        s = tt % 2
        P.add("pool" if A.get("x_cast") else "sp", lambda h, s=s, tt=tt: h.dma_start(out=stg[s][:, :], in_=xsrc(h, tt)),
              reads=xdeps, writes=(stg_b[s],), dma=True, key=f"stg{s}")
        for g in range(4):
            bk = k4 % 4
            k4 += 1
            for j in range(4):
                kc = g * 4 + j
                V("pe", lambda h, bk=bk, j=j, kc=kc, s=s: h.transpose(out=pbank[bk][:, j * 128:(j + 1) * 128], in_=stg[s][:, kc * 128:(kc + 1) * 128], identity=ident[:, :]),
                  (stg_b[s], ident_b), (pb_b[bk],))
            o = xT[:, g * 4:(g + 1) * 4, tt * 128:(tt + 1) * 128]
            i = pbank[bk][:, :].rearrange("p (j t) -> p j t", j=4)
            if g % 2 == 0:
                V("act", lambda h, o=o, i=i: h.copy(out=o, in_=i), (pb_b[bk],), (xT_b[tt],))
            else:
                V("dve", lambda h, o=o, i=i: h.tensor_copy(out=o, in_=i), (pb_b[bk],), (xT_b[tt],))
    allxT = tuple(xT_b)

    wp = [cx.sb([128, 16, PW], BF16, f"wp{s}") for s in range(2)]
    wp_b = [Buf(f"wp{s}") for s in range(2)]
    wvs = [wsel2[k].rearrange("(kc p) f -> p kc f", p=128) for k in HH]
    wi = [0]

    def load_piece(c0, n):
        s = wi[0] % 2
        wi[0] += 1
        wv = wvs[cur["hh"]]
        P.add("pool", lambda h: h.dma_start(out=wp[s][:, :, 0:n], in_=wv[:, :, c0:c0 + n]), writes=(wp_b[s],), dma=True, key=f"wp{s}")
        return s

    def proj_fm(bk, s, c0, m, tg):
        for kc in range(16):
            V("pe", lambda h, kc=kc: h.matmul(pbank[bk][0:m, :], lhsT=wp[s][:, kc, c0:c0 + m], rhs=xT[:, kc, tg * 512:(tg + 1) * 512],
                                             start=(kc == 0), stop=(kc == 15)), (wp_b[s],) + allxT, (pb_b[bk],))

    def proj_tm(bk, s, c0, n, tt, o0=0):
        for kc in range(16):
            V("pe", lambda h, kc=kc: h.matmul(pbank[bk][:, o0:o0 + n], lhsT=xT[:, kc, tt * 128:(tt + 1) * 128], rhs=wp[s][:, kc, c0:c0 + n],
                                             start=(kc == 0), stop=(kc == 15)), (wp_b[s],) + allxT, (pb_b[bk],))

    posf = cx.sb([128, S], F32, "posf")
    posf_b = Buf("posf")
    posi = cx.sb([128, S], I32, "posi")
    posi_b = LD(posi[:, :], pos[0:1, :].partition_broadcast(128), "posi")
    V("dve", lambda h: h.tensor_copy(out=posf[:, :], in_=posi[:, :]), (posi_b,), (posf_b,))
    ctab = cx.sb([128, S], F32, "ctab")
    stab = cx.sb([128, S], F32, "stab")
    tab_b = Buf("tab")

    def make_tables(kind):
        tb = (tab_b, stg_b[0], stg_b[1], posi_b)
        V("dve", lambda h: h.tensor_scalar(out=stg[0][:, :], in0=posf[:, :], scalar1=rc[:, kind:kind + 1], scalar2=None, op0=ALU.mult), (posf_b, rc_b), (stg_b[0],))
        V("dve", lambda h: h.tensor_scalar(out=posi[:, :], in0=stg[0][:, :], scalar1=1.0 / (2.0 * math.pi), scalar2=None, op0=ALU.mult), (stg_b[0],), (posi_b,))
        V("dve", lambda h: h.tensor_copy(out=stg[1][:, :], in_=posi[:, :]), (posi_b,), (stg_b[1],))
        V("dve", lambda h: h.scalar_tensor_tensor(out=stg[0][:, :], in0=stg[1][:, :], scalar=-2.0 * math.pi, in1=stg[0][:, :], op0=ALU.mult, op1=ALU.add), (stg_b[0], stg_b[1]), (stg_b[0],))
        V("act", lambda h: h.activation(out=stg[1][:, :], in_=stg[0][:, :], func=AF.Sin, scale=0.5), (stg_b[0],), (stg_b[1],))
        V("act", lambda h: h.activation(out=ctab[:, :], in_=stg[0][:, :], func=AF.Sin, bias=rc[:, 6:7], scale=-0.5), (stg_b[0], rc_b), (tab_b,))
        V("dve", lambda h: h.scalar_tensor_tensor(out=stab[:, :], in0=stg[1][:, :], scalar=rc[:, 2 + kind:3 + kind], in1=ctab[:, :], op0=ALU.mult, op1=ALU.mult), (stg_b[1], rc_b, tab_b), (tab_b,))
        V("dve", lambda h: h.tensor_tensor(out=ctab[:, :], in0=stg[1][:, :], in1=stg[1][:, :], op=ALU.mult), (stg_b[1], tab_b), (tab_b,))
        V("dve", lambda h: h.tensor_scalar(out=ctab[:, :], in0=ctab[:, :], scalar1=-2.0, scalar2=1.0, op0=ALU.mult, op1=ALU.add), (tab_b,), (tab_b,))

    qT = [cx.sb([128, S], BF16, f"qT{s}") for s in range(2)]
    kT = [cx.sb([128, S], BF16, f"kT{s}") for s in range(2)]
    vv = [cx.sb([128, NT, 128], BF16, f"vv{s}") for s in range(2)]
    qT_b = [Buf(f"qT{s}") for s in range(2)]
    kT_b = [Buf(f"kT{s}") for s in range(2)]
    vv_b = [Buf(f"vv{s}") for s in range(2)]
    qb = [cx.sb([128, 512], BF16, f"qb{s}") for s in range(2)]
    qb_b = [Buf(f"qb{s}") for s in range(2)]
    t1 = [cx.sb([128, 512], F32, f"t1{s}") for s in range(2)]
    t1_b = [Buf(f"t1{s}") for s in range(2)]
    t2 = [cx.sb([128, 512], F32, f"t2{s}") for s in range(2)]
    t2_b = [Buf(f"t2{s}") for s in range(2)]
    NPT = 4
    pT = [cx.sb([128, 512], BF16, f"pT{s}") for s in range(NPT)]
    pT_b = [Buf(f"pT{s}") for s in range(NPT)]
    fa = cx.sb([128, 512], F32, "fa")
    fb = cx.sb([128, 512], F32, "fb")
    fc = cx.sb([128, 512], F32, "fc")
    fa_b, fb_b, fc_b = Buf("fa"), Buf("fb"), Buf("fc")
    kms = cx.sb([128, 8], F32, "kms")
    kms_b = Buf("kms")
    kmb = cx.sb([128, 8], BF16, "kmb")
    kmb_b = Buf("kmb")
    ri = [0]
    pi = [0]

    def rope_proj(s, c0, dst, dst_b, pk, want_kms=False):
        for tg in range(4):
            bk = sbank()
            proj_fm(bk, s, c0, 128, tg)
            i = ri[0] % 2
            ri[0] += 1
            cols = slice(tg * 512, (tg + 1) * 512)
            V("act", lambda h, i=i, bk=bk: h.copy(out=qb[i][:, :], in_=pbank[bk][:, :]), (pb_b[bk],), (qb_b[i],))
            V("dve", lambda h, i=i, bk=bk, cols=cols: h.tensor_tensor(out=t1[i][:, :], in0=pbank[bk][:, :], in1=ctab[:, cols], op=ALU.mult), (pb_b[bk], tab_b), (t1_b[i],))
            b2 = abank()
            V("pe", lambda h, i=i, b2=b2: h.matmul(pbank[b2][:, :], lhsT=perm[:, pk, :], rhs=qb[i][:, :], start=True, stop=True), (perm_b, qb_b[i]), (pb_b[b2],))
            V("dve", lambda h, i=i, b2=b2, cols=cols: h.tensor_tensor(out=t2[i][:, :], in0=pbank[b2][:, :], in1=stab[:, cols], op=ALU.mult), (pb_b[b2], tab_b), (t2_b[i],))
            V("pool", lambda h, i=i: h.tensor_tensor(out=t1[i][:, :], in0=t1[i][:, :], in1=t2[i][:, :], op=ALU.add), (t1_b[i], t2_b[i]), (t1_b[i],))
            V("act", lambda h, i=i, cols=cols: h.copy(out=dst[:, cols], in_=t1[i][:, :]), (t1_b[i],), (dst_b,))
            if want_kms:
                V("dve", lambda h, i=i, tg=tg: h.tensor_reduce(out=kms[:, 2 * tg:2 * tg + 2], in_=t1[i][:, :].rearrange("p (b k) -> p b k", b=2), axis=AX.X, op=ALU.add),
                  (t1_b[i],), (kms_b,))

    def v_proj(s, c0, dst, dst_b):
        for tt in range(NT):
            bk = sbank()
            proj_tm(bk, s, c0, 128, tt)
            if tt % 2 == 0:
                V("act", lambda h, bk=bk, tt=tt: h.copy(out=dst[:, tt, :], in_=pbank[bk][:, 0:128]), (pb_b[bk],), (dst_b,))
            else:
                V("dve", lambda h, bk=bk, tt=tt: h.tensor_copy(out=dst[:, tt, :], in_=pbank[bk][:, 0:128]), (pb_b[bk],), (dst_b,))

    def store_head(kind, hh, j, s):
        r0 = row_of(kind, hh, j)
        if A.get("out_split"):
            P.add("pool" if A.get("out_cast") else "sp", lambda h: [h.dma_start(out=out[t, r0:r0 + 128, :], in_=stg[s][:, t * (S // 2):(t + 1) * (S // 2)]) for t in range(2)],
                  reads=(stg_b[s],), dma=True, key=f"so{s}", ndma=2)
        else:
            P.add("sp", lambda h: h.dma_start(out=out[r0:r0 + 128, :], in_=stg[s][:, :]), reads=(stg_b[s],), dma=True, key=f"so{s}")

    def rms_over_partitions(src, src_b, n):
        V("act", lambda h: h.activation(out=fc[:, 0:n], in_=src, func=AF.Square), (src_b,), (fc_b,))
        b2 = abank()
        V("pe", lambda h: h.matmul(pbank[b2][:, 0:n], lhsT=gcst[:, 2, :], rhs=fc[:, 0:n], start=True, stop=True), (gcst_b, fc_b), (pb_b[b2],))
        V("act", lambda h: h.activation(out=fb[:, 0:n], in_=pbank[b2][:, 0:n], func=AF.Sqrt, bias=rc[:, 5:6], scale=1.0 / 128.0), (pb_b[b2], rc_b), (fb_b,))
        V("dve", lambda h: h.reciprocal(out=fb[:, 0:n], in_=fb[:, 0:n]), (fb_b,), (fb_b,))

    hs = [0]
    so = [0]

    if "diff" in phases:
        make_tables(0)
        lam = cx.sb([128, 4], F32, "lam")
        lam_b = Buf("lam")
        V("dve", lambda h: h.tensor_tensor(out=fa[:, 0:64], in0=dl[:, 0:64], in1=dl[:, 64:128], op=ALU.mult), (dl_b,), (fa_b,))
        V("dve", lambda h: h.tensor_tensor(out=fa[:, 64:128], in0=dl[:, 128:192], in1=dl[:, 192:256], op=ALU.mult), (dl_b, fa_b), (fa_b,))
        V("dve", lambda h: h.tensor_reduce(out=lam[:, 0:2], in_=fa[:, 0:128].rearrange("p (a k) -> p a k", a=2), axis=AX.X, op=ALU.add), (fa_b,), (lam_b,))
        V("act", lambda h: h.activation(out=lam[:, 0:2], in_=lam[:, 0:2], func=AF.Exp), (lam_b,), (lam_b,))
        V("dve", lambda h: h.tensor_tensor(out=lam[:, 2:3], in0=lam[:, 1:2], in1=lam[:, 0:1], op=ALU.subtract), (lam_b,), (lam_b,))
        V("dve", lambda h: h.tensor_tensor(out=lam[:, 2:3], in0=lam[:, 2:3], in1=lc[:, 0:1], op=ALU.subtract), (lam_b, lc_b), (lam_b,))
        V("dve", lambda h: h.tensor_tensor(out=lam[:, 3:4], in0=dngt[:, 0:1], in1=lc[:, 1:2], op=ALU.mult), (dng_b, lc_b, lam_b), (lam_b,))
        for hh, hd in [(u, v) for u in HH for v in range(3)]:
            cur["hh"] = hh
            s = load_piece(hd * 384, 384)
            q = hs[0] % 2
            hs[0] += 1
            rope_proj(s, 0, qT[q], qT_b[q], 0)
            rope_proj(s, 128, kT[q], kT_b[q], 0)
            v_proj(s, 256, vv[q], vv_b[q])
            so_s = so[0] % 2
            so[0] += 1
            for qg in range(4):
                nkc = 4 * qg + 4
                units = [(kc, m) for kc in range(nkc) for m in range(2)]
                ctx = {}

                def ph1(u):
                    kc, m = u
                    bk = sbank()
                    V("pe", lambda h: h.matmul(pbank[bk][:, :], lhsT=kT[q][m * 64:(m + 1) * 64, kc * 128:(kc + 1) * 128],
                                               rhs=qT[q][m * 64:(m + 1) * 64, qg * 512:(qg + 1) * 512], start=True, stop=True),
                      (kT_b[q], qT_b[q]), (pb_b[bk],))
                    pj = pi[0] % NPT
                    pi[0] += 1
                    ctx[u] = pj
                    V("act", lambda h: h.activation(out=pT[pj][:, :], in_=pbank[bk][:, :], func=AF.Exp, scale=0.125), (pb_b[bk],), (pT_b[pj],))
                    if kc >= 4 * qg:
                        j = kc - 4 * qg
                        V("pool", lambda h: h.tensor_tensor(out=pT[pj][:, :], in0=pT[pj][:, :], in1=cmask[:, j, :], op=ALU.mult), (pT_b[pj], cmask_b), (pT_b[pj],))

                def ph2(u):
                    kc, m = u
                    pj = ctx[u]
                    V("pe", lambda h: h.matmul(pbank[4 + m][:, :], lhsT=vv[q][:, kc, :], rhs=pT[pj][:, :], start=(kc == 0), stop=(kc == nkc - 1)),
                      (vv_b[q], pT_b[pj]), (pb_b[4 + m],))
                    V("pe", lambda h: h.matmul(pbank[6 + m][:, :], lhsT=ones_bf[:, :], rhs=pT[pj][:, :], start=(kc == 0), stop=(kc == nkc - 1)),
                      (onesbf_b, pT_b[pj]), (pb_b[6 + m],))

                pipeline(units, ph1, ph2)
                V("dve", lambda h: h.reciprocal(out=fb[:, :], in_=pbank[6][:, :]), (pb_b[6],), (fb_b,))
                V("dve", lambda h: h.tensor_tensor(out=fa[:, :], in0=pbank[4][:, :], in1=fb[:, :], op=ALU.mult), (pb_b[4], fb_b), (fa_b,))
                V("dve", lambda h: h.reciprocal(out=fb[:, :], in_=pbank[7][:, :]), (pb_b[7], fb_b), (fb_b,))
                V("dve", lambda h: h.tensor_tensor(out=fc[:, :], in0=pbank[5][:, :], in1=fb[:, :], op=ALU.mult), (pb_b[5], fb_b), (fc_b,))
                V("dve", lambda h: h.scalar_tensor_tensor(out=fa[:, :], in0=fc[:, :], scalar=lam[:, 2:3], in1=fa[:, :], op0=ALU.mult, op1=ALU.add), (fc_b, fa_b, lam_b), (fa_b,))
                rms_over_partitions(fa[:, :], fa_b, 512)
                V("dve", lambda h, qg=qg: h.scalar_tensor_tensor(out=stg[so_s][:, qg * 512:(qg + 1) * 512], in0=fa[:, :], scalar=lam[:, 3:4], in1=fb[:, :], op0=ALU.mult, op1=ALU.mult),
                  (fa_b, fb_b, lam_b), (stg_b[so_s],))
            store_head("d", hh, hd, so_s)

    if "moba" in phases:
        make_tables(1)
        selT = cx.sb([8, S], BF16, "selT")
        selT_b = Buf("selT")
        gm = cx.sb([128, 8], F32, "gm")
        top8 = cx.sb([128, 8], F32, "top8")
        sel = cx.sb([128, 8], F32, "sel")
        gm_b = Buf("gm")
        SC = 128.0 ** -0.5
        for hh, hd in [(u, v) for u in HH for v in range(3)]:
            cur["hh"] = hh
            s = load_piece(1936 + hd * 384, 384)
            q = hs[0] % 2
            hs[0] += 1
            rope_proj(s, 0, qT[q], qT_b[q], 1)
            rope_proj(s, 128, kT[q], kT_b[q], 1, want_kms=True)
            v_proj(s, 256, vv[q], vv_b[q])
            V("act", lambda h: h.mul(out=kmb[:, :], in_=kms[:, :], mul=1.0 / 256.0), (kms_b,), (kmb_b,))
            for tt in range(NT):
                own = tt // 2
                b2 = abank()
                V("pe", lambda h, b2=b2, tt=tt: h.matmul(pbank[b2][:, 0:8], lhsT=qT[q][:, tt * 128:(tt + 1) * 128], rhs=kmb[:, :], start=True, stop=True),
                  (qT_b[q], kmb_b), (pb_b[b2],))
                V("dve", lambda h, b2=b2, own=own: h.tensor_tensor(out=gm[:, :], in0=pbank[b2][:, 0:8], in1=nmask[:, own * 8:(own + 1) * 8], op=ALU.add), (pb_b[b2], nmask_b), (gm_b,))
                V("dve", lambda h: h.max(out=top8[:, :], in_=gm[:, :]), (gm_b,), (gm_b,))
                V("dve", lambda h, own=own: h.scalar_tensor_tensor(out=sel[:, :], in0=gm[:, :], scalar=top8[:, 2:3], in1=pmask[:, own * 8:(own + 1) * 8], op0=ALU.is_ge, op1=ALU.mult),
                  (gm_b, pmask_b), (gm_b,))
                b3 = abank()
                V("pe", lambda h, b3=b3: h.transpose(out=pbank[b3][0:8, 0:128], in_=sel[:, :], identity=ident[:, :]), (gm_b, ident_b), (pb_b[b3],))
                V("act", lambda h, b3=b3, tt=tt: h.copy(out=selT[0:8, tt * 128:(tt + 1) * 128], in_=pbank[b3][0:8, 0:128]), (pb_b[b3],), (selT_b,))
            so_s = so[0] % 2
            so[0] += 1
            for qblk in range(8):
                ab = 4 + 2 * (qblk % 2)
                zb = ab + 1
                nkc = 2 * qblk + 2
                qc = slice(qblk * 256, (qblk + 1) * 256)
                ctx = {}
                mbs = {}

                def ph1(kc):
                    n = kc // 2
                    bk = sbank()
                    V("pe", lambda h: h.matmul(pbank[bk][:, 0:256], lhsT=kT[q][:, kc * 128:(kc + 1) * 128], rhs=qT[q][:, qc], start=True, stop=True),
                      (kT_b[q], qT_b[q]), (pb_b[bk],))
                    pj = pi[0] % NPT
                    pi[0] += 1
                    ctx[kc] = pj
                    if n < qblk and kc % 2 == 0:
                        mb = abank()
                        mbs[n] = mb
                        V("pe", lambda h: h.matmul(pbank[mb][:, 0:256], lhsT=esel[0:8, n, :], rhs=selT[0:8, qc], start=True, stop=True),
                          (esel_b, selT_b), (pb_b[mb],))
                    V("act", lambda h: h.activation(out=pT[pj][:, 0:256], in_=pbank[bk][:, 0:256], func=AF.Exp, scale=SC), (pb_b[bk],), (pT_b[pj],))
                    if n < qblk:
                        mb = mbs[n]
                        V("dve", lambda h: h.tensor_tensor(out=pT[pj][:, 0:256], in0=pT[pj][:, 0:256], in1=pbank[mb][:, 0:256], op=ALU.mult), (pT_b[pj], pb_b[mb]), (pT_b[pj],))
                    else:
                        j = kc % 2
                        V("pool", lambda h: h.tensor_tensor(out=pT[pj][:, 0:256], in0=pT[pj][:, 0:256], in1=cmask[:, j, 0:256], op=ALU.mult), (pT_b[pj], cmask_b), (pT_b[pj],))

                def ph2(kc):
                    pj = ctx[kc]
                    V("pe", lambda h: h.matmul(pbank[ab][:, 0:256], lhsT=vv[q][:, kc, :], rhs=pT[pj][:, 0:256], start=(kc == 0), stop=(kc == nkc - 1)),
                      (vv_b[q], pT_b[pj]), (pb_b[ab],))
                    V("pe", lambda h: h.matmul(pbank[zb][:, 0:256], lhsT=ones_bf[:, :], rhs=pT[pj][:, 0:256], start=(kc == 0), stop=(kc == nkc - 1)),
                      (onesbf_b, pT_b[pj]), (pb_b[zb],))

                pipeline(list(range(nkc)), ph1, ph2)
                V("dve", lambda h, zb=zb: h.reciprocal(out=fb[:, 0:256], in_=pbank[zb][:, 0:256]), (pb_b[zb],), (fb_b,))
                V("dve", lambda h, ab=ab, qc=qc: h.tensor_tensor(out=stg[so_s][:, qc], in0=pbank[ab][:, 0:256], in1=fb[:, 0:256], op=ALU.mult), (pb_b[ab], fb_b), (stg_b[so_s],))
            store_head("m", hh, hd, so_s)

    if "gla" in phases:
        gq, gk = t2[0], t2[1]
        ggT = cx.sb([16, 512], F32, "ggT")
        grs = cx.sb([128, 2, 512], BF16, "grs")
        gq_b, gk_b, ggT_b, grs_b = t2_b[0], t2_b[1], Buf("ggT"), Buf("grs")
        ktm = cx.sb([128, 128], F32, "ktm")
        gv = cx.sb([128, 256], BF16, "gv")
        la = cx.sb([128, 128], F32, "la")
        ktm_b, gv_b, la_b = Buf("ktm"), Buf("gv"), Buf("la")
        Eq = cx.sb([128, 128], F32, "Eq")
        Ek = cx.sb([128, 128], F32, "Ek")
        Er = cx.sb([128, 128], F32, "Er")
        Eq_b, Ek_b, Er_b = Buf("Eq"), Buf("Ek"), Buf("Er")
        qtl = cx.sb([128, 128], BF16, "qtl")
        kt2 = cx.sb([128, 2, 128], BF16, "kt2")
        khat = cx.sb([128, 128], BF16, "khat")
        attm = [cx.sb([128, 128], BF16, f"attm{i}") for i in range(2)]
        qtl_b, kt2_b, khat_b = Buf("qtl"), Buf("kt2"), Buf("khat")
        attm_b = [Buf(f"attm{i}") for i in range(2)]
        Sst = cx.sb([128, 128], F32, "Sst")
        Sb2 = cx.sb([128, 2, 128], BF16, "Sb2")
        Sst_b, Sb2_b = Buf("Sst"), Buf("Sb2")
        og_b = [t1_b[0], t1_b[1]]
        for hh in HH:
            cur["hh"] = hh
            sa = load_piece(1152, 400)
            sb_ = load_piece(1552, 384)
            V("pool", lambda h: h.memset(kt2[:, :, :], 0.0), (), (kt2_b,))
            V("pool", lambda h: h.memset(Sst[:, :], 0.0), (), (Sst_b,))
            V("pool", lambda h: h.memset(Sb2[:, :, :], 0.0), (), (Sb2_b,))
            so0 = so[0] % 2
            so1 = (so[0] + 1) % 2
            so[0] += 2
            sos = (so0, so1)
            for tg in range(4):
                bk = gbank()
                proj_fm(bk, sa, 0, 128, tg)
                V("act", lambda h, bk=bk: h.copy(out=gq[:, :], in_=pbank[bk][:, :]), (pb_b[bk],), (gq_b,))
                bk = gbank()
                proj_fm(bk, sa, 128, 128, tg)
                V("dve", lambda h, bk=bk: h.tensor_copy(out=gk[:, :], in_=pbank[bk][:, :]), (pb_b[bk],), (gk_b,))
                bk = gbank()
                proj_fm(bk, sa, 256, 16, tg)
                V("act", lambda h, bk=bk: h.copy(out=ggT[:, :], in_=pbank[bk][0:16, :]), (pb_b[bk],), (ggT_b,))
                for hd in range(2):
                    bk = gbank()
                    proj_fm(bk, sb_, 128 + hd * 128, 128, tg)
                    V("act", lambda h, bk=bk, hd=hd: h.activation(out=grs[:, hd, :], in_=pbank[bk][:, :], func=AF.Silu), (pb_b[bk],), (grs_b,))
                for ci in range(4):
                    tt = tg * 4 + ci
                    cc = slice(ci * 128, (ci + 1) * 128)
                    bk = gbank()
                    proj_tm(bk, sa, 128, 128, tt)
                    V("act", lambda h, bk=bk: h.copy(out=ktm[:, :], in_=pbank[bk][:, 0:128]), (pb_b[bk],), (ktm_b,))
                    bk = gbank()
                    proj_tm(bk, sa, 272, 128, tt, 0)
                    proj_tm(bk, sb_, 0, 128, tt, 128)
                    V("dve", lambda h, bk=bk: h.tensor_copy(out=gv[:, :], in_=pbank[bk][:, 0:256]), (pb_b[bk],), (gv_b,))
                    bk = gbank()
                    V("pe", lambda h, bk=bk, cc=cc: h.matmul(pbank[bk][:, 0:128], lhsT=ggT[0:16, cc], rhs=ggut_l[hh][0:16, :], start=True, stop=True), (ggT_b, ggu_bl[hh]), (pb_b[bk],))
                    V("dve", lambda h, bk=bk: h.tensor_tensor(out=la[:, :], in0=pbank[bk][:, 0:128], in1=ggbt_l[hh][:, :], op=ALU.add), (pb_b[bk], ggb_bl[hh]), (la_b,))
                    V("act", lambda h: h.activation(out=la[:, :], in_=la[:, :], func=AF.Sigmoid), (la_b,), (la_b,))
                    V("act", lambda h: h.activation(out=la[:, :], in_=la[:, :], func=AF.Ln), (la_b,), (la_b,))
                    V("dve", lambda h: h.tensor_scalar(out=la[:, :], in0=la[:, :], scalar1=1.0 / 16.0, scalar2=None, op0=ALU.mult), (la_b,), (la_b,))
                    bc = gbank()
                    V("pe", lambda h, bc=bc: h.matmul(pbank[bc][:, 0:128], lhsT=la[:, :], rhs=gcst[:, 0, :], start=True, stop=True), (la_b, gcst_b), (pb_b[bc],))
                    V("act", lambda h, bc=bc: h.activation(out=Eq[:, :], in_=pbank[bc][:, 0:128], func=AF.Exp), (pb_b[bc],), (Eq_b,))
                    V("act", lambda h, bc=bc: h.activation(out=Ek[:, :], in_=pbank[bc][:, 0:128], func=AF.Exp, scale=-1.0), (pb_b[bc],), (Ek_b,))
                    br = gbank()
                    V("pe", lambda h, br=br: h.matmul(pbank[br][:, 0:128], lhsT=gcst[:, 1, :], rhs=la[:, :], start=True, stop=True), (la_b, gcst_b), (pb_b[br],))
                    V("act", lambda h, br=br: h.activation(out=Er[:, :], in_=pbank[br][:, 0:128], func=AF.Exp), (pb_b[br],), (Er_b,))
                    V("dve", lambda h, cc=cc: h.scalar_tensor_tensor(out=qtl[:, :], in0=gq[:, cc], scalar=0.125, in1=Eq[:, :], op0=ALU.mult, op1=ALU.mult), (gq_b, Eq_b), (qtl_b,))
                    for hd in range(2):
                        ps_ = slice(hd * 64, (hd + 1) * 64)
                        V("pool", lambda h, hd=hd, ps_=ps_, cc=cc: h.tensor_tensor(out=kt2[ps_, hd, :], in0=gk[ps_, cc], in1=Ek[ps_, :], op=ALU.mult), (gk_b, Ek_b), (kt2_b,))
                    V("pool", lambda h: h.tensor_tensor(out=khat[:, :], in0=ktm[:, :], in1=Er[:, :], op=ALU.mult), (ktm_b, Er_b), (khat_b,))
                    for hd in range(2):
                        bt = gbank()
                        V("pe", lambda h, bt=bt, hd=hd: h.matmul(pbank[bt][:, 0:128], lhsT=kt2[:, hd, :], rhs=qtl[:, :], start=True, stop=True), (kt2_b, qtl_b), (pb_b[bt],))
                        V("dve", lambda h, bt=bt, hd=hd: h.tensor_tensor(out=attm[hd][:, :], in0=pbank[bt][:, 0:128], in1=tri_bf, op=ALU.mult), (pb_b[bt], cmask_b), (attm_b[hd],))
                        bo = gbank()
                        V("pe", lambda h, bo=bo, hd=hd: h.matmul(pbank[bo][:, 0:128], lhsT=gv[:, hd * 128:(hd + 1) * 128], rhs=attm[hd][:, :], start=True, stop=False), (gv_b, attm_b[hd]), (pb_b[bo],))
                        V("pe", lambda h, bo=bo, hd=hd: h.matmul(pbank[bo][:, 0:128], lhsT=Sb2[:, hd, :], rhs=qtl[:, :], start=False, stop=True), (Sb2_b, qtl_b), (pb_b[bo],))
                        V("act", lambda h, bo=bo, hd=hd, cc=cc: h.copy(out=t1[hd][:, cc], in_=pbank[bo][:, 0:128]), (pb_b[bo],), (og_b[hd],))
                    bkv = gbank()
                    V("pe", lambda h, bkv=bkv: h.matmul(pbank[bkv][:, 0:256], lhsT=khat[:, :], rhs=gv[:, :], start=True, stop=True), (khat_b, gv_b), (pb_b[bkv],))
                    for hd in range(2):
                        ps_ = slice(hd * 64, (hd + 1) * 64)
                        V("dve", lambda h, hd=hd, ps_=ps_, bkv=bkv: h.scalar_tensor_tensor(out=Sst[ps_, :], in0=Sst[ps_, :], scalar=Eq[ps_, 127:128], in1=pbank[bkv][ps_, hd * 128:(hd + 1) * 128],
                                                                                       op0=ALU.mult, op1=ALU.add), (Sst_b, Eq_b, pb_b[bkv]), (Sst_b,))
                        V("act", lambda h, hd=hd, ps_=ps_: h.copy(out=Sb2[ps_, hd, :], in_=Sst[ps_, :]), (Sst_b,), (Sb2_b,))
                for hd in range(2):
                    rms_over_partitions(t1[hd][:, :], og_b[hd], 512)
                    V("dve", lambda h, hd=hd: h.scalar_tensor_tensor(out=fa[:, :], in0=t1[hd][:, :], scalar=gngt[:, 0:1], in1=fb[:, :], op0=ALU.mult, op1=ALU.mult),
                      (og_b[hd], fb_b, gng_b), (fa_b,))
                    V("pool", lambda h, hd=hd, tg=tg: h.tensor_tensor(out=stg[sos[hd]][:, tg * 512:(tg + 1) * 512], in0=fa[:, :], in1=grs[:, hd, :], op=ALU.mult),
                      (fa_b, grs_b), (stg_b[sos[hd]],))
            for hd in range(2):
                store_head("g", hh, hd, sos[hd])


_CACHE = {}

_IN_OFF = {"dq": 0, "dk": 768, "dv": 1536, "gq": 2304, "gk": 2560, "gv": 2816, "gr": 3328, "gg": 3840, "mq": 3856, "mk": 4624, "mv": 5392}


def _wsel_cols(hh):
    o = _IN_OFF
    cols = []
    for h in range(3 * hh, 3 * hh + 3):
        cols += list(range(o["dq"] + h * 128, o["dq"] + (h + 1) * 128))
        cols += list(range(o["dk"] + h * 128, o["dk"] + (h + 1) * 128))
        cols += list(range(o["dv"] + h * 128, o["dv"] + (h + 1) * 128))
    g0 = 2 * hh
    cols += list(range(o["gq"] + g0 * 64, o["gq"] + (g0 + 2) * 64))
    cols += list(range(o["gk"] + g0 * 64, o["gk"] + (g0 + 2) * 64))
    cols += list(range(o["gg"], o["gg"] + 16))
    cols += list(range(o["gv"] + g0 * 128, o["gv"] + (g0 + 1) * 128))
    cols += list(range(o["gv"] + (g0 + 1) * 128, o["gv"] + (g0 + 2) * 128))
    cols += list(range(o["gr"] + g0 * 128, o["gr"] + (g0 + 2) * 128))
    for h in range(3 * hh, 3 * hh + 3):
        cols += list(range(o["mq"] + h * 128, o["mq"] + (h + 1) * 128))
        cols += list(range(o["mk"] + h * 128, o["mk"] + (h + 1) * 128))
        cols += list(range(o["mv"] + h * 128, o["mv"] + (h + 1) * 128))
    assert len(cols) == NCOL
    return np.array(cols)


_CONST_SHAPES = {"ident": [128, 128], "rc": [128, 8], "perm": [2, 128, 128], "cmask": [4, 128, 512], "gcst": [3, 128, 128],
                 "nmask": [1, 64], "pmask": [1, 64], "esel": [8, 8, 128]}


def build_fused():
    cx = Ctx()
    P = cx.P
    x = cx.din("x", [TOK, D])
    pos = cx.din("pos", [1, SEQ], I32)
    cst = {k: cx.din(k, shp) for k, shp in _CONST_SHAPES.items()}
    L = []
    for i in range(DEPTH):
        L.append({
            "p": cx.din(f"p{i}", [TOK, 256]), "wsel2": cx.din(f"wsel{i}", [1, D, NCOL]), "wo": cx.din(f"wo{i}", [D, D]),
            "f1g": cx.din(f"f1g{i}", [D, DFF]), "f1u": cx.din(f"f1u{i}", [D, DFF]), "f1d": cx.din(f"f1d{i}", [DFF, D]),
            "f2g": cx.din(f"f2g{i}", [D, DFF]), "f2u": cx.din(f"f2u{i}", [D, DFF]), "f2d": cx.din(f"f2d{i}", [DFF, D]),
            "wpe": cx.din(f"wpe{i}", [256, D]), "wpg": cx.din(f"wpg{i}", [D, D]),
            "lng": cx.din(f"lng{i}", [4, D]), "lnb": cx.din(f"lnb{i}", [4, D]),
            "dlam": cx.din(f"dlam{i}", [1, 256]), "lamc": cx.din(f"lamc{i}", [1, 2]), "dng": cx.din(f"dng{i}", [128, 1]),
            "ggu2": cx.din(f"ggu{i}", [1, 16, 128]), "ggb2": cx.din(f"ggb{i}", [1, 1, 128]), "gng": cx.din(f"gng{i}", [128, 1]),
        })
    y = cx.dout("y", [TOK, D])
    HB = D // 2
    x1loc = cx.dscratch("x1loc", [TOK, D])
    x1b = cx.dscratch("x1b", [TOK, D], BF16)
    xg = cx.dscratch("xg", [4, TOK, D], BF16)
    xsel = cx.dscratch("xsel", [SEQ, D], BF16)
    oTloc = cx.dscratch("oTloc", [2, HB, TOK], BF16)
    og = cx.dscratch("og", [4 * 4 * (HB // 2), TOK], BF16)
    oTsel = cx.dscratch("oTsel", [D, TOK], BF16)
    ident = cst["ident"]
    RG = [[0, 1, 2, 3], [4, 5, 6, 7]]
    dyn = {}

    def beta(h):
        return (h.partition_id() // 2) % 2

    def half(h):
        return h.partition_id() % 2

    def dval(h, name, fn, hi):
        if (name, id(h)) not in dyn:
            dyn[(name, id(h))] = h.snap(fn(h), min_val=0, max_val=hi)
        return dyn[(name, id(h))]

    def cc(src, dst):
        P.add("pool", lambda h: h.collective_compute("AllGather", ALU.bypass, replica_groups=RG, ins=[src], outs=[dst]),
              dma=True, key="cc", cc=True)

    def exchange_x():
        toks = []
        for k in range(4):
            gb = Buf(f"xg{k}")
            P.add("pool", lambda h, k=k: h.collective_compute("AllGather", ALU.bypass, replica_groups=RG, ins=[x1b[k * 256:(k + 1) * 256, :]], outs=[xg[k]]),
                  writes=(gb,), dma=True, key="cc", cc=True)
            pb = Buf(f"xsel{k}")
            P.add("sp", lambda h, k=k: [h.dma_start(out=xsel[a * TOK + k * 256:a * TOK + (k + 1) * 256, :],
                                                    in_=xg[k][bass.ds(dval(h, "xrow", lambda e: beta(e) * 512, 512), 512), :][a * 256:(a + 1) * 256, :])
                                        for a in range(2)], reads=(gb,), writes=(pb,), dma=True, key="pickx", ndma=2)
            toks.append(pb)
        return toks

    def exchange_o():
        toks = []
        for t in range(2):
            for f2 in range(2):
                k = t * 2 + f2
                gb = Buf(f"og{k}")
                toks.append(gb)
                P.add("pool", lambda h, t=t, f2=f2, k=k: h.collective_compute("AllGather", ALU.bypass, replica_groups=RG,
                                                                               ins=[oTloc[t, f2 * 512:(f2 + 1) * 512, :]], outs=[og[k * 2048:(k + 1) * 2048, :]]),
                      writes=(gb,), dma=True, key="cc", cc=True)
        out = []
        for hh in range(2):
            for f2 in range(2):
                pb = Buf(f"osel{hh}{f2}")
                P.add("act", lambda h, hh=hh, f2=f2: h.dma_start(
                    out=oTsel[hh * HB + f2 * 512:hh * HB + (f2 + 1) * 512, :],
                    in_=og[bass.ds(dval(h, "orow", lambda e: half(e) * 4096 + beta(e) * 1024, 5120), 3072), :][f2 * 2048 + hh * 512:f2 * 2048 + (hh + 1) * 512, :]),
                    reads=tuple(toks), writes=(pb,), dma=True, key="picko")
                out.append(pb)
        return out

    cx.begin_stage()
    emit_A(cx, x, x1loc, L[0]["f1g"], L[0]["f1u"], L[0]["f1d"], L[0]["lng"], L[0]["lnb"], ident, yb=x1b)
    cx.end_stage()
    for i in range(DEPTH):
        w = L[i]
        cx.begin_stage()
        A = {"x": xsel, "pos": pos, "out": oTloc, "nhh": 1, "out_split": True, "x_cast": True, "out_cast": True, "pre": exchange_x,
             "row_of": lambda kind, hh, j: {"d": j * 128, "g": 384 + j * 128, "m": 640 + j * 128}[kind]}
        A.update({k: w[k] for k in ("wsel2", "dlam", "lamc", "dng", "ggu2", "ggb2", "gng")})
        A.update(cst)
        emit_mix(cx, A)
        cx.end_stage()
        cx.begin_stage()
        if i + 1 < DEPTH:
            n = L[i + 1]
            nxt = (n["f1g"], n["f1u"], n["f1d"], n["lng"], n["lnb"])
            emit_C(cx, x1loc, oTsel, w["p"], w["wo"], w["f2g"], w["f2u"], w["f2d"], w["wpe"], w["wpg"], w["lng"], w["lnb"], ident, x1loc, nxt, yb=x1b, pre=exchange_o)
        else:
            emit_C(cx, x1loc, oTsel, w["p"], w["wo"], w["f2g"], w["f2u"], w["f2d"], w["wpe"], w["wpg"], w["lng"], w["lnb"], ident, y, None, pre=exchange_o)
        cx.end_stage()
    return cx.finish()


def _wo_rows():
    rows = []
    for hh in range(2):
        for j in range(3):
            rows += list(range((3 * hh + j) * 128, (3 * hh + j + 1) * 128))
        for j in range(2):
            rows += list(range(768 + (2 * hh + j) * 128, 768 + (2 * hh + j + 1) * 128))
        for j in range(3):
            rows += list(range(1280 + (3 * hh + j) * 128, 1280 + (3 * hh + j + 1) * 128))
    return np.array(rows)


def kernel(**inputs):
    f = lambda k: np.ascontiguousarray(np.asarray(inputs[k]), dtype=np.float32)
    x = f("x")
    p = f("p")
    positions = np.ascontiguousarray(np.asarray(inputs["positions"]).astype(np.int32))
    w_in, w_out = f("w_in"), f("w_out")
    dlam, dng = f("diff_lambda"), f("diff_norm_g")
    ggu, ggb, gng = f("gla_gate_up"), f("gla_gate_b"), f("gla_norm_g")
    f1g, f1u, f1d = f("ffn1_gate"), f("ffn1_up"), f("ffn1_down")
    f2g, f2u, f2d = f("ffn2_gate"), f("ffn2_up"), f("ffn2_down")
    wpe, wpg = f("w_pe"), f("w_pg")
    lng, lnb = f("ln_g"), f("ln_b")
    shared = dict(mix_consts())
    per_hh = [dict(), dict()]
    cols = [_wsel_cols(hh) for hh in range(2)]
    wor = _wo_rows()
    for i in range(DEPTH):
        lam_init = 0.8 - 0.6 * math.exp(-0.3 * i)
        shared.update({
            f"wo{i}": np.ascontiguousarray(w_out[i][wor, :]), f"f1g{i}": f1g[i], f"f1u{i}": f1u[i], f"f1d{i}": f1d[i],
            f"f2g{i}": f2g[i], f"f2u{i}": f2u[i], f"f2d{i}": f2d[i], f"wpe{i}": wpe[i], f"wpg{i}": wpg[i],
            f"lng{i}": lng[i], f"lnb{i}": lnb[i], f"dlam{i}": np.ascontiguousarray(dlam[i].reshape(1, 256)),
            f"lamc{i}": np.array([[lam_init, 1.0 - lam_init]], np.float32), f"dng{i}": np.ascontiguousarray(dng[i].reshape(128, 1)),
            f"gng{i}": np.ascontiguousarray(gng[i].reshape(128, 1)),
        })
        for hh in range(2):
            per_hh[hh].update({
                f"wsel{i}": np.ascontiguousarray(w_in[i][:, cols[hh]])[None],
                f"ggu{i}": np.ascontiguousarray(ggu[i][:, hh * 128:(hh + 1) * 128])[None],
                f"ggb{i}": np.ascontiguousarray(ggb[i][hh * 128:(hh + 1) * 128].reshape(1, 1, 128)),
            })
    in_maps = []
    for c in range(NCORES):
        b, h = c // 2, c % 2
        m = dict(shared)
        m.update(per_hh[h])
        m["x"] = np.ascontiguousarray(x[b, h * TOK:(h + 1) * TOK])
        m["pos"] = np.ascontiguousarray(positions[b].reshape(1, SEQ))
        for i in range(DEPTH):
            m[f"p{i}"] = np.ascontiguousarray(p[i, b, h * TOK:(h + 1) * TOK])
        in_maps.append(m)
    if "fused" not in _CACHE:
        _CACHE["fused"] = build_fused()
    res = run_bass_kernel_spmd(_CACHE["fused"], in_maps, core_ids=list(range(NCORES))).results
    out = np.concatenate([res[c]["y"] for c in range(NCORES)], axis=0)
    return out.reshape(NB, SEQ, D).astype(np.float32)
```

```python
import math
import types
from contextlib import ExitStack
import numpy as np
import concourse.bass as bass
import concourse.mybir as mybir
from concourse.bass_utils import run_bass_kernel_spmd

F32 = mybir.dt.float32
BF16 = mybir.dt.bfloat16
I32 = mybir.dt.int32
AF = mybir.ActivationFunctionType
ALU = mybir.AluOpType
AX = mybir.AxisListType

D = 2048
DFF = 5632
SEQ = 2048
NB = 4
DEPTH = 2
NCORES = 8
TOK = 1024
ALPHA = (2 * DEPTH) ** 0.25
EPS = 1e-5
DIN = 6160


class Buf:
    __slots__ = ("name", "lw", "rd", "excl")

    def __init__(self, name, excl=False):
        self.name = name
        self.lw = None
        self.rd = {}
        self.excl = excl


def _freeze(fn):
    if getattr(fn, "__closure__", None) is None:
        return fn
    cells = []
    for c in fn.__closure__:
        try:
            cells.append(types.CellType(c.cell_contents))
        except ValueError:
            cells.append(c)
    return types.FunctionType(fn.__code__, fn.__globals__, fn.__name__, fn.__defaults__, tuple(cells))


def _flat(x):
    for b in x:
        if isinstance(b, (tuple, list)):
            yield from _flat(b)
        else:
            yield b


class Op:
    __slots__ = ("eng", "fn", "deps", "sig", "cnt", "dma", "key", "ndma", "chan", "cc")


class Prog:
    ENGS = ("pe", "dve", "act", "pool", "sp")

    def __init__(self):
        self.ops = []
        self.bar = []

    def barrier(self):
        last = {}
        for op in self.ops:
            last[op.chan] = op
        self.bar = list(last.values())

    def add(self, eng, fn, reads=(), writes=(), dma=False, key=None, ndma=1, cc=False):
        op = Op()
        op.cc = cc
        op.eng = eng
        op.fn = _freeze(fn)
        op.sig = False
        op.cnt = 0
        op.dma = dma
        op.key = key
        op.ndma = ndma
        op.chan = ("dma", key) if dma else eng
        deps = {}

        def need(d):
            if d is None or d is op:
                return
            c = d.chan
            if c not in deps or deps[c][0] < d.cnt:
                deps[c] = (d.cnt, d)

        op.cnt = len(self.ops)
        reads = tuple(_flat(reads))
        writes = tuple(_flat(writes))
        for d in self.bar:
            need(d)
        writes = tuple(writes) + tuple(b for b in reads if b.excl)
        for b in reads:
            need(b.lw)
        for b in writes:
            need(b.lw)
            for r in b.rd.values():
                need(r)
        for b in reads:
            b.rd[op.chan] = op
        for b in writes:
            b.lw = op
            b.rd = {}
        op.deps = [d for (_, d) in deps.values()]
        if not dma and eng == "pe":
            op.deps = [d for d in op.deps if d.chan != "pe"]
        for d in op.deps:
            d.sig = True
        self.ops.append(op)
        return op

    def emit(self, nc, stack):
        cnt = {}
        for op in self.ops:
            if op.dma:
                cnt[op.chan] = cnt.get(op.chan, 0) + (1 if op.cc else 16 * op.ndma)
                op.cnt = cnt[op.chan]
            elif op.sig:
                cnt[op.chan] = cnt.get(op.chan, 0) + 1
                op.cnt = cnt[op.chan]
            else:
                op.cnt = None
        sems = {}
        for c in cnt:
            nm = ("s_" + (c if isinstance(c, str) else "d_" + str(c[1]))).replace(":", "_")
            sems[c] = stack.enter_context(nc.semaphore(nm))
        self.nsems = len(sems)
        self.final_cnt = dict(cnt)
        block = stack.enter_context(nc.Block())
        streams = {e: [op for op in self.ops if op.eng == e] for e in self.ENGS}

        def run(e, h):
            waited = {}
            for op in streams[e]:
                for d in op.deps:
                    if waited.get(d.chan, 0) < d.cnt:
                        h.wait_ge(sems[d.chan], d.cnt)
                        waited[d.chan] = d.cnt
                ins = op.fn(h)
                if op.cc:
                    ins.then_inc(sems[op.chan])
                elif op.dma:
                    if not isinstance(ins, (list, tuple)):
                        ins = [ins]
                    assert len(ins) == op.ndma
                    for i in ins:
                        i.then_inc(sems[op.chan], 16)
                elif op.sig:
                    ins.then_inc(sems[op.chan], 1)
            if e == "sp":
                for c, v in cnt.items():
                    if waited.get(c, 0) < v:
                        h.wait_ge(sems[c], v)

        @block.tensor
        def _(h):
            run("pe", h)

        @block.vector
        def _(h):
            run("dve", h)

        @block.scalar
        def _(h):
            run("act", h)

        @block.gpsimd
        def _(h):
            run("pool", h)

        @block.sync
        def _(h):
            run("sp", h)


class Ctx:
    def __init__(self):
        self.nc = bass.Bass("TRN2", target_bir_lowering=False)
        self.P = Prog()
        self.stack = ExitStack()
        self.stage = None
        self.n = 0
        self.nstage = 0

    def begin_stage(self):
        self.stage = ExitStack()
        self.nstage += 1

    def end_stage(self):
        self.stage.close()
        self.stage = None
        self.P.barrier()

    def sb(self, shape, dt, name=None):
        self.n += 1
        st = self.stage if self.stage is not None else self.stack
        return st.enter_context(self.nc.sbuf_tensor(f"sb{self.nstage}_" + (name or f"t{self.n}"), list(shape), dt))

    def ps(self, shape, dt=F32, name=None):
        self.n += 1
        st = self.stage if self.stage is not None else self.stack
        return st.enter_context(self.nc.psum_tensor(f"ps{self.nstage}_" + (name or f"p{self.n}"), list(shape), dt))

    def din(self, name, shape, dt=F32):
        return self.nc.dram_tensor(name, list(shape), dt, kind="ExternalInput").ap()

    def dout(self, name, shape, dt=F32):
        return self.nc.dram_tensor(name, list(shape), dt, kind="ExternalOutput").ap()

    def dscratch(self, name, shape, dt=F32):
        return self.nc.dram_tensor(name, list(shape), dt).ap()

    def finish(self):
        if self.stage is not None:
            self.end_stage()
        self.P.emit(self.nc, self.stack)
        self.stack.close()
        self.nc._prog = self.P
        return self.nc


def pipeline(units, ph1, ph2, lag=2):
    n = len(units)
    for i in range(n + lag):
        if i < n:
            ph1(units[i])
        if i >= lag:
            ph2(units[i - lag])


class RowStage:
    def __init__(self, cx, ntok):
        self.cx = cx
        self.ntok = ntok
        self.NT = ntok // 128
        self.xacc = cx.sb([128, self.NT, D], F32, "xacc")
        self.xacc_cb = [[Buf(f"xacc{t}_{c}") for c in range(4)] for t in range(self.NT)]
        self.xacc_b = [tuple(self.xacc_cb[t]) for t in range(self.NT)]
        self.acc_tmp = [cx.sb([128, 512], F32, f"acct{k}") for k in range(2)]
        self.acc_tmp_b = [Buf(f"acct{k}") for k in range(2)]
        self.xT = cx.sb([128, 16, ntok], BF16, "xT")
        self.xT_b = [Buf(f"xT{t}") for t in range(self.NT)]
        self.ident = cx.sb([128, 128], F32, "ident")
        self.ident_b = Buf("ident")
        self.pbank = [cx.ps([128, 512], F32, f"bank{i}") for i in range(8)]
        self.pbank_b = [Buf(f"bank{i}", excl=True) for i in range(8)]
        self.rr = 0
        self.wbuf = [cx.sb([128, 8192], BF16, f"wbuf{s}") for s in range(2)]
        self.wbuf_b = [Buf(f"wbuf{s}") for s in range(2)]
        self.wdb = [cx.sb([128, 2, D], BF16, f"wdb{s}") for s in range(2)]
        self.wdb_b = [Buf(f"wdb{s}") for s in range(2)]
        self.actT = [cx.sb([128, 2, ntok], BF16, f"actT{s}") for s in range(2)]
        self.actT_b = [Buf(f"actT{s}") for s in range(2)]
        self.sg = [cx.sb([128, 512], F32, f"sg{s}") for s in range(2)]
        self.sg_b = [Buf(f"sg{s}") for s in range(2)]
        self.gbc = cx.sb([128, 2, D], F32, "gbc")
        self.gb_b = Buf("gbc")
        self.ln_stats = [(cx.sb([128, 4, 6], F32, f"stats{k}"), Buf(f"stats{k}")) for k in range(2)]
        self.ln_mv = cx.sb([128, self.NT, 2], F32, "ln_mv")
        self.ln_sc = cx.sb([128, self.NT], F32, "ln_sc")
        self.ln_nb = cx.sb([128, self.NT], F32, "ln_nb")
        self.ln_b = Buf("ln")
        self.xin = [cx.sb([128, D], F32, f"xin{s}") for s in range(2)]
        self.xin_b = [Buf(f"xin{s}") for s in range(2)]
        self.wslot = 0

    def load_ln(self, lng, lnb, idx):
        gbc = self.gbc
        self.cx.P.add("sp", lambda h: [h.dma_start(out=gbc[:, 0, :], in_=lng[idx:idx + 1, :].partition_broadcast(128)),
                                       h.dma_start(out=gbc[:, 1, :], in_=lnb[idx:idx + 1, :].partition_broadcast(128))],
                      writes=(self.gb_b,), dma=True, key="gbc", ndma=2)

    def ln_all(self, zscale, out_fn=None):
        P = self.cx.P
        NT = self.NT
        mv, sc, nb, lb = self.ln_mv, self.ln_sc, self.ln_nb, self.ln_b
        for tt in range(NT):
            stats, sb_ = self.ln_stats[tt % 2]
            P.add("dve", lambda h, tt=tt, stats=stats: [h.bn_stats(out=stats[:, c, :], in_=self.xacc[:, tt, c * 512:(c + 1) * 512]) for c in range(4)][-1],
                  reads=(self.xacc_b[tt],), writes=(sb_,))
            P.add("dve", lambda h, tt=tt, stats=stats: h.bn_aggr(out=mv[:, tt, :], in_=stats[:, :, :]), reads=(sb_,), writes=(lb,))
        P.add("dve", lambda h: h.tensor_scalar_add(out=sc[:, :], in0=mv[:, :, 1], scalar1=EPS / (zscale * zscale)), reads=(lb,), writes=(lb,))
        P.add("act", lambda h: h.sqrt(out=sc[:, :], in_=sc[:, :]), reads=(lb,), writes=(lb,))
        P.add("dve", lambda h: h.reciprocal(out=sc[:, :], in_=sc[:, :]), reads=(lb,), writes=(lb,))
        P.add("dve", lambda h: h.scalar_tensor_tensor(out=nb[:, :], in0=mv[:, :, 0], scalar=-1.0, in1=sc[:, :], op0=ALU.mult, op1=ALU.mult),
              reads=(lb,), writes=(lb,))
        for tt in range(NT):
            if out_fn is None:
                o, ob = self.xacc[:, tt, :], self.xacc_b[tt]
            else:
                o, ob = out_fn(tt)
            xa = self.xacc[:, tt, :]
            P.add("act", lambda h, xa=xa, tt=tt: h.activation(out=xa, in_=xa, func=AF.Identity, bias=nb[:, tt:tt + 1], scale=sc[:, tt:tt + 1]),
                  reads=(lb, self.xacc_b[tt]), writes=(self.xacc_b[tt],))
            P.add("pool", lambda h, xa=xa: h.tensor_tensor(out=xa, in0=xa, in1=self.gbc[:, 0, :], op=ALU.mult), reads=(self.xacc_b[tt], self.gb_b), writes=(self.xacc_b[tt],))
            P.add("dve", lambda h, xa=xa, o=o: h.tensor_tensor(out=o, in0=xa, in1=self.gbc[:, 1, :], op=ALU.add), reads=(self.xacc_b[tt], self.gb_b), writes=(ob,))
            if out_fn is not None:
                out_fn(tt, done=True)


def load_const(cx, dst, dst_b, src_ap, key):
    cx.P.add("sp", lambda h: h.dma_start(out=dst, in_=src_ap), reads=(), writes=(dst_b,), dma=True, key=key)


def build_xT(cx, st, tt, src, src_b, banks=(4, 5, 6, 7)):
    P = cx.P
    for g in range(4):
        bk = banks[(st.rr) % len(banks)]
        st.rr += 1
        pb = st.pbank[bk]
        for j in range(4):
            kc = g * 4 + j
            P.add("pe", lambda h, pb=pb, j=j, kc=kc: h.transpose(out=pb[:, j * 128:(j + 1) * 128], in_=src[:, kc * 128:(kc + 1) * 128], identity=st.ident[:, :]),
                  reads=(src_b, st.ident_b), writes=(st.pbank_b[bk],))
        eng = "act" if g % 2 == 0 else "dve"
        o = st.xT[:, g * 4:(g + 1) * 4, tt * 128:(tt + 1) * 128]
        i = pb[:, :].rearrange("p (j t) -> p j t", j=4)
        if eng == "act":
            P.add("act", lambda h, o=o, i=i: h.copy(out=o, in_=i), reads=(st.pbank_b[bk],), writes=(st.xT_b[tt],))
        else:
            P.add("dve", lambda h, o=o, i=i: h.tensor_copy(out=o, in_=i), reads=(st.pbank_b[bk],), writes=(st.xT_b[tt],))


def ln_stats(cx, st, tt, zscale, tmp):
    P = cx.P
    stats, mv, sc, nb, stat_b = tmp
    P.add("dve", lambda h: [h.bn_stats(out=stats[:, c, :], in_=st.xacc[:, tt, c * 512:(c + 1) * 512]) for c in range(4)][-1],
          reads=(st.xacc_b[tt],), writes=(stat_b,))
    P.add("dve", lambda h: h.bn_aggr(out=mv[:, :], in_=stats[:, :, :]), reads=(stat_b,), writes=(stat_b,))
    P.add("dve", lambda h: h.tensor_scalar_add(out=sc[:, :], in0=mv[:, 1:2], scalar1=EPS / (zscale * zscale)),
          reads=(stat_b,), writes=(stat_b,))
    P.add("act", lambda h: h.sqrt(out=sc[:, :], in_=sc[:, :]), reads=(stat_b,), writes=(stat_b,))
    P.add("dve", lambda h: h.reciprocal(out=sc[:, :], in_=sc[:, :]), reads=(stat_b,), writes=(stat_b,))
    P.add("dve", lambda h: h.scalar_tensor_tensor(out=nb[:, :], in0=mv[:, 0:1], scalar=-1.0, in1=sc[:, :], op0=ALU.mult, op1=ALU.mult),
          reads=(stat_b,), writes=(stat_b,))


def ln_apply(cx, st, tt, g_bc, b_bc, gb_b, out_ap, out_b, tmp):
    P = cx.P
    xa = st.xacc[:, tt, :]
    stats, mv, sc, nb, stat_b = tmp
    P.add("act", lambda h: h.activation(out=xa, in_=xa, func=AF.Identity, bias=nb[:, 0:1], scale=sc[:, 0:1]),
          reads=(stat_b, st.xacc_b[tt]), writes=(st.xacc_b[tt],))
    P.add("pool", lambda h: h.tensor_tensor(out=xa, in0=xa, in1=g_bc, op=ALU.mult), reads=(st.xacc_b[tt], gb_b), writes=(st.xacc_b[tt],))
    P.add("dve", lambda h: h.tensor_tensor(out=out_ap, in0=xa, in1=b_bc, op=ALU.add), reads=(st.xacc_b[tt], gb_b), writes=(out_b,))


def ffn_stage(cx, st, wg, wu, wd, FB=256):
    P = cx.P
    nblk = DFF // FB
    CPB = FB // 128
    wgu = [w[:, :].rearrange("p (w k f) -> p w k f", w=2, k=16) for w in st.wbuf]
    wgu_b = st.wbuf_b
    wdb, wdb_b = st.wdb, st.wdb_b
    actT, actT_b = st.actT, st.actT_b
    sg, sg_b = st.sg, st.sg_b
    wgv = wg.rearrange("(kc p) f -> p kc f", p=128)
    wuv = wu.rearrange("(kc p) f -> p kc f", p=128)
    wdv = wd.rearrange("(c p) n -> p c n", p=128)
    NTG = st.ntok // 512
    allxT = tuple(st.xT_b)
    gi = 0
    di = 0

    def load_gu(b):
        s = b % 2
        P.add("pool", lambda h: [h.dma_start(out=wgu[s][:, 0, :, :], in_=wgv[:, :, b * FB:(b + 1) * FB]),
                                 h.dma_start(out=wgu[s][:, 1, :, :], in_=wuv[:, :, b * FB:(b + 1) * FB])],
              writes=(wgu_b[s],), dma=True, key=f"wbuf{s}", ndma=2)

    def load_d(b):
        s = b % 2
        P.add("pool", lambda h: h.dma_start(out=wdb[s][:, :, :], in_=wdv[:, b * CPB:(b + 1) * CPB, :]),
              writes=(wdb_b[s],), dma=True, key=f"wdb{s}")

    def gateup(b):
        nonlocal gi
        s = b % 2
        for c in range(CPB):
            for tg in range(NTG):
                bg = gi % 2
                bu = 2 + gi % 2
                gi += 1
                for (w, bk) in ((0, bg), (1, bu)):
                    for kc in range(16):
                        P.add("pe", lambda h, w=w, bk=bk, kc=kc, c=c, tg=tg: h.matmul(
                            st.pbank[bk][:, :], lhsT=wgu[s][:, w, kc, c * 128:(c + 1) * 128],
                            rhs=st.xT[:, kc, tg * 512:(tg + 1) * 512], start=(kc == 0), stop=(kc == 15)),
                            reads=(wgu_b[s],) + allxT, writes=(st.pbank_b[bk],))
                sgi = gi % 2
                P.add("act", lambda h, bg=bg, sgi=sgi: h.activation(out=sg[sgi][:, :], in_=st.pbank[bg][:, :], func=AF.Silu),
                      reads=(st.pbank_b[bg],), writes=(sg_b[sgi],))
                P.add("dve", lambda h, bu=bu, sgi=sgi, c=c, tg=tg: h.tensor_tensor(
                    out=actT[s][:, c, tg * 512:(tg + 1) * 512], in0=sg[sgi][:, :], in1=st.pbank[bu][:, :], op=ALU.mult),
                    reads=(sg_b[sgi], st.pbank_b[bu]), writes=(actT_b[s],))

    def down(b):
        nonlocal di
        s = b % 2
        for tt in range(st.NT):
            for cg in range(4):
                bk = 4 + di % 4
                di += 1
                for c in range(CPB):
                    P.add("pe", lambda h, bk=bk, c=c, tt=tt, cg=cg: h.matmul(
                        st.pbank[bk][:, :], lhsT=actT[s][:, c, tt * 128:(tt + 1) * 128],
                        rhs=wdb[s][:, c, cg * 512:(cg + 1) * 512], start=(c == 0), stop=(c == CPB - 1)),
                        reads=(actT_b[s], wdb_b[s]), writes=(st.pbank_b[bk],))
                xa = st.xacc[:, tt, cg * 512:(cg + 1) * 512]
                xb = st.xacc_cb[tt][cg]
                if di % 3 == 0:
                    k = (di // 3) % 2
                    P.add("act", lambda h, bk=bk, k=k: h.copy(out=st.acc_tmp[k][:, :], in_=st.pbank[bk][:, :]),
                          reads=(st.pbank_b[bk],), writes=(st.acc_tmp_b[k],))
                    P.add("pool", lambda h, xa=xa, k=k: h.tensor_tensor(out=xa, in0=xa, in1=st.acc_tmp[k][:, :], op=ALU.add),
                          reads=(st.acc_tmp_b[k], xb), writes=(xb,))
                else:
                    P.add("dve", lambda h, xa=xa, bk=bk: h.tensor_tensor(out=xa, in0=xa, in1=st.pbank[bk][:, :], op=ALU.add),
                          reads=(st.pbank_b[bk], xb), writes=(xb,))

    load_gu(0)
    load_gu(1)
    load_d(0)
    for b in range(nblk):
        gateup(b)
        if b + 2 < nblk:
            load_gu(b + 2)
        if b >= 1:
            down(b - 1)
        if b + 1 < nblk:
            load_d(b + 1)
    down(nblk - 1)


def load_x_tiles(cx, st, x, scale, with_xT=True):
    P = cx.P
    for tt in range(st.NT):
        s = tt % 2
        P.add("sp", lambda h, s=s, tt=tt: h.dma_start(out=st.xin[s][:, :], in_=x[tt * 128:(tt + 1) * 128, :]),
              writes=(st.xin_b[s],), dma=True, key=f"xin{s}")
        if with_xT:
            build_xT(cx, st, tt, st.xin[s], st.xin_b[s])
        P.add("act", lambda h, s=s, tt=tt: h.mul(out=st.xacc[:, tt, :], in_=st.xin[s][:, :], mul=scale),
              reads=(st.xin_b[s],), writes=(st.xacc_b[tt],))


def refresh_xT(cx, st, scale):
    for tt in range(st.NT):
        build_xT(cx, st, tt, st.xacc[:, tt, :], st.xacc_b[tt])
        if scale != 1.0:
            cx.P.add("act", lambda h, tt=tt: h.mul(out=st.xacc[:, tt, :], in_=st.xacc[:, tt, :], mul=scale),
                     reads=(st.xacc_b[tt],), writes=(st.xacc_b[tt],))


def store_out(cx, st, y, yb=None):
    def out_fn(tt, done=False):
        s = tt % 2
        if done:
            cx.P.add("sp", lambda h: h.dma_start(out=y[tt * 128:(tt + 1) * 128, :], in_=st.xin[s][:, :]),
                     reads=(st.xin_b[s],), dma=True, key=f"yo{s}")
            if yb is not None:
                cx.P.add("pool", lambda h: h.dma_start(out=yb[tt * 128:(tt + 1) * 128, :], in_=st.xin[s][:, :]),
                         reads=(st.xin_b[s],), dma=True, key=f"yb{s}")
            return None
        return st.xin[s][:, :], st.xin_b[s]
    return out_fn


def dense_acc(cx, st, w, lhs_fn, nk, lhs_bufs, post):
    P = cx.P
    wv = w.rearrange("(kc p) n -> p kc n", p=128)
    for cg in range(4):
        s = st.wslot % 2
        st.wslot += 1
        wt = st.wbuf[s][:, 0:nk * 512].rearrange("p (k f) -> p k f", k=nk)
        P.add("pool", lambda h, wt=wt, cg=cg: h.dma_start(out=wt, in_=wv[:, :, cg * 512:(cg + 1) * 512]),
              writes=(st.wbuf_b[s],), dma=True, key=f"wbuf{s}")
        for tt in range(st.NT):
            bk = 4 + st.rr % 4
            st.rr += 1
            for kc in range(nk):
                P.add("pe", lambda h, bk=bk, kc=kc, tt=tt, wt=wt: h.matmul(st.pbank[bk][:, :], lhsT=lhs_fn(kc, tt), rhs=wt[:, kc, :],
                                                                        start=(kc == 0), stop=(kc == nk - 1)),
                      reads=(st.wbuf_b[s],) + tuple(lhs_bufs), writes=(st.pbank_b[bk],))
            post(tt, cg, bk)


def emit_A(cx, x, y, wg, wu, wd, lng, lnb, identd, ntok=TOK, yb=None):
    st = RowStage(cx, ntok)
    load_const(cx, st.ident[:, :], st.ident_b, identd[:, :], "ident")
    st.load_ln(lng, lnb, 0)
    load_x_tiles(cx, st, x, 2.0 * ALPHA)
    ffn_stage(cx, st, wg, wu, wd)
    st.ln_all(0.5, store_out(cx, st, y, yb))


def emit_C(cx, x, oT, pp, wo, wg, wu, wd, wpe, wpg, lng, lnb, identd, y, nxt=None, ntok=TOK, yb=None, pre=None):
    P = cx.P
    with_next = nxt is not None
    odeps = tuple(pre()) if pre else ()
    st = RowStage(cx, ntok)
    load_const(cx, st.ident[:, :], st.ident_b, identd[:, :], "ident")
    st.load_ln(lng, lnb, 1)
    load_x_tiles(cx, st, x, ALPHA, with_xT=False)
    if callable(oT):
        for kc in range(16):
            sl = kc % 2
            P.add("sp", lambda h, kc=kc, sl=sl: h.dma_start(out=st.xin[sl][:, 0:ntok], in_=oT(h, kc)), writes=(st.xin_b[sl],), dma=True, key=f"xin{sl}")
            if kc % 2 == 0:
                P.add("act", lambda h, kc=kc, sl=sl: h.copy(out=st.xT[:, kc, :], in_=st.xin[sl][:, 0:ntok]), reads=(st.xin_b[sl],), writes=tuple(st.xT_b))
            else:
                P.add("dve", lambda h, kc=kc, sl=sl: h.tensor_copy(out=st.xT[:, kc, :], in_=st.xin[sl][:, 0:ntok]), reads=(st.xin_b[sl],), writes=tuple(st.xT_b))
    else:
        oTv = oT.rearrange("(kc p) t -> p kc t", p=128)
        P.add("pool", lambda h: h.dma_start(out=st.xT[:, :, :], in_=oTv), reads=odeps, writes=tuple(st.xT_b), dma=True, key="oT")

    def post_add(tt, cg, bk):
        xa = st.xacc[:, tt, cg * 512:(cg + 1) * 512]
        P.add("dve", lambda h: h.tensor_tensor(out=xa, in0=xa, in1=st.pbank[bk][:, :], op=ALU.add),
              reads=(st.pbank_b[bk], st.xacc_cb[tt][cg]), writes=(st.xacc_cb[tt][cg],))

    dense_acc(cx, st, wo, lambda kc, tt: st.xT[:, kc, tt * 128:(tt + 1) * 128], 16, st.xT_b, post_add)
    st.ln_all(1.0)
    st.load_ln(lng, lnb, 2)
    refresh_xT(cx, st, 2.0 * ALPHA)
    ffn_stage(cx, st, wg, wu, wd)
    st.ln_all(0.5)
    st.load_ln(lng, lnb, 3)
    refresh_xT(cx, st, 1.0)
    pT = cx.sb([128, 2, ntok], BF16, "pT")
    pT_b = Buf("pT")
    pin = cx.sb([128, 256], F32, "pin")
    pin_b = Buf("pin")
    for tt in range(st.NT):
        P.add("sp", lambda h, tt=tt: h.dma_start(out=pin[:, :], in_=pp[tt * 128:(tt + 1) * 128, :]), writes=(pin_b,), dma=True, key="pin")
        bk = 4 + st.rr % 4
        st.rr += 1
        for j in range(2):
            P.add("pe", lambda h, j=j, bk=bk: h.transpose(out=st.pbank[bk][:, j * 128:(j + 1) * 128], in_=pin[:, j * 128:(j + 1) * 128], identity=st.ident[:, :]),
                  reads=(pin_b, st.ident_b), writes=(st.pbank_b[bk],))
        P.add("act", lambda h, tt=tt, bk=bk: h.copy(out=pT[:, :, tt * 128:(tt + 1) * 128], in_=st.pbank[bk][:, 0:256].rearrange("p (j t) -> p j t", j=2)),
              reads=(st.pbank_b[bk],), writes=(pT_b,))
    wpet = cx.sb([128, 2, D], BF16, "wpe")
    wpe_b = Buf("wpe")
    P.add("pool", lambda h: h.dma_start(out=wpet[:, :, :], in_=wpe.rearrange("(c p) n -> p c n", p=128)), writes=(wpe_b,), dma=True, key="wpe")
    et, et_b = st.acc_tmp, st.acc_tmp_b
    ei = [0]

    def post_ple(tt, cg, bk):
        be = ei[0] % 2
        s = ei[0] % 2
        ei[0] += 1
        for c in range(2):
            P.add("pe", lambda h, c=c: h.matmul(st.pbank[be][:, :], lhsT=pT[:, c, tt * 128:(tt + 1) * 128], rhs=wpet[:, c, cg * 512:(cg + 1) * 512],
                                               start=(c == 0), stop=(c == 1)),
                  reads=(pT_b, wpe_b), writes=(st.pbank_b[be],))
        P.add("act", lambda h: h.activation(out=st.sg[s][:, :], in_=st.pbank[bk][:, :], func=AF.Sigmoid),
              reads=(st.pbank_b[bk],), writes=(st.sg_b[s],))
        P.add("dve", lambda h: h.tensor_tensor(out=et[s][:, :], in0=st.sg[s][:, :], in1=st.pbank[be][:, :], op=ALU.mult),
              reads=(st.sg_b[s], st.pbank_b[be]), writes=(et_b[s],))
        xa = st.xacc[:, tt, cg * 512:(cg + 1) * 512]
        P.add("dve", lambda h: h.scalar_tensor_tensor(out=xa, in0=xa, scalar=ALPHA, in1=et[s][:, :], op0=ALU.mult, op1=ALU.add),
              reads=(et_b[s], st.xacc_cb[tt][cg]), writes=(st.xacc_cb[tt][cg],))

    dense_acc(cx, st, wpg, lambda kc, tt: st.xT[:, kc, tt * 128:(tt + 1) * 128], 16, st.xT_b, post_ple)
    if not with_next:
        st.ln_all(1.0, store_out(cx, st, y))
        return
    wg2, wu2, wd2, lng2, lnb2 = nxt
    st.ln_all(1.0)
    st.load_ln(lng2, lnb2, 0)
    refresh_xT(cx, st, 2.0 * ALPHA)
    ffn_stage(cx, st, wg2, wu2, wd2)
    st.ln_all(0.5, store_out(cx, st, y, yb))


NCOL = 3088
PW = 400


def mix_consts():
    c = {}
    c["ident"] = np.eye(128, dtype=np.float32)
    rc = np.zeros((128, 8), np.float32)
    p = np.arange(128)
    rc[:, 0] = 10000.0 ** (-(2.0 * (p % 32)) / 64.0)
    rc[:, 1] = 10000.0 ** (-(2.0 * (p % 64)) / 128.0)
    rc[:, 2] = np.where((p % 64) < 32, -2.0, 2.0)
    rc[:, 3] = np.where(p < 64, -2.0, 2.0)
    rc[:, 6] = math.pi / 2.0
    rc[:, 4] = -math.pi
    rc[:, 5] = EPS
    c["rc"] = rc
    perm = np.zeros((2, 128, 128), np.float32)
    for m in range(128):
        perm[0, (m // 64) * 64 + ((m % 64) + 32) % 64, m] = 1.0
        perm[1, (m + 64) % 128, m] = 1.0
    c["perm"] = perm
    k = np.arange(128)[:, None]
    q = np.arange(512)[None, :]
    c["cmask"] = (np.arange(128)[:, None] <= np.arange(896)[None, :] - 384).astype(np.float32)
    tri = (np.arange(128)[:, None] <= np.arange(128)[None, :]).astype(np.float32)
    c["gcst"] = np.stack([tri, 1.0 - tri, np.ones((128, 128), np.float32)])
    own = np.arange(8)[:, None]
    n = np.arange(8)[None, :]
    c["nmask"] = np.where(n < own, 0.0, -1e30).astype(np.float32).reshape(1, 64)
    c["pmask"] = (n < own).astype(np.float32).reshape(1, 64)
    es = np.zeros((8, 8, 128), np.float32)
    for j in range(8):
        es[j, j, :] = 1.0
    c["esel"] = es
    return c


def emit_mix(cx, A):
    P = cx.P
    S = SEQ
    NT = S // 128
    x, pos, wsel2, dlam, lamc, dng, ggu2, ggb2, gng = (A[k] for k in ("x", "pos", "wsel2", "dlam", "lamc", "dng", "ggu2", "ggb2", "gng"))
    d_ident, d_rc, d_perm, d_cmask, d_gcst, d_nmask, d_pmask, d_esel = (A[k] for k in ("ident", "rc", "perm", "cmask", "gcst", "nmask", "pmask", "esel"))
    out = A["out"]
    phases = ("diff", "gla", "moba")
    cur = {"hh": 0}
    xdeps = tuple(A["pre"]()) if A.get("pre") else ()
    HH = tuple(range(A.get("nhh", 2)))
    xsrc = A.get("x_dyn") or (lambda h, tt: x[tt * 128:(tt + 1) * 128, :])
    row_of = A.get("row_of") or (lambda kind, hh, j: {"d": (3 * hh + j) * 128, "g": 768 + (2 * hh + j) * 128, "m": 1280 + (3 * hh + j) * 128}[kind])

    def V(eng, fn, r=(), w=()):
        return P.add(eng, fn, reads=r, writes=w)

    def LD(dst, src, name, cast=False, n=1):
        b = Buf(name)
        P.add("pool" if cast else "sp", lambda h: h.dma_start(out=dst, in_=src), writes=(b,), dma=True, key=name)
        return b

    ident = cx.sb([128, 128], F32, "ident")
    ident_b = LD(ident[:, :], d_ident[:, :], "ident")
    rc = cx.sb([128, 8], F32, "rc")
    rc_b = LD(rc[:, :], d_rc[:, :], "rc")
    perm = cx.sb([128, 2, 128], BF16, "perm")
    perm_b = LD(perm[:, :, :], d_perm.rearrange("a p m -> p a m"), "perm", cast=True)
    cmask = cx.sb([128, 896], BF16, "cmask")
    cmask_b = LD(cmask[:, :], d_cmask[:, :], "cmask", cast=True)
    gcst = cx.sb([128, 3, 128], F32, "gcst")
    gcst_b = LD(gcst[:, :, :], d_gcst.rearrange("a p m -> p a m"), "gcst")
    ones_bf = cx.sb([128, 128], BF16, "ones_bf")
    onesbf_b = LD(ones_bf[:, :], d_gcst[2, :, :], "ones_bf", cast=True)
    tri_bf = cmask[:, 384:512]
    nmask = cx.sb([128, 64], F32, "nmask")
    nmask_b = LD(nmask[:, :], d_nmask[0:1, :].partition_broadcast(128), "nmask")
    pmask = cx.sb([128, 64], F32, "pmask")
    pmask_b = LD(pmask[:, :], d_pmask[0:1, :].partition_broadcast(128), "pmask")
    esel = cx.sb([8, 8, 128], BF16, "esel")
    esel_b = LD(esel[:, :, :], d_esel[:, :, :], "esel", cast=True)
    dl = cx.sb([128, 256], F32, "dl")
    dl_b = LD(dl[:, :], dlam[0:1, :].partition_broadcast(128), "dl")
    lc = cx.sb([128, 2], F32, "lc")
    lc_b = LD(lc[:, :], lamc[0:1, :].partition_broadcast(128), "lc")
    dngt = cx.sb([128, 1], F32, "dngt")
    dng_b = LD(dngt[:, :], dng[:, :], "dng")
    gngt = cx.sb([128, 1], F32, "gngt")
    gng_b = LD(gngt[:, :], gng[:, :], "gng")
    ggut_l = [cx.sb([16, 128], F32, f"ggut{k}") for k in HH]
    ggu_bl = [LD(ggut_l[k][:, :], ggu2[k, :, :], f"ggu{k}") for k in HH]
    ggbt_l = [cx.sb([128, 128], F32, f"ggbt{k}") for k in HH]
    ggb_bl = [LD(ggbt_l[k][:, :], ggb2[k, 0:1, :].partition_broadcast(128), f"ggb{k}") for k in HH]

    pbank = [cx.ps([128, 512], F32, f"bank{i}") for i in range(8)]
    pb_b = [Buf(f"bank{i}", excl=True) for i in range(8)]
    rr = {"s": 0, "a": 0, "g": 0}

    def sbank():
        rr["s"] += 1
        return rr["s"] % 2

    def abank():
        rr["a"] += 1
        return 2 + rr["a"] % 2

    def gbank():
        rr["g"] += 1
        return rr["g"] % 8

    xT = cx.sb([128, 16, S], BF16, "xT")
    xT_b = [Buf(f"xT{t}") for t in range(NT)]
    stg = [cx.sb([128, S], F32, f"stg{s}") for s in range(2)]
    stg_b = [Buf(f"stg{s}") for s in range(2)]
    k4 = 0
    for tt in range(NT):
        s = tt % 2
        P.add("pool" if A.get("x_cast") else "sp", lambda h, s=s, tt=tt: h.dma_start(out=stg[s][:, :], in_=xsrc(h, tt)),
              reads=xdeps, writes=(stg_b[s],), dma=True, key=f"stg{s}")
        for g in range(4):
            bk = k4 % 4
            k4 += 1
            for j in range(4):
                kc = g * 4 + j
                V("pe", lambda h, bk=bk, j=j, kc=kc, s=s: h.transpose(out=pbank[bk][:, j * 128:(j + 1) * 128], in_=stg[s][:, kc * 128:(kc + 1) * 128], identity=ident[:, :]),
                  (stg_b[s], ident_b), (pb_b[bk],))
            o = xT[:, g * 4:(g + 1) * 4, tt * 128:(tt + 1) * 128]
            i = pbank[bk][:, :].rearrange("p (j t) -> p j t", j=4)
            if g % 2 == 0:
                V("act", lambda h, o=o, i=i: h.copy(out=o, in_=i), (pb_b[bk],), (xT_b[tt],))
            else:
                V("dve", lambda h, o=o, i=i: h.tensor_copy(out=o, in_=i), (pb_b[bk],), (xT_b[tt],))
    allxT = tuple(xT_b)

    wp = [cx.sb([128, 16, PW], BF16, f"wp{s}") for s in range(2)]
    wp_b = [Buf(f"wp{s}") for s in range(2)]
    wvs = [wsel2[k].rearrange("(kc p) f -> p kc f", p=128) for k in HH]
    wi = [0]

    def load_piece(c0, n):
        s = wi[0] % 2
        wi[0] += 1
        wv = wvs[cur["hh"]]
        P.add("pool", lambda h: h.dma_start(out=wp[s][:, :, 0:n], in_=wv[:, :, c0:c0 + n]), writes=(wp_b[s],), dma=True, key=f"wp{s}")
        return s

    def proj_fm(bk, s, c0, m, tg):
        for kc in range(16):
            V("pe", lambda h, kc=kc: h.matmul(pbank[bk][0:m, :], lhsT=wp[s][:, kc, c0:c0 + m], rhs=xT[:, kc, tg * 512:(tg + 1) * 512],
                                             start=(kc == 0), stop=(kc == 15)), (wp_b[s],) + allxT, (pb_b[bk],))

    def proj_tm(bk, s, c0, n, tt, o0=0):
        for kc in range(16):
            V("pe", lambda h, kc=kc: h.matmul(pbank[bk][:, o0:o0 + n], lhsT=xT[:, kc, tt * 128:(tt + 1) * 128], rhs=wp[s][:, kc, c0:c0 + n],
                                             start=(kc == 0), stop=(kc == 15)), (wp_b[s],) + allxT, (pb_b[bk],))

    posf = cx.sb([128, S], F32, "posf")
    posf_b = Buf("posf")
    posi = cx.sb([128, 512], I32, "posi")
    posi_b = Buf("posi")
    for c4 in range(4):
        P.add("sp", lambda h, c4=c4: h.dma_start(out=posi[:, :], in_=pos[0:1, c4 * 512:(c4 + 1) * 512].partition_broadcast(128)), writes=(posi_b,), dma=True, key="posi")
        V("dve", lambda h, c4=c4: h.tensor_copy(out=posf[:, c4 * 512:(c4 + 1) * 512], in_=posi[:, :]), (posi_b,), (posf_b,))
    ctab = cx.sb([128, S], F32, "ctab")
    stab = cx.sb([128, S], F32, "stab")
    tab_b = Buf("tab")

    def make_tables(kind):
        tb = (tab_b, stg_b[0], stg_b[1], posi_b)
        V("dve", lambda h: h.tensor_scalar(out=stg[0][:, :], in0=posf[:, :], scalar1=rc[:, kind:kind + 1], scalar2=None, op0=ALU.mult), (posf_b, rc_b), (stg_b[0],))
        for c4 in range(4):
            cs = slice(c4 * 512, (c4 + 1) * 512)
            V("dve", lambda h, cs=cs: h.tensor_scalar(out=posi[:, :], in0=stg[0][:, cs], scalar1=1.0 / (2.0 * math.pi), scalar2=None, op0=ALU.mult), (stg_b[0],), (posi_b,))
            V("dve", lambda h, cs=cs: h.tensor_copy(out=stg[1][:, cs], in_=posi[:, :]), (posi_b, stg_b[1]), (stg_b[1],))
        V("dve", lambda h: h.scalar_tensor_tensor(out=stg[0][:, :], in0=stg[1][:, :], scalar=-2.0 * math.pi, in1=stg[0][:, :], op0=ALU.mult, op1=ALU.add), (stg_b[0], stg_b[1]), (stg_b[0],))
        V("act", lambda h: h.activation(out=stg[1][:, :], in_=stg[0][:, :], func=AF.Sin, scale=0.5), (stg_b[0],), (stg_b[1],))
        V("act", lambda h: h.activation(out=ctab[:, :], in_=stg[0][:, :], func=AF.Sin, bias=rc[:, 6:7], scale=-0.5), (stg_b[0], rc_b), (tab_b,))
        V("dve", lambda h: h.scalar_tensor_tensor(out=stab[:, :], in0=stg[1][:, :], scalar=rc[:, 2 + kind:3 + kind], in1=ctab[:, :], op0=ALU.mult, op1=ALU.mult), (stg_b[1], rc_b, tab_b), (tab_b,))
        V("dve", lambda h: h.tensor_tensor(out=ctab[:, :], in0=stg[1][:, :], in1=stg[1][:, :], op=ALU.mult), (stg_b[1], tab_b), (tab_b,))
        V("dve", lambda h: h.tensor_scalar(out=ctab[:, :], in0=ctab[:, :], scalar1=-2.0, scalar2=1.0, op0=ALU.mult, op1=ALU.add), (tab_b,), (tab_b,))

    qT2 = [cx.sb([128, 2, S], BF16, f"qT{s}") for s in range(2)]
    qT = [t[:, 0, :] for t in qT2]
    kT = [cx.sb([128, S], BF16, f"kT{s}") for s in range(2)]
    vv = [cx.sb([128, NT, 128], BF16, f"vv{s}") for s in range(2)]
    qT_b = [Buf(f"qT{s}") for s in range(2)]
    kT_b = [Buf(f"kT{s}") for s in range(2)]
    vv_b = [Buf(f"vv{s}") for s in range(2)]
    qb = [cx.sb([128, 512], BF16, f"qb{s}") for s in range(2)]
    qb_b = [Buf(f"qb{s}") for s in range(2)]
    t1 = [cx.sb([128, 512], F32, f"t1{s}") for s in range(2)]
    t1_b = [Buf(f"t1{s}") for s in range(2)]
    t2 = [cx.sb([128, 512], F32, f"t2{s}") for s in range(2)]
    t2_b = [Buf(f"t2{s}") for s in range(2)]
    NPT = 4
    pT = [cx.sb([128, 512], BF16, f"pT{s}") for s in range(NPT)]
    pT_b = [Buf(f"pT{s}") for s in range(NPT)]
    fa = cx.sb([128, 512], F32, "fa")
    fb = cx.sb([128, 512], F32, "fb")
    fc = cx.sb([128, 512], F32, "fc")
    fa_b, fb_b, fc_b = Buf("fa"), Buf("fb"), Buf("fc")
    kms = cx.sb([128, 8], F32, "kms")
    kms_b = Buf("kms")
    kmb = cx.sb([128, 8], BF16, "kmb")
    kmb_b = Buf("kmb")
    ri = [0]
    pi = [0]

    def rope_proj(s, c0, dst, dst_b, pk, want_kms=False, split=None):
        for tg in range(4):
            bk = sbank()
            proj_fm(bk, s, c0, 128, tg)
            i = ri[0] % 2
            ri[0] += 1
            cols = slice(tg * 512, (tg + 1) * 512)
            V("act", lambda h, i=i, bk=bk: h.copy(out=qb[i][:, :], in_=pbank[bk][:, :]), (pb_b[bk],), (qb_b[i],))
            V("dve", lambda h, i=i, bk=bk, cols=cols: h.tensor_tensor(out=t1[i][:, :], in0=pbank[bk][:, :], in1=ctab[:, cols], op=ALU.mult), (pb_b[bk], tab_b), (t1_b[i],))
            b2 = abank()
            V("pe", lambda h, i=i, b2=b2: h.matmul(pbank[b2][:, :], lhsT=perm[:, pk, :], rhs=qb[i][:, :], start=True, stop=True), (perm_b, qb_b[i]), (pb_b[b2],))
            V("dve", lambda h, i=i, b2=b2, cols=cols: h.tensor_tensor(out=t2[i][:, :], in0=pbank[b2][:, :], in1=stab[:, cols], op=ALU.mult), (pb_b[b2], tab_b), (t2_b[i],))
            V("pool", lambda h, i=i: h.tensor_tensor(out=t1[i][:, :], in0=t1[i][:, :], in1=t2[i][:, :], op=ALU.add), (t1_b[i], t2_b[i]), (t1_b[i],))
            if split is None:
                V("act", lambda h, i=i, cols=cols: h.copy(out=dst[:, cols], in_=t1[i][:, :]), (t1_b[i],), (dst_b,))
            else:
                for m in range(2):
                    V("act", lambda h, i=i, cols=cols, m=m: h.copy(out=split[m * 64:(m + 1) * 64, m, cols], in_=t1[i][m * 64:(m + 1) * 64, :]), (t1_b[i],), (dst_b,))
            if want_kms:
                V("dve", lambda h, i=i, tg=tg: h.tensor_reduce(out=kms[:, 2 * tg:2 * tg + 2], in_=t1[i][:, :].rearrange("p (b k) -> p b k", b=2), axis=AX.X, op=ALU.add),
                  (t1_b[i],), (kms_b,))

    def v_proj(s, c0, dst, dst_b):
        for tt in range(NT):
            bk = sbank()
            proj_tm(bk, s, c0, 128, tt)
            if tt % 2 == 0:
                V("act", lambda h, bk=bk, tt=tt: h.copy(out=dst[:, tt, :], in_=pbank[bk][:, 0:128]), (pb_b[bk],), (dst_b,))
            else:
                V("dve", lambda h, bk=bk, tt=tt: h.tensor_copy(out=dst[:, tt, :], in_=pbank[bk][:, 0:128]), (pb_b[bk],), (dst_b,))

    def store_head(kind, hh, j, s):
        r0 = row_of(kind, hh, j)
        if A.get("out_split"):
            P.add("pool" if A.get("out_cast") else "sp", lambda h: [h.dma_start(out=out[t, r0:r0 + 128, :], in_=stg[s][:, t * (S // 2):(t + 1) * (S // 2)]) for t in range(2)],
                  reads=(stg_b[s],), dma=True, key=f"so{s}", ndma=2)
        else:
            P.add("sp", lambda h: h.dma_start(out=out[r0:r0 + 128, :], in_=stg[s][:, :]), reads=(stg_b[s],), dma=True, key=f"so{s}")

    def rms_over_partitions(src, src_b, n):
        V("act", lambda h: h.activation(out=fc[:, 0:n], in_=src, func=AF.Square), (src_b,), (fc_b,))
        b2 = abank()
        V("pe", lambda h: h.matmul(pbank[b2][:, 0:n], lhsT=gcst[:, 2, :], rhs=fc[:, 0:n], start=True, stop=True), (gcst_b, fc_b), (pb_b[b2],))
        V("act", lambda h: h.activation(out=fb[:, 0:n], in_=pbank[b2][:, 0:n], func=AF.Sqrt, bias=rc[:, 5:6], scale=1.0 / 128.0), (pb_b[b2], rc_b), (fb_b,))
        V("dve", lambda h: h.reciprocal(out=fb[:, 0:n], in_=fb[:, 0:n]), (fb_b,), (fb_b,))

    hs = [0]
    so = [0]

    if "diff" in phases:
        make_tables(0)
        lam = cx.sb([128, 4], F32, "lam")
        lam_b = Buf("lam")
        V("dve", lambda h: h.tensor_tensor(out=fa[:, 0:64], in0=dl[:, 0:64], in1=dl[:, 64:128], op=ALU.mult), (dl_b,), (fa_b,))
        V("dve", lambda h: h.tensor_tensor(out=fa[:, 64:128], in0=dl[:, 128:192], in1=dl[:, 192:256], op=ALU.mult), (dl_b, fa_b), (fa_b,))
        V("dve", lambda h: h.tensor_reduce(out=lam[:, 0:2], in_=fa[:, 0:128].rearrange("p (a k) -> p a k", a=2), axis=AX.X, op=ALU.add), (fa_b,), (lam_b,))
        V("act", lambda h: h.activation(out=lam[:, 0:2], in_=lam[:, 0:2], func=AF.Exp), (lam_b,), (lam_b,))
        V("dve", lambda h: h.tensor_tensor(out=lam[:, 2:3], in0=lam[:, 1:2], in1=lam[:, 0:1], op=ALU.subtract), (lam_b,), (lam_b,))
        V("dve", lambda h: h.tensor_tensor(out=lam[:, 2:3], in0=lam[:, 2:3], in1=lc[:, 0:1], op=ALU.subtract), (lam_b, lc_b), (lam_b,))
        V("dve", lambda h: h.tensor_tensor(out=lam[:, 3:4], in0=dngt[:, 0:1], in1=lc[:, 1:2], op=ALU.mult), (dng_b, lc_b, lam_b), (lam_b,))
        for k2 in range(2):
            V("pool", lambda h, k2=k2: h.memset(qT2[k2][:, :, :], 0.0), (), (qT_b[k2],))
        for hh, hd in [(u, v) for u in HH for v in range(3)]:
            cur["hh"] = hh
            s = load_piece(hd * 384, 384)
            q = hs[0] % 2
            hs[0] += 1
            rope_proj(s, 0, None, qT_b[q], 0, split=qT2[q])
            rope_proj(s, 128, kT[q], kT_b[q], 0)
            v_proj(s, 256, vv[q], vv_b[q])
            so_s = so[0] % 2
            so[0] += 1
            for qg in range(4):
                nkc = 4 * qg + 4
                units = [(kc, m) for kc in range(nkc) for m in range(2)]
                ctx = {}

                def ph1(u):
                    kc, m = u
                    bk = sbank()
                    V("pe", lambda h: h.matmul(pbank[bk][:, :], lhsT=kT[q][:, kc * 128:(kc + 1) * 128],
                                               rhs=qT2[q][:, m, qg * 512:(qg + 1) * 512], start=True, stop=True),
                      (kT_b[q], qT_b[q]), (pb_b[bk],))
                    pj = pi[0] % NPT
                    pi[0] += 1
                    ctx[u] = pj
                    V("act", lambda h: h.activation(out=pT[pj][:, :], in_=pbank[bk][:, :], func=AF.Exp, scale=0.125), (pb_b[bk],), (pT_b[pj],))
                    if kc >= 4 * qg:
                        j = kc - 4 * qg
                        V("pool", lambda h: h.tensor_tensor(out=pT[pj][:, :], in0=pT[pj][:, :], in1=cmask[:, (3 - j) * 128:(3 - j) * 128 + 512], op=ALU.mult), (pT_b[pj], cmask_b), (pT_b[pj],))

                def ph2(u):
                    kc, m = u
                    pj = ctx[u]
                    V("pe", lambda h: h.matmul(pbank[4 + m][:, :], lhsT=vv[q][:, kc, :], rhs=pT[pj][:, :], start=(kc == 0), stop=(kc == nkc - 1)),
                      (vv_b[q], pT_b[pj]), (pb_b[4 + m],))
                    V("pe", lambda h: h.matmul(pbank[6 + m][:, :], lhsT=ones_bf[:, :], rhs=pT[pj][:, :], start=(kc == 0), stop=(kc == nkc - 1)),
                      (onesbf_b, pT_b[pj]), (pb_b[6 + m],))

                pipeline(units, ph1, ph2)
                V("dve", lambda h: h.reciprocal(out=fb[:, :], in_=pbank[6][:, :]), (pb_b[6],), (fb_b,))
                V("dve", lambda h: h.tensor_tensor(out=fa[:, :], in0=pbank[4][:, :], in1=fb[:, :], op=ALU.mult), (pb_b[4], fb_b), (fa_b,))
                V("dve", lambda h: h.reciprocal(out=fb[:, :], in_=pbank[7][:, :]), (pb_b[7], fb_b), (fb_b,))
                V("dve", lambda h: h.tensor_tensor(out=fc[:, :], in0=pbank[5][:, :], in1=fb[:, :], op=ALU.mult), (pb_b[5], fb_b), (fc_b,))
                V("dve", lambda h: h.scalar_tensor_tensor(out=fa[:, :], in0=fc[:, :], scalar=lam[:, 2:3], in1=fa[:, :], op0=ALU.mult, op1=ALU.add), (fc_b, fa_b, lam_b), (fa_b,))
                rms_over_partitions(fa[:, :], fa_b, 512)
                V("dve", lambda h, qg=qg: h.scalar_tensor_tensor(out=stg[so_s][:, qg * 512:(qg + 1) * 512], in0=fa[:, :], scalar=lam[:, 3:4], in1=fb[:, :], op0=ALU.mult, op1=ALU.mult),
                  (fa_b, fb_b, lam_b), (stg_b[so_s],))
            store_head("d", hh, hd, so_s)

    if "moba" in phases:
        make_tables(1)
        selT = cx.sb([8, S], BF16, "selT")
        selT_b = Buf("selT")
        gm = cx.sb([128, 8], F32, "gm")
        top8 = cx.sb([128, 8], F32, "top8")
        sel = cx.sb([128, 8], F32, "sel")
        gm_b = Buf("gm")
        SC = 128.0 ** -0.5
        for hh, hd in [(u, v) for u in HH for v in range(3)]:
            cur["hh"] = hh
            s = load_piece(1936 + hd * 384, 384)
            q = hs[0] % 2
            hs[0] += 1
            rope_proj(s, 0, qT[q], qT_b[q], 1)
            rope_proj(s, 128, kT[q], kT_b[q], 1, want_kms=True)
            v_proj(s, 256, vv[q], vv_b[q])
            V("act", lambda h: h.mul(out=kmb[:, :], in_=kms[:, :], mul=1.0 / 256.0), (kms_b,), (kmb_b,))
            for tt in range(NT):
                own = tt // 2
                b2 = abank()
                V("pe", lambda h, b2=b2, tt=tt: h.matmul(pbank[b2][:, 0:8], lhsT=qT[q][:, tt * 128:(tt + 1) * 128], rhs=kmb[:, :], start=True, stop=True),
                  (qT_b[q], kmb_b), (pb_b[b2],))
                V("dve", lambda h, b2=b2, own=own: h.tensor_tensor(out=gm[:, :], in0=pbank[b2][:, 0:8], in1=nmask[:, own * 8:(own + 1) * 8], op=ALU.add), (pb_b[b2], nmask_b), (gm_b,))
                V("dve", lambda h: h.max(out=top8[:, :], in_=gm[:, :]), (gm_b,), (gm_b,))
                V("dve", lambda h, own=own: h.scalar_tensor_tensor(out=sel[:, :], in0=gm[:, :], scalar=top8[:, 2:3], in1=pmask[:, own * 8:(own + 1) * 8], op0=ALU.is_ge, op1=ALU.mult),
                  (gm_b, pmask_b), (gm_b,))
                b3 = abank()
                V("pe", lambda h, b3=b3: h.transpose(out=pbank[b3][0:8, 0:128], in_=sel[:, :], identity=ident[:, :]), (gm_b, ident_b), (pb_b[b3],))
                V("act", lambda h, b3=b3, tt=tt: h.copy(out=selT[0:8, tt * 128:(tt + 1) * 128], in_=pbank[b3][0:8, 0:128]), (pb_b[b3],), (selT_b,))
            so_s = so[0] % 2
            so[0] += 1
            for qblk in range(8):
                ab = 4 + 2 * (qblk % 2)
                zb = ab + 1
                nkc = 2 * qblk + 2
                qc = slice(qblk * 256, (qblk + 1) * 256)
                ctx = {}
                mbs = {}

                def ph1(kc):
                    n = kc // 2
                    bk = sbank()
                    V("pe", lambda h: h.matmul(pbank[bk][:, 0:256], lhsT=kT[q][:, kc * 128:(kc + 1) * 128], rhs=qT[q][:, qc], start=True, stop=True),
                      (kT_b[q], qT_b[q]), (pb_b[bk],))
                    pj = pi[0] % NPT
                    pi[0] += 1
                    ctx[kc] = pj
                    if n < qblk and kc % 2 == 0:
                        mb = abank()
                        mbs[n] = mb
                        V("pe", lambda h: h.matmul(pbank[mb][:, 0:256], lhsT=esel[0:8, n, :], rhs=selT[0:8, qc], start=True, stop=True),
                          (esel_b, selT_b), (pb_b[mb],))
                    V("act", lambda h: h.activation(out=pT[pj][:, 0:256], in_=pbank[bk][:, 0:256], func=AF.Exp, scale=SC), (pb_b[bk],), (pT_b[pj],))
                    if n < qblk:
                        mb = mbs[n]
                        V("dve", lambda h: h.tensor_tensor(out=pT[pj][:, 0:256], in0=pT[pj][:, 0:256], in1=pbank[mb][:, 0:256], op=ALU.mult), (pT_b[pj], pb_b[mb]), (pT_b[pj],))
                    else:
                        j = kc % 2
                        V("pool", lambda h: h.tensor_tensor(out=pT[pj][:, 0:256], in0=pT[pj][:, 0:256], in1=cmask[:, (3 - j) * 128:(3 - j) * 128 + 256], op=ALU.mult), (pT_b[pj], cmask_b), (pT_b[pj],))

                def ph2(kc):
                    pj = ctx[kc]
                    V("pe", lambda h: h.matmul(pbank[ab][:, 0:256], lhsT=vv[q][:, kc, :], rhs=pT[pj][:, 0:256], start=(kc == 0), stop=(kc == nkc - 1)),
                      (vv_b[q], pT_b[pj]), (pb_b[ab],))
                    V("pe", lambda h: h.matmul(pbank[zb][:, 0:256], lhsT=ones_bf[:, :], rhs=pT[pj][:, 0:256], start=(kc == 0), stop=(kc == nkc - 1)),
                      (onesbf_b, pT_b[pj]), (pb_b[zb],))

                pipeline(list(range(nkc)), ph1, ph2)
                V("dve", lambda h, zb=zb: h.reciprocal(out=fb[:, 0:256], in_=pbank[zb][:, 0:256]), (pb_b[zb],), (fb_b,))
                V("dve", lambda h, ab=ab, qc=qc: h.tensor_tensor(out=stg[so_s][:, qc], in0=pbank[ab][:, 0:256], in1=fb[:, 0:256], op=ALU.mult), (pb_b[ab], fb_b), (stg_b[so_s],))
            store_head("m", hh, hd, so_s)

    if "gla" in phases:
        gq, gk = t2[0], t2[1]
        ggT = cx.sb([16, 512], F32, "ggT")
        grs = cx.sb([128, 2, 512], BF16, "grs")
        gq_b, gk_b, ggT_b, grs_b = t2_b[0], t2_b[1], Buf("ggT"), Buf("grs")
        ktm = cx.sb([128, 128], F32, "ktm")
        gv = cx.sb([128, 256], BF16, "gv")
        la = cx.sb([128, 128], F32, "la")
        ktm_b, gv_b, la_b = Buf("ktm"), Buf("gv"), Buf("la")
        Eq = cx.sb([128, 128], F32, "Eq")
        Ek = cx.sb([128, 128], F32, "Ek")
        Er = cx.sb([128, 128], F32, "Er")
        Eq_b, Ek_b, Er_b = Buf("Eq"), Buf("Ek"), Buf("Er")
        qtl = cx.sb([128, 128], BF16, "qtl")
        kt2 = cx.sb([128, 2, 128], BF16, "kt2")
        khat = cx.sb([128, 128], BF16, "khat")
        attm = [cx.sb([128, 128], BF16, f"attm{i}") for i in range(2)]
        qtl_b, kt2_b, khat_b = Buf("qtl"), Buf("kt2"), Buf("khat")
        attm_b = [Buf(f"attm{i}") for i in range(2)]
        Sst = cx.sb([128, 128], F32, "Sst")
        Sb2 = cx.sb([128, 2, 128], BF16, "Sb2")
        Sst_b, Sb2_b = Buf("Sst"), Buf("Sb2")
        og_b = [t1_b[0], t1_b[1]]
        for hh in HH:
            cur["hh"] = hh
            sa = load_piece(1152, 400)
            sb_ = load_piece(1552, 384)
            V("pool", lambda h: h.memset(kt2[:, :, :], 0.0), (), (kt2_b,))
            V("pool", lambda h: h.memset(Sst[:, :], 0.0), (), (Sst_b,))
            V("pool", lambda h: h.memset(Sb2[:, :, :], 0.0), (), (Sb2_b,))
            so0 = so[0] % 2
            so1 = (so[0] + 1) % 2
            so[0] += 2
            sos = (so0, so1)
            for tg in range(4):
                bk = gbank()
                proj_fm(bk, sa, 0, 128, tg)
                V("act", lambda h, bk=bk: h.copy(out=gq[:, :], in_=pbank[bk][:, :]), (pb_b[bk],), (gq_b,))
                bk = gbank()
                proj_fm(bk, sa, 128, 128, tg)
                V("dve", lambda h, bk=bk: h.tensor_copy(out=gk[:, :], in_=pbank[bk][:, :]), (pb_b[bk],), (gk_b,))
                bk = gbank()
                proj_fm(bk, sa, 256, 16, tg)
                V("act", lambda h, bk=bk: h.copy(out=ggT[:, :], in_=pbank[bk][0:16, :]), (pb_b[bk],), (ggT_b,))
                for hd in range(2):
                    bk = gbank()
                    proj_fm(bk, sb_, 128 + hd * 128, 128, tg)
                    V("act", lambda h, bk=bk, hd=hd: h.activation(out=grs[:, hd, :], in_=pbank[bk][:, :], func=AF.Silu), (pb_b[bk],), (grs_b,))
                for ci in range(4):
                    tt = tg * 4 + ci
                    cc = slice(ci * 128, (ci + 1) * 128)
                    bk = gbank()
                    proj_tm(bk, sa, 128, 128, tt)
                    V("act", lambda h, bk=bk: h.copy(out=ktm[:, :], in_=pbank[bk][:, 0:128]), (pb_b[bk],), (ktm_b,))
                    bk = gbank()
                    proj_tm(bk, sa, 272, 128, tt, 0)
                    proj_tm(bk, sb_, 0, 128, tt, 128)
                    V("dve", lambda h, bk=bk: h.tensor_copy(out=gv[:, :], in_=pbank[bk][:, 0:256]), (pb_b[bk],), (gv_b,))
                    bk = gbank()
                    V("pe", lambda h, bk=bk, cc=cc: h.matmul(pbank[bk][:, 0:128], lhsT=ggT[0:16, cc], rhs=ggut_l[hh][0:16, :], start=True, stop=True), (ggT_b, ggu_bl[hh]), (pb_b[bk],))
                    V("dve", lambda h, bk=bk: h.tensor_tensor(out=la[:, :], in0=pbank[bk][:, 0:128], in1=ggbt_l[hh][:, :], op=ALU.add), (pb_b[bk], ggb_bl[hh]), (la_b,))
                    V("act", lambda h: h.activation(out=la[:, :], in_=la[:, :], func=AF.Sigmoid), (la_b,), (la_b,))
                    V("act", lambda h: h.activation(out=la[:, :], in_=la[:, :], func=AF.Ln), (la_b,), (la_b,))
                    V("dve", lambda h: h.tensor_scalar(out=la[:, :], in0=la[:, :], scalar1=1.0 / 16.0, scalar2=None, op0=ALU.mult), (la_b,), (la_b,))
                    bc = gbank()
                    V("pe", lambda h, bc=bc: h.matmul(pbank[bc][:, 0:128], lhsT=la[:, :], rhs=gcst[:, 0, :], start=True, stop=True), (la_b, gcst_b), (pb_b[bc],))
                    V("act", lambda h, bc=bc: h.activation(out=Eq[:, :], in_=pbank[bc][:, 0:128], func=AF.Exp), (pb_b[bc],), (Eq_b,))
                    V("act", lambda h, bc=bc: h.activation(out=Ek[:, :], in_=pbank[bc][:, 0:128], func=AF.Exp, scale=-1.0), (pb_b[bc],), (Ek_b,))
                    br = gbank()
                    V("pe", lambda h, br=br: h.matmul(pbank[br][:, 0:128], lhsT=gcst[:, 1, :], rhs=la[:, :], start=True, stop=True), (la_b, gcst_b), (pb_b[br],))
                    V("act", lambda h, br=br: h.activation(out=Er[:, :], in_=pbank[br][:, 0:128], func=AF.Exp), (pb_b[br],), (Er_b,))
                    V("dve", lambda h, cc=cc: h.scalar_tensor_tensor(out=qtl[:, :], in0=gq[:, cc], scalar=0.125, in1=Eq[:, :], op0=ALU.mult, op1=ALU.mult), (gq_b, Eq_b), (qtl_b,))
                    for hd in range(2):
                        ps_ = slice(hd * 64, (hd + 1) * 64)
                        V("pool", lambda h, hd=hd, ps_=ps_, cc=cc: h.tensor_tensor(out=kt2[ps_, hd, :], in0=gk[ps_, cc], in1=Ek[ps_, :], op=ALU.mult), (gk_b, Ek_b), (kt2_b,))
                    V("pool", lambda h: h.tensor_tensor(out=khat[:, :], in0=ktm[:, :], in1=Er[:, :], op=ALU.mult), (ktm_b, Er_b), (khat_b,))
                    for hd in range(2):
                        bt = gbank()
                        V("pe", lambda h, bt=bt, hd=hd: h.matmul(pbank[bt][:, 0:128], lhsT=kt2[:, hd, :], rhs=qtl[:, :], start=True, stop=True), (kt2_b, qtl_b), (pb_b[bt],))
                        V("dve", lambda h, bt=bt, hd=hd: h.tensor_tensor(out=attm[hd][:, :], in0=pbank[bt][:, 0:128], in1=tri_bf, op=ALU.mult), (pb_b[bt], cmask_b), (attm_b[hd],))
                        bo = gbank()
                        V("pe", lambda h, bo=bo, hd=hd: h.matmul(pbank[bo][:, 0:128], lhsT=gv[:, hd * 128:(hd + 1) * 128], rhs=attm[hd][:, :], start=True, stop=False), (gv_b, attm_b[hd]), (pb_b[bo],))
                        V("pe", lambda h, bo=bo, hd=hd: h.matmul(pbank[bo][:, 0:128], lhsT=Sb2[:, hd, :], rhs=qtl[:, :], start=False, stop=True), (Sb2_b, qtl_b), (pb_b[bo],))
                        V("act", lambda h, bo=bo, hd=hd, cc=cc: h.copy(out=t1[hd][:, cc], in_=pbank[bo][:, 0:128]), (pb_b[bo],), (og_b[hd],))
                    bkv = gbank()
                    V("pe", lambda h, bkv=bkv: h.matmul(pbank[bkv][:, 0:256], lhsT=khat[:, :], rhs=gv[:, :], start=True, stop=True), (khat_b, gv_b), (pb_b[bkv],))
                    for hd in range(2):
                        ps_ = slice(hd * 64, (hd + 1) * 64)
                        V("dve", lambda h, hd=hd, ps_=ps_, bkv=bkv: h.scalar_tensor_tensor(out=Sst[ps_, :], in0=Sst[ps_, :], scalar=Eq[ps_, 127:128], in1=pbank[bkv][ps_, hd * 128:(hd + 1) * 128],
                                                                                       op0=ALU.mult, op1=ALU.add), (Sst_b, Eq_b, pb_b[bkv]), (Sst_b,))
                        V("act", lambda h, hd=hd, ps_=ps_: h.copy(out=Sb2[ps_, hd, :], in_=Sst[ps_, :]), (Sst_b,), (Sb2_b,))
                for hd in range(2):
                    rms_over_partitions(t1[hd][:, :], og_b[hd], 512)
                    V("dve", lambda h, hd=hd: h.scalar_tensor_tensor(out=fa[:, :], in0=t1[hd][:, :], scalar=gngt[:, 0:1], in1=fb[:, :], op0=ALU.mult, op1=ALU.mult),
                      (og_b[hd], fb_b, gng_b), (fa_b,))
                    V("pool", lambda h, hd=hd, tg=tg: h.tensor_tensor(out=stg[sos[hd]][:, tg * 512:(tg + 1) * 512], in0=fa[:, :], in1=grs[:, hd, :], op=ALU.mult),
                      (fa_b, grs_b), (stg_b[sos[hd]],))
            for hd in range(2):
                store_head("g", hh, hd, sos[hd])


_CACHE = {}

_IN_OFF = {"dq": 0, "dk": 768, "dv": 1536, "gq": 2304, "gk": 2560, "gv": 2816, "gr": 3328, "gg": 3840, "mq": 3856, "mk": 4624, "mv": 5392}


def _wsel_cols(hh):
    o = _IN_OFF
    cols = []
    for h in range(3 * hh, 3 * hh + 3):
        cols += list(range(o["dq"] + h * 128, o["dq"] + (h + 1) * 128))
        cols += list(range(o["dk"] + h * 128, o["dk"] + (h + 1) * 128))
        cols += list(range(o["dv"] + h * 128, o["dv"] + (h + 1) * 128))
    g0 = 2 * hh
    cols += list(range(o["gq"] + g0 * 64, o["gq"] + (g0 + 2) * 64))
    cols += list(range(o["gk"] + g0 * 64, o["gk"] + (g0 + 2) * 64))
    cols += list(range(o["gg"], o["gg"] + 16))
    cols += list(range(o["gv"] + g0 * 128, o["gv"] + (g0 + 1) * 128))
    cols += list(range(o["gv"] + (g0 + 1) * 128, o["gv"] + (g0 + 2) * 128))
    cols += list(range(o["gr"] + g0 * 128, o["gr"] + (g0 + 2) * 128))
    for h in range(3 * hh, 3 * hh + 3):
        cols += list(range(o["mq"] + h * 128, o["mq"] + (h + 1) * 128))
        cols += list(range(o["mk"] + h * 128, o["mk"] + (h + 1) * 128))
        cols += list(range(o["mv"] + h * 128, o["mv"] + (h + 1) * 128))
    assert len(cols) == NCOL
    return np.array(cols)


_CONST_SHAPES = {"ident": [128, 128], "rc": [128, 8], "perm": [2, 128, 128], "cmask": [128, 896], "gcst": [3, 128, 128],
                 "nmask": [1, 64], "pmask": [1, 64], "esel": [8, 8, 128]}


def build_fused():
    cx = Ctx()
    P = cx.P
    x = cx.din("x", [TOK, D])
    pos = cx.din("pos", [1, SEQ], I32)
    cst = {k: cx.din(k, shp) for k, shp in _CONST_SHAPES.items()}
    L = []
    for i in range(DEPTH):
        L.append({
            "p": cx.din(f"p{i}", [TOK, 256]), "wsel2": cx.din(f"wsel{i}", [1, D, NCOL]), "wo": cx.din(f"wo{i}", [D, D]),
            "f1g": cx.din(f"f1g{i}", [D, DFF]), "f1u": cx.din(f"f1u{i}", [D, DFF]), "f1d": cx.din(f"f1d{i}", [DFF, D]),
            "f2g": cx.din(f"f2g{i}", [D, DFF]), "f2u": cx.din(f"f2u{i}", [D, DFF]), "f2d": cx.din(f"f2d{i}", [DFF, D]),
            "wpe": cx.din(f"wpe{i}", [256, D]), "wpg": cx.din(f"wpg{i}", [D, D]),
            "lng": cx.din(f"lng{i}", [4, D]), "lnb": cx.din(f"lnb{i}", [4, D]),
            "dlam": cx.din(f"dlam{i}", [1, 256]), "lamc": cx.din(f"lamc{i}", [1, 2]), "dng": cx.din(f"dng{i}", [128, 1]),
            "ggu2": cx.din(f"ggu{i}", [1, 16, 128]), "ggb2": cx.din(f"ggb{i}", [1, 1, 128]), "gng": cx.din(f"gng{i}", [128, 1]),
        })
    y = cx.dout("y", [TOK, D])
    HB = D // 2
    x1loc = cx.dscratch("x1loc", [TOK, D])
    x1b = cx.dscratch("x1b", [TOK, D], BF16)
    xg = cx.dscratch("xg", [4, TOK, D], BF16)
    xsel = cx.dscratch("xsel", [SEQ, D], BF16)
    oTloc = cx.dscratch("oTloc", [2, HB, TOK], BF16)
    og = cx.dscratch("og", [4 * 4 * (HB // 2), TOK], BF16)
    oTsel = cx.dscratch("oTsel", [D, TOK], BF16)
    ident = cst["ident"]
    RG = [[0, 1, 2, 3], [4, 5, 6, 7]]
    dyn = {}

    def beta(h):
        return (h.partition_id() // 2) % 2

    def half(h):
        return h.partition_id() % 2

    def dval(h, name, fn, hi):
        if (name, id(h)) not in dyn:
            dyn[(name, id(h))] = h.snap(fn(h), min_val=0, max_val=hi)
        return dyn[(name, id(h))]

    def cc(src, dst):
        P.add("pool", lambda h: h.collective_compute("AllGather", ALU.bypass, replica_groups=RG, ins=[src], outs=[dst]),
              dma=True, key="cc", cc=True)

    def exchange_x():
        toks = []
        for k in range(4):
            gb = Buf(f"xg{k}")
            P.add("pool", lambda h, k=k: h.collective_compute("AllGather", ALU.bypass, replica_groups=RG, ins=[x1b[k * 256:(k + 1) * 256, :]], outs=[xg[k]]),
                  writes=(gb,), dma=True, key="cc", cc=True)
            pb = Buf(f"xsel{k}")
            P.add("sp", lambda h, k=k: [h.dma_start(out=xsel[a * TOK + k * 256:a * TOK + (k + 1) * 256, :],
                                                    in_=xg[k][bass.ds(dval(h, "xrow", lambda e: beta(e) * 512, 512), 512), :][a * 256:(a + 1) * 256, :])
                                        for a in range(2)], reads=(gb,), writes=(pb,), dma=True, key="pickx", ndma=2)
            toks.append(pb)
        return toks

    def exchange_o():
        toks = []
        for t in range(2):
            for f2 in range(2):
                k = t * 2 + f2
                gb = Buf(f"og{k}")
                toks.append(gb)
                P.add("pool", lambda h, t=t, f2=f2, k=k: h.collective_compute("AllGather", ALU.bypass, replica_groups=RG,
                                                                               ins=[oTloc[t, f2 * 512:(f2 + 1) * 512, :]], outs=[og[k * 2048:(k + 1) * 2048, :]]),
                      writes=(gb,), dma=True, key="cc", cc=True)
        out = []
        for hh in range(2):
            for f2 in range(2):
                pb = Buf(f"osel{hh}{f2}")
                P.add("act", lambda h, hh=hh, f2=f2: h.dma_start(
                    out=oTsel[hh * HB + f2 * 512:hh * HB + (f2 + 1) * 512, :],
                    in_=og[bass.ds(dval(h, "orow", lambda e: half(e) * 4096 + beta(e) * 1024, 5120), 3072), :][f2 * 2048 + hh * 512:f2 * 2048 + (hh + 1) * 512, :]),
                    reads=tuple(toks), writes=(pb,), dma=True, key="picko")
                out.append(pb)
        return out

    cx.begin_stage()
    emit_A(cx, x, x1loc, L[0]["f1g"], L[0]["f1u"], L[0]["f1d"], L[0]["lng"], L[0]["lnb"], ident, yb=x1b)
    cx.end_stage()
    for i in range(DEPTH):
        w = L[i]
        cx.begin_stage()
        A = {"x": xsel, "pos": pos, "out": oTloc, "nhh": 1, "out_split": True, "x_cast": True, "out_cast": True, "pre": exchange_x,
             "row_of": lambda kind, hh, j: {"d": j * 128, "g": 384 + j * 128, "m": 640 + j * 128}[kind]}
        A.update({k: w[k] for k in ("wsel2", "dlam", "lamc", "dng", "ggu2", "ggb2", "gng")})
        A.update(cst)
        emit_mix(cx, A)
        cx.end_stage()
        cx.begin_stage()
        if i + 1 < DEPTH:
            n = L[i + 1]
            nxt = (n["f1g"], n["f1u"], n["f1d"], n["lng"], n["lnb"])
            emit_C(cx, x1loc, oTsel, w["p"], w["wo"], w["f2g"], w["f2u"], w["f2d"], w["wpe"], w["wpg"], w["lng"], w["lnb"], ident, x1loc, nxt, yb=x1b, pre=exchange_o)
        else:
            emit_C(cx, x1loc, oTsel, w["p"], w["wo"], w["f2g"], w["f2u"], w["f2d"], w["wpe"], w["wpg"], w["lng"], w["lnb"], ident, y, None, pre=exchange_o)
        cx.end_stage()
    return cx.finish()


def _wo_rows():
    rows = []
    for hh in range(2):
        for j in range(3):
            rows += list(range((3 * hh + j) * 128, (3 * hh + j + 1) * 128))
        for j in range(2):
            rows += list(range(768 + (2 * hh + j) * 128, 768 + (2 * hh + j + 1) * 128))
        for j in range(3):
            rows += list(range(1280 + (3 * hh + j) * 128, 1280 + (3 * hh + j + 1) * 128))
    return np.array(rows)


def kernel(**inputs):
    f = lambda k: np.ascontiguousarray(np.asarray(inputs[k]), dtype=np.float32)
    x = f("x")
    p = f("p")
    positions = np.ascontiguousarray(np.asarray(inputs["positions"]).astype(np.int32))
    w_in, w_out = f("w_in"), f("w_out")
    dlam, dng = f("diff_lambda"), f("diff_norm_g")
    ggu, ggb, gng = f("gla_gate_up"), f("gla_gate_b"), f("gla_norm_g")
    f1g, f1u, f1d = f("ffn1_gate"), f("ffn1_up"), f("ffn1_down")
    f2g, f2u, f2d = f("ffn2_gate"), f("ffn2_up"), f("ffn2_down")
    wpe, wpg = f("w_pe"), f("w_pg")
    lng, lnb = f("ln_g"), f("ln_b")
    shared = dict(mix_consts())
    per_hh = [dict(), dict()]
    cols = [_wsel_cols(hh) for hh in range(2)]
    wor = _wo_rows()
    for i in range(DEPTH):
        lam_init = 0.8 - 0.6 * math.exp(-0.3 * i)
        shared.update({
            f"wo{i}": np.ascontiguousarray(w_out[i][wor, :]), f"f1g{i}": f1g[i], f"f1u{i}": f1u[i], f"f1d{i}": f1d[i],
            f"f2g{i}": f2g[i], f"f2u{i}": f2u[i], f"f2d{i}": f2d[i], f"wpe{i}": wpe[i], f"wpg{i}": wpg[i],
            f"lng{i}": lng[i], f"lnb{i}": lnb[i], f"dlam{i}": np.ascontiguousarray(dlam[i].reshape(1, 256)),
            f"lamc{i}": np.array([[lam_init, 1.0 - lam_init]], np.float32), f"dng{i}": np.ascontiguousarray(dng[i].reshape(128, 1)),
            f"gng{i}": np.ascontiguousarray(gng[i].reshape(128, 1)),
        })
        for hh in range(2):
            per_hh[hh].update({
                f"wsel{i}": np.ascontiguousarray(w_in[i][:, cols[hh]])[None],
                f"ggu{i}": np.ascontiguousarray(ggu[i][:, hh * 128:(hh + 1) * 128])[None],
                f"ggb{i}": np.ascontiguousarray(ggb[i][hh * 128:(hh + 1) * 128].reshape(1, 1, 128)),
            })
    in_maps = []
    for c in range(NCORES):
        b, h = c // 2, c % 2
        m = dict(shared)
        m.update(per_hh[h])
        m["x"] = np.ascontiguousarray(x[b, h * TOK:(h + 1) * TOK])
        m["pos"] = np.ascontiguousarray(positions[b].reshape(1, SEQ))
        for i in range(DEPTH):
            m[f"p{i}"] = np.ascontiguousarray(p[i, b, h * TOK:(h + 1) * TOK])
        in_maps.append(m)
    if "fused" not in _CACHE:
        _CACHE["fused"] = build_fused()
    res = run_bass_kernel_spmd(_CACHE["fused"], in_maps, core_ids=list(range(NCORES))).results
    out = np.concatenate([res[c]["y"] for c in range(NCORES)], axis=0)
    return out.reshape(NB, SEQ, D).astype(np.float32)
```

```python
import math
import types
from contextlib import ExitStack
import numpy as np
import concourse.bass as bass
import concourse.mybir as mybir
from concourse.bass_utils import run_bass_kernel_spmd

F32 = mybir.dt.float32
BF16 = mybir.dt.bfloat16
I32 = mybir.dt.int32
AF = mybir.ActivationFunctionType
ALU = mybir.AluOpType
AX = mybir.AxisListType

D = 2048
DFF = 5632
SEQ = 2048
NB = 4
DEPTH = 2
NCORES = 8
TOK = 1024
ALPHA = (2 * DEPTH) ** 0.25
EPS = 1e-5
DIN = 6160


class Buf:
    __slots__ = ("name", "lw", "rd", "excl")

    def __init__(self, name, excl=False):
        self.name = name
        self.lw = None
        self.rd = {}
        self.excl = excl


def _freeze(fn):
    if getattr(fn, "__closure__", None) is None:
        return fn
    cells = []
    for c in fn.__closure__:
        try:
            cells.append(types.CellType(c.cell_contents))
        except ValueError:
            cells.append(c)
    return types.FunctionType(fn.__code__, fn.__globals__, fn.__name__, fn.__defaults__, tuple(cells))


def _flat(x):
    for b in x:
        if isinstance(b, (tuple, list)):
            yield from _flat(b)
        else:
            yield b


class Op:
    __slots__ = ("eng", "fn", "deps", "sig", "cnt", "dma", "key", "ndma", "chan", "cc")


class Prog:
    ENGS = ("pe", "dve", "act", "pool", "sp")

    def __init__(self):
        self.ops = []
        self.bar = []

    def barrier(self):
        last = {}
        for op in self.ops:
            last[op.chan] = op
        self.bar = list(last.values())

    def add(self, eng, fn, reads=(), writes=(), dma=False, key=None, ndma=1, cc=False):
        op = Op()
        op.cc = cc
        op.eng = eng
        op.fn = _freeze(fn)
        op.sig = False
        op.cnt = 0
        op.dma = dma
        op.key = key
        op.ndma = ndma
        op.chan = ("dma", key) if dma else eng
        deps = {}

        def need(d):
            if d is None or d is op:
                return
            c = d.chan
            if c not in deps or deps[c][0] < d.cnt:
                deps[c] = (d.cnt, d)

        op.cnt = len(self.ops)
        reads = tuple(_flat(reads))
        writes = tuple(_flat(writes))
        for d in self.bar:
            need(d)
        writes = tuple(writes) + tuple(b for b in reads if b.excl)
        for b in reads:
            need(b.lw)
        for b in writes:
            need(b.lw)
            for r in b.rd.values():
                need(r)
        for b in reads:
            b.rd[op.chan] = op
        for b in writes:
            b.lw = op
            b.rd = {}
        op.deps = [d for (_, d) in deps.values()]
        if not dma and eng == "pe":
            op.deps = [d for d in op.deps if d.chan != "pe"]
        for d in op.deps:
            d.sig = True
        self.ops.append(op)
        return op

    def emit(self, nc, stack):
        cnt = {}
        for op in self.ops:
            if op.dma:
                cnt[op.chan] = cnt.get(op.chan, 0) + (1 if op.cc else 16 * op.ndma)
                op.cnt = cnt[op.chan]
            elif op.sig:
                cnt[op.chan] = cnt.get(op.chan, 0) + 1
                op.cnt = cnt[op.chan]
            else:
                op.cnt = None
        sems = {}
        for c in cnt:
            nm = ("s_" + (c if isinstance(c, str) else "d_" + str(c[1]))).replace(":", "_")
            sems[c] = stack.enter_context(nc.semaphore(nm))
        self.nsems = len(sems)
        self.final_cnt = dict(cnt)
        block = stack.enter_context(nc.Block())
        streams = {e: [op for op in self.ops if op.eng == e] for e in self.ENGS}

        def run(e, h):
            waited = {}
            for op in streams[e]:
                for d in op.deps:
                    if waited.get(d.chan, 0) < d.cnt:
                        h.wait_ge(sems[d.chan], d.cnt)
                        waited[d.chan] = d.cnt
                ins = op.fn(h)
                if op.cc:
                    ins.then_inc(sems[op.chan])
                elif op.dma:
                    if not isinstance(ins, (list, tuple)):
                        ins = [ins]
                    assert len(ins) == op.ndma
                    for i in ins:
                        i.then_inc(sems[op.chan], 16)
                elif op.sig:
                    ins.then_inc(sems[op.chan], 1)
            if e == "sp":
                for c, v in cnt.items():
                    if waited.get(c, 0) < v:
                        h.wait_ge(sems[c], v)

        @block.tensor
        def _(h):
            run("pe", h)

        @block.vector
        def _(h):
            run("dve", h)

        @block.scalar
        def _(h):
            run("act", h)

        @block.gpsimd
        def _(h):
            run("pool", h)

        @block.sync
        def _(h):
            run("sp", h)


class Ctx:
    def __init__(self):
        self.nc = bass.Bass("TRN2", target_bir_lowering=False)
        self.P = Prog()
        self.stack = ExitStack()
        self.stage = None
        self.n = 0
        self.nstage = 0

    def begin_stage(self):
        self.stage = ExitStack()
        self.nstage += 1

    def end_stage(self):
        self.stage.close()
        self.stage = None
        self.P.barrier()

    def sb(self, shape, dt, name=None):
        self.n += 1
        st = self.stage if self.stage is not None else self.stack
        return st.enter_context(self.nc.sbuf_tensor(f"sb{self.nstage}_" + (name or f"t{self.n}"), list(shape), dt))

    def ps(self, shape, dt=F32, name=None):
        self.n += 1
        st = self.stage if self.stage is not None else self.stack
        return st.enter_context(self.nc.psum_tensor(f"ps{self.nstage}_" + (name or f"p{self.n}"), list(shape), dt))

    def din(self, name, shape, dt=F32):
        return self.nc.dram_tensor(name, list(shape), dt, kind="ExternalInput").ap()

    def dout(self, name, shape, dt=F32):
        return self.nc.dram_tensor(name, list(shape), dt, kind="ExternalOutput").ap()

    def dscratch(self, name, shape, dt=F32):
        return self.nc.dram_tensor(name, list(shape), dt).ap()

    def finish(self):
        if self.stage is not None:
            self.end_stage()
        self.P.emit(self.nc, self.stack)
        self.stack.close()
        self.nc._prog = self.P
        return self.nc


def pipeline(units, ph1, ph2, lag=2, before_ph2=None):
    n = len(units)
    for i in range(n + lag):
        if i < n:
            ph1(units[i])
        if i >= lag:
            if i == lag and before_ph2 is not None:
                before_ph2()
            ph2(units[i - lag])
    if n == 0 and before_ph2 is not None:
        before_ph2()


class RowStage:
    def __init__(self, cx, ntok):
        self.cx = cx
        self.ntok = ntok
        self.NT = ntok // 128
        self.xacc = cx.sb([128, self.NT, D], F32, "xacc")
        self.xacc_cb = [[Buf(f"xacc{t}_{c}") for c in range(4)] for t in range(self.NT)]
        self.xacc_b = [tuple(self.xacc_cb[t]) for t in range(self.NT)]
        self.acc_tmp = [cx.sb([128, 512], F32, f"acct{k}") for k in range(2)]
        self.acc_tmp_b = [Buf(f"acct{k}") for k in range(2)]
        self.xT = cx.sb([128, 16, ntok], BF16, "xT")
        self.xT_b = [Buf(f"xT{t}") for t in range(self.NT)]
        self.ident = cx.sb([128, 128], F32, "ident")
        self.ident_b = Buf("ident")
        self.pbank = [cx.ps([128, 512], F32, f"bank{i}") for i in range(8)]
        self.pbank_b = [Buf(f"bank{i}", excl=True) for i in range(8)]
        self.rr = 0
        self.wbuf = [cx.sb([128, 8192], BF16, f"wbuf{s}") for s in range(2)]
        self.wbuf_b = [Buf(f"wbuf{s}") for s in range(2)]
        self.wdb = [cx.sb([128, 2, D], BF16, f"wdb{s}") for s in range(2)]
        self.wdb_b = [Buf(f"wdb{s}") for s in range(2)]
        self.actT = [cx.sb([128, 2, ntok], BF16, f"actT{s}") for s in range(2)]
        self.actT_b = [Buf(f"actT{s}") for s in range(2)]
        self.sg = [cx.sb([128, 512], F32, f"sg{s}") for s in range(2)]
        self.sg_b = [Buf(f"sg{s}") for s in range(2)]
        self.gbc = cx.sb([128, 2, D], F32, "gbc")
        self.gb_b = Buf("gbc")
        self.ln_stats = [(cx.sb([128, 4, 6], F32, f"stats{k}"), Buf(f"stats{k}")) for k in range(2)]
        self.ln_mv = cx.sb([128, self.NT, 2], F32, "ln_mv")
        self.ln_sc = cx.sb([128, self.NT], F32, "ln_sc")
        self.ln_nb = cx.sb([128, self.NT], F32, "ln_nb")
        self.ln_b = Buf("ln")
        self.xin = [cx.sb([128, D], F32, f"xin{s}") for s in range(2)]
        self.xin_b = [Buf(f"xin{s}") for s in range(2)]
        self.wslot = 0

    def load_ln(self, lng, lnb, idx):
        gbc = self.gbc
        self.cx.P.add("sp", lambda h: [h.dma_start(out=gbc[:, 0, :], in_=lng[idx:idx + 1, :].partition_broadcast(128)),
                                       h.dma_start(out=gbc[:, 1, :], in_=lnb[idx:idx + 1, :].partition_broadcast(128))],
                      writes=(self.gb_b,), dma=True, key="gbc", ndma=2)

    def ln_all(self, zscale, out_fn=None):
        P = self.cx.P
        NT = self.NT
        mv, sc, nb, lb = self.ln_mv, self.ln_sc, self.ln_nb, self.ln_b
        for tt in range(NT):
            stats, sb_ = self.ln_stats[tt % 2]
            P.add("dve", lambda h, tt=tt, stats=stats: [h.bn_stats(out=stats[:, c, :], in_=self.xacc[:, tt, c * 512:(c + 1) * 512]) for c in range(4)][-1],
                  reads=(self.xacc_b[tt],), writes=(sb_,))
            P.add("dve", lambda h, tt=tt, stats=stats: h.bn_aggr(out=mv[:, tt, :], in_=stats[:, :, :]), reads=(sb_,), writes=(lb,))
        P.add("dve", lambda h: h.tensor_scalar_add(out=sc[:, :], in0=mv[:, :, 1], scalar1=EPS / (zscale * zscale)), reads=(lb,), writes=(lb,))
        P.add("act", lambda h: h.sqrt(out=sc[:, :], in_=sc[:, :]), reads=(lb,), writes=(lb,))
        P.add("dve", lambda h: h.reciprocal(out=sc[:, :], in_=sc[:, :]), reads=(lb,), writes=(lb,))
        P.add("dve", lambda h: h.scalar_tensor_tensor(out=nb[:, :], in0=mv[:, :, 0], scalar=-1.0, in1=sc[:, :], op0=ALU.mult, op1=ALU.mult),
              reads=(lb,), writes=(lb,))
        for tt in range(NT):
            if out_fn is None:
                o, ob = self.xacc[:, tt, :], self.xacc_b[tt]
            else:
                o, ob = out_fn(tt)
            xa = self.xacc[:, tt, :]
            P.add("act", lambda h, xa=xa, tt=tt: h.activation(out=xa, in_=xa, func=AF.Identity, bias=nb[:, tt:tt + 1], scale=sc[:, tt:tt + 1]),
                  reads=(lb, self.xacc_b[tt]), writes=(self.xacc_b[tt],))
            P.add("pool", lambda h, xa=xa: h.tensor_tensor(out=xa, in0=xa, in1=self.gbc[:, 0, :], op=ALU.mult), reads=(self.xacc_b[tt], self.gb_b), writes=(self.xacc_b[tt],))
            P.add("dve", lambda h, xa=xa, o=o: h.tensor_tensor(out=o, in0=xa, in1=self.gbc[:, 1, :], op=ALU.add), reads=(self.xacc_b[tt], self.gb_b), writes=(ob,))
            if out_fn is not None:
                out_fn(tt, done=True)


def load_const(cx, dst, dst_b, src_ap, key):
    cx.P.add("sp", lambda h: h.dma_start(out=dst, in_=src_ap), reads=(), writes=(dst_b,), dma=True, key=key)


def build_xT(cx, st, tt, src, src_b, banks=(4, 5, 6, 7)):
    P = cx.P
    for g in range(4):
        bk = banks[(st.rr) % len(banks)]
        st.rr += 1
        pb = st.pbank[bk]
        for j in range(4):
            kc = g * 4 + j
            P.add("pe", lambda h, pb=pb, j=j, kc=kc: h.transpose(out=pb[:, j * 128:(j + 1) * 128], in_=src[:, kc * 128:(kc + 1) * 128], identity=st.ident[:, :]),
                  reads=(src_b, st.ident_b), writes=(st.pbank_b[bk],))
        eng = "act" if g % 2 == 0 else "dve"
        o = st.xT[:, g * 4:(g + 1) * 4, tt * 128:(tt + 1) * 128]
        i = pb[:, :].rearrange("p (j t) -> p j t", j=4)
        if eng == "act":
            P.add("act", lambda h, o=o, i=i: h.copy(out=o, in_=i), reads=(st.pbank_b[bk],), writes=(st.xT_b[tt],))
        else:
            P.add("dve", lambda h, o=o, i=i: h.tensor_copy(out=o, in_=i), reads=(st.pbank_b[bk],), writes=(st.xT_b[tt],))


def ln_stats(cx, st, tt, zscale, tmp):
    P = cx.P
    stats, mv, sc, nb, stat_b = tmp
    P.add("dve", lambda h: [h.bn_stats(out=stats[:, c, :], in_=st.xacc[:, tt, c * 512:(c + 1) * 512]) for c in range(4)][-1],
          reads=(st.xacc_b[tt],), writes=(stat_b,))
    P.add("dve", lambda h: h.bn_aggr(out=mv[:, :], in_=stats[:, :, :]), reads=(stat_b,), writes=(stat_b,))
    P.add("dve", lambda h: h.tensor_scalar_add(out=sc[:, :], in0=mv[:, 1:2], scalar1=EPS / (zscale * zscale)),
          reads=(stat_b,), writes=(stat_b,))
    P.add("act", lambda h: h.sqrt(out=sc[:, :], in_=sc[:, :]), reads=(stat_b,), writes=(stat_b,))
    P.add("dve", lambda h: h.reciprocal(out=sc[:, :], in_=sc[:, :]), reads=(stat_b,), writes=(stat_b,))
    P.add("dve", lambda h: h.scalar_tensor_tensor(out=nb[:, :], in0=mv[:, 0:1], scalar=-1.0, in1=sc[:, :], op0=ALU.mult, op1=ALU.mult),
          reads=(stat_b,), writes=(stat_b,))


def ln_apply(cx, st, tt, g_bc, b_bc, gb_b, out_ap, out_b, tmp):
    P = cx.P
    xa = st.xacc[:, tt, :]
    stats, mv, sc, nb, stat_b = tmp
    P.add("act", lambda h: h.activation(out=xa, in_=xa, func=AF.Identity, bias=nb[:, 0:1], scale=sc[:, 0:1]),
          reads=(stat_b, st.xacc_b[tt]), writes=(st.xacc_b[tt],))
    P.add("pool", lambda h: h.tensor_tensor(out=xa, in0=xa, in1=g_bc, op=ALU.mult), reads=(st.xacc_b[tt], gb_b), writes=(st.xacc_b[tt],))
    P.add("dve", lambda h: h.tensor_tensor(out=out_ap, in0=xa, in1=b_bc, op=ALU.add), reads=(st.xacc_b[tt], gb_b), writes=(out_b,))


def ffn_stage(cx, st, wg, wu, wd, FB=256):
    P = cx.P
    nblk = DFF // FB
    CPB = FB // 128
    wgu = [w[:, :].rearrange("p (w k f) -> p w k f", w=2, k=16) for w in st.wbuf]
    wgu_b = st.wbuf_b
    wdb, wdb_b = st.wdb, st.wdb_b
    actT, actT_b = st.actT, st.actT_b
    sg, sg_b = st.sg, st.sg_b
    wgv = wg.rearrange("(kc p) f -> p kc f", p=128)
    wuv = wu.rearrange("(kc p) f -> p kc f", p=128)
    wdv = wd.rearrange("(c p) n -> p c n", p=128)
    NTG = st.ntok // 512
    allxT = tuple(st.xT_b)
    gi = 0
    di = 0

    def load_gu(b):
        s = b % 2
        P.add("pool", lambda h: [h.dma_start(out=wgu[s][:, 0, :, :], in_=wgv[:, :, b * FB:(b + 1) * FB]),
                                 h.dma_start(out=wgu[s][:, 1, :, :], in_=wuv[:, :, b * FB:(b + 1) * FB])],
              writes=(wgu_b[s],), dma=True, key=f"wbuf{s}", ndma=2)

    def load_d(b):
        s = b % 2
        P.add("pool", lambda h: h.dma_start(out=wdb[s][:, :, :], in_=wdv[:, b * CPB:(b + 1) * CPB, :]),
              writes=(wdb_b[s],), dma=True, key=f"wdb{s}")

    def gateup(b):
        nonlocal gi
        s = b % 2
        for c in range(CPB):
            for tg in range(NTG):
                bg = gi % 2
                bu = 2 + gi % 2
                gi += 1
                for (w, bk) in ((0, bg), (1, bu)):
                    for kc in range(16):
                        P.add("pe", lambda h, w=w, bk=bk, kc=kc, c=c, tg=tg: h.matmul(
                            st.pbank[bk][:, :], lhsT=wgu[s][:, w, kc, c * 128:(c + 1) * 128],
                            rhs=st.xT[:, kc, tg * 512:(tg + 1) * 512], start=(kc == 0), stop=(kc == 15)),
                            reads=(wgu_b[s],) + allxT, writes=(st.pbank_b[bk],))
                sgi = gi % 2
                P.add("act", lambda h, bg=bg, sgi=sgi: h.activation(out=sg[sgi][:, :], in_=st.pbank[bg][:, :], func=AF.Silu),
                      reads=(st.pbank_b[bg],), writes=(sg_b[sgi],))
                P.add("dve", lambda h, bu=bu, sgi=sgi, c=c, tg=tg: h.tensor_tensor(
                    out=actT[s][:, c, tg * 512:(tg + 1) * 512], in0=sg[sgi][:, :], in1=st.pbank[bu][:, :], op=ALU.mult),
                    reads=(sg_b[sgi], st.pbank_b[bu]), writes=(actT_b[s],))

    def down(b):
        nonlocal di
        s = b % 2
        for tt in range(st.NT):
            for cg in range(4):
                bk = 4 + di % 4
                di += 1
                for c in range(CPB):
                    P.add("pe", lambda h, bk=bk, c=c, tt=tt, cg=cg: h.matmul(
                        st.pbank[bk][:, :], lhsT=actT[s][:, c, tt * 128:(tt + 1) * 128],
                        rhs=wdb[s][:, c, cg * 512:(cg + 1) * 512], start=(c == 0), stop=(c == CPB - 1)),
                        reads=(actT_b[s], wdb_b[s]), writes=(st.pbank_b[bk],))
                xa = st.xacc[:, tt, cg * 512:(cg + 1) * 512]
                xb = st.xacc_cb[tt][cg]
                if di % 3 == 0:
                    k = (di // 3) % 2
                    P.add("act", lambda h, bk=bk, k=k: h.copy(out=st.acc_tmp[k][:, :], in_=st.pbank[bk][:, :]),
                          reads=(st.pbank_b[bk],), writes=(st.acc_tmp_b[k],))
                    P.add("pool", lambda h, xa=xa, k=k: h.tensor_tensor(out=xa, in0=xa, in1=st.acc_tmp[k][:, :], op=ALU.add),
                          reads=(st.acc_tmp_b[k], xb), writes=(xb,))
                else:
                    P.add("dve", lambda h, xa=xa, bk=bk: h.tensor_tensor(out=xa, in0=xa, in1=st.pbank[bk][:, :], op=ALU.add),
                          reads=(st.pbank_b[bk], xb), writes=(xb,))

    load_gu(0)
    load_gu(1)
    load_d(0)
    for b in range(nblk):
        gateup(b)
        if b + 2 < nblk:
            load_gu(b + 2)
        if b >= 1:
            down(b - 1)
        if b + 1 < nblk:
            load_d(b + 1)
    down(nblk - 1)


def load_x_tiles(cx, st, x, scale, with_xT=True):
    P = cx.P
    for tt in range(st.NT):
        s = tt % 2
        P.add("sp", lambda h, s=s, tt=tt: h.dma_start(out=st.xin[s][:, :], in_=x[tt * 128:(tt + 1) * 128, :]),
              writes=(st.xin_b[s],), dma=True, key=f"xin{s}")
        if with_xT:
            build_xT(cx, st, tt, st.xin[s], st.xin_b[s])
        P.add("act", lambda h, s=s, tt=tt: h.mul(out=st.xacc[:, tt, :], in_=st.xin[s][:, :], mul=scale),
              reads=(st.xin_b[s],), writes=(st.xacc_b[tt],))


def refresh_xT(cx, st, scale):
    for tt in range(st.NT):
        build_xT(cx, st, tt, st.xacc[:, tt, :], st.xacc_b[tt])
        if scale != 1.0:
            cx.P.add("act", lambda h, tt=tt: h.mul(out=st.xacc[:, tt, :], in_=st.xacc[:, tt, :], mul=scale),
                     reads=(st.xacc_b[tt],), writes=(st.xacc_b[tt],))


def store_out(cx, st, y, yb=None):
    def out_fn(tt, done=False):
        s = tt % 2
        if done:
            cx.P.add("sp", lambda h: h.dma_start(out=y[tt * 128:(tt + 1) * 128, :], in_=st.xin[s][:, :]),
                     reads=(st.xin_b[s],), dma=True, key=f"yo{s}")
            if yb is not None:
                cx.P.add("pool", lambda h: h.dma_start(out=yb[tt * 128:(tt + 1) * 128, :], in_=st.xin[s][:, :]),
                         reads=(st.xin_b[s],), dma=True, key=f"yb{s}")
            return None
        return st.xin[s][:, :], st.xin_b[s]
    return out_fn


def dense_acc(cx, st, w, lhs_fn, nk, lhs_bufs, post):
    P = cx.P
    wv = w.rearrange("(kc p) n -> p kc n", p=128)
    for cg in range(4):
        s = st.wslot % 2
        st.wslot += 1
        wt = st.wbuf[s][:, 0:nk * 512].rearrange("p (k f) -> p k f", k=nk)
        P.add("pool", lambda h, wt=wt, cg=cg: h.dma_start(out=wt, in_=wv[:, :, cg * 512:(cg + 1) * 512]),
              writes=(st.wbuf_b[s],), dma=True, key=f"wbuf{s}")
        for tt in range(st.NT):
            bk = 4 + st.rr % 4
            st.rr += 1
            for kc in range(nk):
                P.add("pe", lambda h, bk=bk, kc=kc, tt=tt, wt=wt: h.matmul(st.pbank[bk][:, :], lhsT=lhs_fn(kc, tt), rhs=wt[:, kc, :],
                                                                        start=(kc == 0), stop=(kc == nk - 1)),
                      reads=(st.wbuf_b[s],) + tuple(lhs_bufs), writes=(st.pbank_b[bk],))
            post(tt, cg, bk)


def emit_A(cx, x, y, wg, wu, wd, lng, lnb, identd, ntok=TOK, yb=None):
    st = RowStage(cx, ntok)
    load_const(cx, st.ident[:, :], st.ident_b, identd[:, :], "ident")
    st.load_ln(lng, lnb, 0)
    load_x_tiles(cx, st, x, 2.0 * ALPHA)
    ffn_stage(cx, st, wg, wu, wd)
    st.ln_all(0.5, store_out(cx, st, y, yb))


def emit_C(cx, x, oT, pp, wo, wg, wu, wd, wpe, wpg, lng, lnb, identd, y, nxt=None, ntok=TOK, yb=None, pre=None):
    P = cx.P
    with_next = nxt is not None
    odeps = tuple(pre()) if pre else ()
    st = RowStage(cx, ntok)
    load_const(cx, st.ident[:, :], st.ident_b, identd[:, :], "ident")
    st.load_ln(lng, lnb, 1)
    load_x_tiles(cx, st, x, ALPHA, with_xT=False)
    if callable(oT):
        for kc in range(16):
            sl = kc % 2
            P.add("sp", lambda h, kc=kc, sl=sl: h.dma_start(out=st.xin[sl][:, 0:ntok], in_=oT(h, kc)), writes=(st.xin_b[sl],), dma=True, key=f"xin{sl}")
            if kc % 2 == 0:
                P.add("act", lambda h, kc=kc, sl=sl: h.copy(out=st.xT[:, kc, :], in_=st.xin[sl][:, 0:ntok]), reads=(st.xin_b[sl],), writes=tuple(st.xT_b))
            else:
                P.add("dve", lambda h, kc=kc, sl=sl: h.tensor_copy(out=st.xT[:, kc, :], in_=st.xin[sl][:, 0:ntok]), reads=(st.xin_b[sl],), writes=tuple(st.xT_b))
    else:
        oTv = oT.rearrange("(kc p) t -> p kc t", p=128)
        P.add("pool", lambda h: h.dma_start(out=st.xT[:, :, :], in_=oTv), reads=odeps, writes=tuple(st.xT_b), dma=True, key="oT")

    def post_add(tt, cg, bk):
        xa = st.xacc[:, tt, cg * 512:(cg + 1) * 512]
        P.add("dve", lambda h: h.tensor_tensor(out=xa, in0=xa, in1=st.pbank[bk][:, :], op=ALU.add),
              reads=(st.pbank_b[bk], st.xacc_cb[tt][cg]), writes=(st.xacc_cb[tt][cg],))

    dense_acc(cx, st, wo, lambda kc, tt: st.xT[:, kc, tt * 128:(tt + 1) * 128], 16, st.xT_b, post_add)
    st.ln_all(1.0)
    st.load_ln(lng, lnb, 2)
    refresh_xT(cx, st, 2.0 * ALPHA)
    ffn_stage(cx, st, wg, wu, wd)
    st.ln_all(0.5)
    st.load_ln(lng, lnb, 3)
    refresh_xT(cx, st, 1.0)
    pT = cx.sb([128, 2, ntok], BF16, "pT")
    pT_b = Buf("pT")
    pin = cx.sb([128, 256], F32, "pin")
    pin_b = Buf("pin")
    for tt in range(st.NT):
        P.add("sp", lambda h, tt=tt: h.dma_start(out=pin[:, :], in_=pp[tt * 128:(tt + 1) * 128, :]), writes=(pin_b,), dma=True, key="pin")
        bk = 4 + st.rr % 4
        st.rr += 1
        for j in range(2):
            P.add("pe", lambda h, j=j, bk=bk: h.transpose(out=st.pbank[bk][:, j * 128:(j + 1) * 128], in_=pin[:, j * 128:(j + 1) * 128], identity=st.ident[:, :]),
                  reads=(pin_b, st.ident_b), writes=(st.pbank_b[bk],))
        P.add("act", lambda h, tt=tt, bk=bk: h.copy(out=pT[:, :, tt * 128:(tt + 1) * 128], in_=st.pbank[bk][:, 0:256].rearrange("p (j t) -> p j t", j=2)),
              reads=(st.pbank_b[bk],), writes=(pT_b,))
    wpet = cx.sb([128, 2, D], BF16, "wpe")
    wpe_b = Buf("wpe")
    P.add("pool", lambda h: h.dma_start(out=wpet[:, :, :], in_=wpe.rearrange("(c p) n -> p c n", p=128)), writes=(wpe_b,), dma=True, key="wpe")
    et, et_b = st.acc_tmp, st.acc_tmp_b
    ei = [0]

    def post_ple(tt, cg, bk):
        be = ei[0] % 2
        s = ei[0] % 2
        ei[0] += 1
        for c in range(2):
            P.add("pe", lambda h, c=c: h.matmul(st.pbank[be][:, :], lhsT=pT[:, c, tt * 128:(tt + 1) * 128], rhs=wpet[:, c, cg * 512:(cg + 1) * 512],
                                               start=(c == 0), stop=(c == 1)),
                  reads=(pT_b, wpe_b), writes=(st.pbank_b[be],))
        P.add("act", lambda h: h.activation(out=st.sg[s][:, :], in_=st.pbank[bk][:, :], func=AF.Sigmoid),
              reads=(st.pbank_b[bk],), writes=(st.sg_b[s],))
        P.add("dve", lambda h: h.tensor_tensor(out=et[s][:, :], in0=st.sg[s][:, :], in1=st.pbank[be][:, :], op=ALU.mult),
              reads=(st.sg_b[s], st.pbank_b[be]), writes=(et_b[s],))
        xa = st.xacc[:, tt, cg * 512:(cg + 1) * 512]
        P.add("dve", lambda h: h.scalar_tensor_tensor(out=xa, in0=xa, scalar=ALPHA, in1=et[s][:, :], op0=ALU.mult, op1=ALU.add),
              reads=(et_b[s], st.xacc_cb[tt][cg]), writes=(st.xacc_cb[tt][cg],))

    dense_acc(cx, st, wpg, lambda kc, tt: st.xT[:, kc, tt * 128:(tt + 1) * 128], 16, st.xT_b, post_ple)
    if not with_next:
        st.ln_all(1.0, store_out(cx, st, y))
        return
    wg2, wu2, wd2, lng2, lnb2 = nxt
    st.ln_all(1.0)
    st.load_ln(lng2, lnb2, 0)
    refresh_xT(cx, st, 2.0 * ALPHA)
    ffn_stage(cx, st, wg2, wu2, wd2)
    st.ln_all(0.5, store_out(cx, st, y, yb))


NCOL = 3088
PW = 400


def mix_consts():
    c = {}
    c["ident"] = np.eye(128, dtype=np.float32)
    rc = np.zeros((128, 8), np.float32)
    p = np.arange(128)
    rc[:, 0] = 10000.0 ** (-(2.0 * (p % 32)) / 64.0)
    rc[:, 1] = 10000.0 ** (-(2.0 * (p % 64)) / 128.0)
    rc[:, 2] = np.where((p % 64) < 32, -2.0, 2.0)
    rc[:, 3] = np.where(p < 64, -2.0, 2.0)
    rc[:, 6] = math.pi / 2.0
    rc[:, 4] = -math.pi
    rc[:, 5] = EPS
    c["rc"] = rc
    perm = np.zeros((2, 128, 128), np.float32)
    for m in range(128):
        perm[0, (m // 64) * 64 + ((m % 64) + 32) % 64, m] = 1.0
        perm[1, (m + 64) % 128, m] = 1.0
    c["perm"] = perm
    k = np.arange(128)[:, None]
    q = np.arange(512)[None, :]
    c["cmask"] = (np.arange(128)[:, None] <= np.arange(896)[None, :] - 384).astype(np.float32)
    tri = (np.arange(128)[:, None] <= np.arange(128)[None, :]).astype(np.float32)
    c["gcst"] = np.stack([tri, 1.0 - tri, np.ones((128, 128), np.float32)])
    own = np.arange(8)[:, None]
    n = np.arange(8)[None, :]
    c["nmask"] = np.where(n < own, 0.0, -1e30).astype(np.float32).reshape(1, 64)
    c["pmask"] = (n < own).astype(np.float32).reshape(1, 64)
    es = np.zeros((8, 8, 128), np.float32)
    for j in range(8):
        es[j, j, :] = 1.0
    c["esel"] = es
    return c


def emit_mix(cx, A):
    P = cx.P
    S = SEQ
    NT = S // 128
    x, pos, wsel2, dlam, lamc, dng, ggu2, ggb2, gng = (A[k] for k in ("x", "pos", "wsel2", "dlam", "lamc", "dng", "ggu2", "ggb2", "gng"))
    d_ident, d_rc, d_perm, d_cmask, d_gcst, d_nmask, d_pmask, d_esel = (A[k] for k in ("ident", "rc", "perm", "cmask", "gcst", "nmask", "pmask", "esel"))
    out = A["out"]
    phases = ("diff", "gla", "moba")
    cur = {"hh": 0}
    xdeps = tuple(A["pre"]()) if A.get("pre") else ()
    HH = tuple(range(A.get("nhh", 2)))
    xsrc = A.get("x_dyn") or (lambda h, tt: x[tt * 128:(tt + 1) * 128, :])
    row_of = A.get("row_of") or (lambda kind, hh, j: {"d": (3 * hh + j) * 128, "g": 768 + (2 * hh + j) * 128, "m": 1280 + (3 * hh + j) * 128}[kind])

    def V(eng, fn, r=(), w=()):
        return P.add(eng, fn, reads=r, writes=w)

    def LD(dst, src, name, cast=False, n=1):
        b = Buf(name)
        P.add("pool" if cast else "sp", lambda h: h.dma_start(out=dst, in_=src), writes=(b,), dma=True, key=name)
        return b

    ident = cx.sb([128, 128], F32, "ident")
    ident_b = LD(ident[:, :], d_ident[:, :], "ident")
    rc = cx.sb([128, 8], F32, "rc")
    rc_b = LD(rc[:, :], d_rc[:, :], "rc")
    perm = cx.sb([128, 2, 128], BF16, "perm")
    perm_b = LD(perm[:, :, :], d_perm.rearrange("a p m -> p a m"), "perm", cast=True)
    cmask = cx.sb([128, 896], BF16, "cmask")
    cmask_b = LD(cmask[:, :], d_cmask[:, :], "cmask", cast=True)
    gcst = cx.sb([128, 3, 128], F32, "gcst")
    gcst_b = LD(gcst[:, :, :], d_gcst.rearrange("a p m -> p a m"), "gcst")
    ones_bf = cx.sb([128, 128], BF16, "ones_bf")
    onesbf_b = LD(ones_bf[:, :], d_gcst[2, :, :], "ones_bf", cast=True)
    tri_bf = cmask[:, 384:512]
    nmask = cx.sb([128, 64], F32, "nmask")
    nmask_b = LD(nmask[:, :], d_nmask[0:1, :].partition_broadcast(128), "nmask")
    pmask = cx.sb([128, 64], F32, "pmask")
    pmask_b = LD(pmask[:, :], d_pmask[0:1, :].partition_broadcast(128), "pmask")
    esel = cx.sb([8, 8, 128], BF16, "esel")
    esel_b = LD(esel[:, :, :], d_esel[:, :, :], "esel", cast=True)
    dl = cx.sb([128, 256], F32, "dl")
    dl_b = LD(dl[:, :], dlam[0:1, :].partition_broadcast(128), "dl")
    lc = cx.sb([128, 2], F32, "lc")
    lc_b = LD(lc[:, :], lamc[0:1, :].partition_broadcast(128), "lc")
    dngt = cx.sb([128, 1], F32, "dngt")
    dng_b = LD(dngt[:, :], dng[:, :], "dng")
    gngt = cx.sb([128, 1], F32, "gngt")
    gng_b = LD(gngt[:, :], gng[:, :], "gng")
    ggut_l = [cx.sb([16, 128], F32, f"ggut{k}") for k in HH]
    ggu_bl = [LD(ggut_l[k][:, :], ggu2[k, :, :], f"ggu{k}") for k in HH]
    ggbt_l = [cx.sb([128, 128], F32, f"ggbt{k}") for k in HH]
    ggb_bl = [LD(ggbt_l[k][:, :], ggb2[k, 0:1, :].partition_broadcast(128), f"ggb{k}") for k in HH]

    pbank = [cx.ps([128, 512], F32, f"bank{i}") for i in range(8)]
    pb_b = [Buf(f"bank{i}", excl=True) for i in range(8)]
    rr = {"s": 0, "a": 0, "g": 0}

    def sbank():
        rr["s"] += 1
        return rr["s"] % 2

    def abank():
        rr["a"] += 1
        return 2 + rr["a"] % 2

    def gbank():
        rr["g"] += 1
        return rr["g"] % 8

    xT = cx.sb([128, 16, S], BF16, "xT")
    xT_b = [Buf(f"xT{t}") for t in range(NT)]
    stg = [cx.sb([128, S], F32, f"stg{s}") for s in range(2)]
    stg_b = [Buf(f"stg{s}") for s in range(2)]
    k4 = 0
    for tt in range(NT):
        s = tt % 2
        P.add("pool" if A.get("x_cast") else "sp", lambda h, s=s, tt=tt: h.dma_start(out=stg[s][:, :], in_=xsrc(h, tt)),
              reads=xdeps, writes=(stg_b[s],), dma=True, key=f"stg{s}")
        for g in range(4):
            bk = k4 % 4
            k4 += 1
            for j in range(4):
                kc = g * 4 + j
                V("pe", lambda h, bk=bk, j=j, kc=kc, s=s: h.transpose(out=pbank[bk][:, j * 128:(j + 1) * 128], in_=stg[s][:, kc * 128:(kc + 1) * 128], identity=ident[:, :]),
                  (stg_b[s], ident_b), (pb_b[bk],))
            o = xT[:, g * 4:(g + 1) * 4, tt * 128:(tt + 1) * 128]
            i = pbank[bk][:, :].rearrange("p (j t) -> p j t", j=4)
            if g % 2 == 0:
                V("act", lambda h, o=o, i=i: h.copy(out=o, in_=i), (pb_b[bk],), (xT_b[tt],))
            else:
                V("dve", lambda h, o=o, i=i: h.tensor_copy(out=o, in_=i), (pb_b[bk],), (xT_b[tt],))
    allxT = tuple(xT_b)

    wp = [cx.sb([128, 16, PW], BF16, f"wp{s}") for s in range(2)]
    wp_b = [Buf(f"wp{s}") for s in range(2)]
    wvs = [wsel2[k].rearrange("(kc p) f -> p kc f", p=128) for k in HH]
    wi = [0]

    def load_piece(c0, n):
        s = wi[0] % 2
        wi[0] += 1
        wv = wvs[cur["hh"]]
        P.add("pool", lambda h: h.dma_start(out=wp[s][:, :, 0:n], in_=wv[:, :, c0:c0 + n]), writes=(wp_b[s],), dma=True, key=f"wp{s}")
        return s

    def proj_fm(bk, s, c0, m, tg):
        for kc in range(16):
            V("pe", lambda h, kc=kc: h.matmul(pbank[bk][0:m, :], lhsT=wp[s][:, kc, c0:c0 + m], rhs=xT[:, kc, tg * 512:(tg + 1) * 512],
                                             start=(kc == 0), stop=(kc == 15)), (wp_b[s],) + allxT, (pb_b[bk],))

    def proj_tm(bk, s, c0, n, tt, o0=0):
        for kc in range(16):
            V("pe", lambda h, kc=kc: h.matmul(pbank[bk][:, o0:o0 + n], lhsT=xT[:, kc, tt * 128:(tt + 1) * 128], rhs=wp[s][:, kc, c0:c0 + n],
                                             start=(kc == 0), stop=(kc == 15)), (wp_b[s],) + allxT, (pb_b[bk],))

    posf = cx.sb([128, S], F32, "posf")
    posf_b = Buf("posf")
    posi = cx.sb([128, 512], I32, "posi")
    posi_b = Buf("posi")
    for c4 in range(4):
        P.add("sp", lambda h, c4=c4: h.dma_start(out=posi[:, :], in_=pos[0:1, c4 * 512:(c4 + 1) * 512].partition_broadcast(128)), writes=(posi_b,), dma=True, key="posi")
        V("dve", lambda h, c4=c4: h.tensor_copy(out=posf[:, c4 * 512:(c4 + 1) * 512], in_=posi[:, :]), (posi_b,), (posf_b,))
    ctab = cx.sb([128, S], F32, "ctab")
    stab = cx.sb([128, S], F32, "stab")
    tab_b = Buf("tab")

    def make_tables(kind):
        tb = (tab_b, stg_b[0], stg_b[1], posi_b)
        V("dve", lambda h: h.tensor_scalar(out=stg[0][:, :], in0=posf[:, :], scalar1=rc[:, kind:kind + 1], scalar2=None, op0=ALU.mult), (posf_b, rc_b), (stg_b[0],))
        for c4 in range(4):
            cs = slice(c4 * 512, (c4 + 1) * 512)
            V("dve", lambda h, cs=cs: h.tensor_scalar(out=posi[:, :], in0=stg[0][:, cs], scalar1=1.0 / (2.0 * math.pi), scalar2=None, op0=ALU.mult), (stg_b[0],), (posi_b,))
            V("dve", lambda h, cs=cs: h.tensor_copy(out=stg[1][:, cs], in_=posi[:, :]), (posi_b, stg_b[1]), (stg_b[1],))
        V("dve", lambda h: h.scalar_tensor_tensor(out=stg[0][:, :], in0=stg[1][:, :], scalar=-2.0 * math.pi, in1=stg[0][:, :], op0=ALU.mult, op1=ALU.add), (stg_b[0], stg_b[1]), (stg_b[0],))
        V("act", lambda h: h.activation(out=stg[1][:, :], in_=stg[0][:, :], func=AF.Sin, scale=0.5), (stg_b[0],), (stg_b[1],))
        V("act", lambda h: h.activation(out=ctab[:, :], in_=stg[0][:, :], func=AF.Sin, bias=rc[:, 6:7], scale=-0.5), (stg_b[0], rc_b), (tab_b,))
        V("dve", lambda h: h.scalar_tensor_tensor(out=stab[:, :], in0=stg[1][:, :], scalar=rc[:, 2 + kind:3 + kind], in1=ctab[:, :], op0=ALU.mult, op1=ALU.mult), (stg_b[1], rc_b, tab_b), (tab_b,))
        V("dve", lambda h: h.tensor_tensor(out=ctab[:, :], in0=stg[1][:, :], in1=stg[1][:, :], op=ALU.mult), (stg_b[1], tab_b), (tab_b,))
        V("dve", lambda h: h.tensor_scalar(out=ctab[:, :], in0=ctab[:, :], scalar1=-2.0, scalar2=1.0, op0=ALU.mult, op1=ALU.add), (tab_b,), (tab_b,))

    qT2 = [cx.sb([128, 2, S], BF16, f"qT{s}") for s in range(2)]
    qT = [t[:, 0, :] for t in qT2]
    kT = [cx.sb([128, S], BF16, f"kT{s}") for s in range(2)]
    vv = [cx.sb([128, NT, 128], BF16, f"vv{s}") for s in range(2)]
    qT_b = [Buf(f"qT{s}") for s in range(2)]
    kT_b = [Buf(f"kT{s}") for s in range(2)]
    vv_b = [Buf(f"vv{s}") for s in range(2)]
    qb = [cx.sb([128, 512], BF16, f"qb{s}") for s in range(2)]
    qb_b = [Buf(f"qb{s}") for s in range(2)]
    t1 = [cx.sb([128, 512], F32, f"t1{s}") for s in range(2)]
    t1_b = [Buf(f"t1{s}") for s in range(2)]
    t2 = [cx.sb([128, 512], F32, f"t2{s}") for s in range(2)]
    t2_b = [Buf(f"t2{s}") for s in range(2)]
    NPT = 4
    pT = [cx.sb([128, 512], BF16, f"pT{s}") for s in range(NPT)]
    pT_b = [Buf(f"pT{s}") for s in range(NPT)]
    fa = cx.sb([128, 512], F32, "fa")
    fb = cx.sb([128, 512], F32, "fb")
    fc = cx.sb([128, 512], F32, "fc")
    fa_b, fb_b, fc_b = Buf("fa"), Buf("fb"), Buf("fc")
    kms = cx.sb([128, 8], F32, "kms")
    kms_b = Buf("kms")
    kmb = cx.sb([128, 8], BF16, "kmb")
    kmb_b = Buf("kmb")
    ri = [0]
    pi = [0]

    def rope_proj(s, c0, dst, dst_b, pk, want_kms=False, split=None):
        for tg in range(4):
            bk = sbank()
            proj_fm(bk, s, c0, 128, tg)
            i = ri[0] % 2
            ri[0] += 1
            cols = slice(tg * 512, (tg + 1) * 512)
            V("act", lambda h, i=i, bk=bk: h.copy(out=qb[i][:, :], in_=pbank[bk][:, :]), (pb_b[bk],), (qb_b[i],))
            V("dve", lambda h, i=i, bk=bk, cols=cols: h.tensor_tensor(out=t1[i][:, :], in0=pbank[bk][:, :], in1=ctab[:, cols], op=ALU.mult), (pb_b[bk], tab_b), (t1_b[i],))
            b2 = abank()
            V("pe", lambda h, i=i, b2=b2: h.matmul(pbank[b2][:, :], lhsT=perm[:, pk, :], rhs=qb[i][:, :], start=True, stop=True), (perm_b, qb_b[i]), (pb_b[b2],))
            V("dve", lambda h, i=i, b2=b2, cols=cols: h.tensor_tensor(out=t2[i][:, :], in0=pbank[b2][:, :], in1=stab[:, cols], op=ALU.mult), (pb_b[b2], tab_b), (t2_b[i],))
            V("pool", lambda h, i=i: h.tensor_tensor(out=t1[i][:, :], in0=t1[i][:, :], in1=t2[i][:, :], op=ALU.add), (t1_b[i], t2_b[i]), (t1_b[i],))
            if split is None:
                V("act", lambda h, i=i, cols=cols: h.copy(out=dst[:, cols], in_=t1[i][:, :]), (t1_b[i],), (dst_b,))
            else:
                for m in range(2):
                    V("act", lambda h, i=i, cols=cols, m=m: h.copy(out=split[m * 64:(m + 1) * 64, m, cols], in_=t1[i][m * 64:(m + 1) * 64, :]), (t1_b[i],), (dst_b,))
            if want_kms:
                V("dve", lambda h, i=i, tg=tg: h.tensor_reduce(out=kms[:, 2 * tg:2 * tg + 2], in_=t1[i][:, :].rearrange("p (b k) -> p b k", b=2), axis=AX.X, op=ALU.add),
                  (t1_b[i],), (kms_b,))

    def v_proj(s, c0, dst, dst_b):
        for tt in range(NT):
            bk = sbank()
            proj_tm(bk, s, c0, 128, tt)
            if tt % 2 == 0:
                V("act", lambda h, bk=bk, tt=tt: h.copy(out=dst[:, tt, :], in_=pbank[bk][:, 0:128]), (pb_b[bk],), (dst_b,))
            else:
                V("dve", lambda h, bk=bk, tt=tt: h.tensor_copy(out=dst[:, tt, :], in_=pbank[bk][:, 0:128]), (pb_b[bk],), (dst_b,))

    def store_head(kind, hh, j, s):
        r0 = row_of(kind, hh, j)
        if A.get("out_split"):
            P.add("pool" if A.get("out_cast") else "sp", lambda h: [h.dma_start(out=out[t, r0:r0 + 128, :], in_=stg[s][:, t * (S // 2):(t + 1) * (S // 2)]) for t in range(2)],
                  reads=(stg_b[s],), dma=True, key=f"so{s}", ndma=2)
        else:
            P.add("sp", lambda h: h.dma_start(out=out[r0:r0 + 128, :], in_=stg[s][:, :]), reads=(stg_b[s],), dma=True, key=f"so{s}")

    def rms_over_partitions(src, src_b, n):
        V("act", lambda h: h.activation(out=fc[:, 0:n], in_=src, func=AF.Square), (src_b,), (fc_b,))
        b2 = abank()
        V("pe", lambda h: h.matmul(pbank[b2][:, 0:n], lhsT=gcst[:, 2, :], rhs=fc[:, 0:n], start=True, stop=True), (gcst_b, fc_b), (pb_b[b2],))
        V("act", lambda h: h.activation(out=fb[:, 0:n], in_=pbank[b2][:, 0:n], func=AF.Sqrt, bias=rc[:, 5:6], scale=1.0 / 128.0), (pb_b[b2], rc_b), (fb_b,))
        V("dve", lambda h: h.reciprocal(out=fb[:, 0:n], in_=fb[:, 0:n]), (fb_b,), (fb_b,))

    hs = [0]
    so = [0]

    if "diff" in phases:
        make_tables(0)
        lam = cx.sb([128, 4], F32, "lam")
        lam_b = Buf("lam")
        V("dve", lambda h: h.tensor_tensor(out=fa[:, 0:64], in0=dl[:, 0:64], in1=dl[:, 64:128], op=ALU.mult), (dl_b,), (fa_b,))
        V("dve", lambda h: h.tensor_tensor(out=fa[:, 64:128], in0=dl[:, 128:192], in1=dl[:, 192:256], op=ALU.mult), (dl_b, fa_b), (fa_b,))
        V("dve", lambda h: h.tensor_reduce(out=lam[:, 0:2], in_=fa[:, 0:128].rearrange("p (a k) -> p a k", a=2), axis=AX.X, op=ALU.add), (fa_b,), (lam_b,))
        V("act", lambda h: h.activation(out=lam[:, 0:2], in_=lam[:, 0:2], func=AF.Exp), (lam_b,), (lam_b,))
        V("dve", lambda h: h.tensor_tensor(out=lam[:, 2:3], in0=lam[:, 1:2], in1=lam[:, 0:1], op=ALU.subtract), (lam_b,), (lam_b,))
        V("dve", lambda h: h.tensor_tensor(out=lam[:, 2:3], in0=lam[:, 2:3], in1=lc[:, 0:1], op=ALU.subtract), (lam_b, lc_b), (lam_b,))
        V("dve", lambda h: h.tensor_tensor(out=lam[:, 3:4], in0=dngt[:, 0:1], in1=lc[:, 1:2], op=ALU.mult), (dng_b, lc_b, lam_b), (lam_b,))
        for k2 in range(2):
            V("pool", lambda h, k2=k2: h.memset(qT2[k2][:, :, :], 0.0), (), (qT_b[k2],))
        for hh, hd in [(u, v) for u in HH for v in range(3)]:
            cur["hh"] = hh
            s = load_piece(hd * 384, 384)
            q = hs[0] % 2
            hs[0] += 1
            rope_proj(s, 0, None, qT_b[q], 0, split=qT2[q])
            rope_proj(s, 128, kT[q], kT_b[q], 0)
            v_proj(s, 256, vv[q], vv_b[q])
            so_s = so[0] % 2
            so[0] += 1
            pend = []
            for qg in range(4):
                nkc = 4 * qg + 4
                units = [(kc, m) for kc in range(nkc) for m in range(2)]
                ctx = {}

                def ph1(u):
                    kc, m = u
                    bk = sbank()
                    V("pe", lambda h: h.matmul(pbank[bk][:, :], lhsT=kT[q][:, kc * 128:(kc + 1) * 128],
                                               rhs=qT2[q][:, m, qg * 512:(qg + 1) * 512], start=True, stop=True),
                      (kT_b[q], qT_b[q]), (pb_b[bk],))
                    pj = pi[0] % NPT
                    pi[0] += 1
                    ctx[u] = pj
                    V("act", lambda h: h.activation(out=pT[pj][:, :], in_=pbank[bk][:, :], func=AF.Exp, scale=0.125), (pb_b[bk],), (pT_b[pj],))
                    if kc >= 4 * qg:
                        j = kc - 4 * qg
                        V("pool", lambda h: h.tensor_tensor(out=pT[pj][:, :], in0=pT[pj][:, :], in1=cmask[:, (3 - j) * 128:(3 - j) * 128 + 512], op=ALU.mult), (pT_b[pj], cmask_b), (pT_b[pj],))

                def ph2(u):
                    kc, m = u
                    pj = ctx[u]
                    V("pe", lambda h: h.matmul(pbank[4 + m][:, :], lhsT=vv[q][:, kc, :], rhs=pT[pj][:, :], start=(kc == 0), stop=(kc == nkc - 1)),
                      (vv_b[q], pT_b[pj]), (pb_b[4 + m],))
                    V("pe", lambda h: h.matmul(pbank[6 + m][:, :], lhsT=ones_bf[:, :], rhs=pT[pj][:, :], start=(kc == 0), stop=(kc == nkc - 1)),
                      (onesbf_b, pT_b[pj]), (pb_b[6 + m],))

                pipeline(units, ph1, ph2, before_ph2=pend.pop() if pend else None)
                def finalize(qg=qg, so_s=so_s):
                    V("dve", lambda h: h.reciprocal(out=fb[:, :], in_=pbank[6][:, :]), (pb_b[6],), (fb_b,))
                    V("dve", lambda h: h.tensor_tensor(out=fa[:, :], in0=pbank[4][:, :], in1=fb[:, :], op=ALU.mult), (pb_b[4], fb_b), (fa_b,))
                    V("dve", lambda h: h.reciprocal(out=fb[:, :], in_=pbank[7][:, :]), (pb_b[7], fb_b), (fb_b,))
                    V("dve", lambda h: h.tensor_tensor(out=fc[:, :], in0=pbank[5][:, :], in1=fb[:, :], op=ALU.mult), (pb_b[5], fb_b), (fc_b,))
                    V("dve", lambda h: h.scalar_tensor_tensor(out=fa[:, :], in0=fc[:, :], scalar=lam[:, 2:3], in1=fa[:, :], op0=ALU.mult, op1=ALU.add), (fc_b, fa_b, lam_b), (fa_b,))
                    rms_over_partitions(fa[:, :], fa_b, 512)
                    V("dve", lambda h, qg=qg: h.scalar_tensor_tensor(out=stg[so_s][:, qg * 512:(qg + 1) * 512], in0=fa[:, :], scalar=lam[:, 3:4], in1=fb[:, :], op0=ALU.mult, op1=ALU.mult),
                      (fa_b, fb_b, lam_b), (stg_b[so_s],))
                pend.append(finalize)
            pend.pop()()
            store_head("d", hh, hd, so_s)

    if "moba" in phases:
        make_tables(1)
        selT = cx.sb([8, S], BF16, "selT")
        selT_b = Buf("selT")
        gm = cx.sb([128, 8], F32, "gm")
        top8 = cx.sb([128, 8], F32, "top8")
        sel = cx.sb([128, 8], F32, "sel")
        gm_b = Buf("gm")
        SC = 128.0 ** -0.5
        for hh, hd in [(u, v) for u in HH for v in range(3)]:
            cur["hh"] = hh
            s = load_piece(1936 + hd * 384, 384)
            q = hs[0] % 2
            hs[0] += 1
            rope_proj(s, 0, qT[q], qT_b[q], 1)
            rope_proj(s, 128, kT[q], kT_b[q], 1, want_kms=True)
            v_proj(s, 256, vv[q], vv_b[q])
            V("act", lambda h: h.mul(out=kmb[:, :], in_=kms[:, :], mul=1.0 / 256.0), (kms_b,), (kmb_b,))
            for tt in range(NT):
                own = tt // 2
                b2 = abank()
                V("pe", lambda h, b2=b2, tt=tt: h.matmul(pbank[b2][:, 0:8], lhsT=qT[q][:, tt * 128:(tt + 1) * 128], rhs=kmb[:, :], start=True, stop=True),
                  (qT_b[q], kmb_b), (pb_b[b2],))
                V("dve", lambda h, b2=b2, own=own: h.tensor_tensor(out=gm[:, :], in0=pbank[b2][:, 0:8], in1=nmask[:, own * 8:(own + 1) * 8], op=ALU.add), (pb_b[b2], nmask_b), (gm_b,))
                V("dve", lambda h: h.max(out=top8[:, :], in_=gm[:, :]), (gm_b,), (gm_b,))
                V("dve", lambda h, own=own: h.scalar_tensor_tensor(out=sel[:, :], in0=gm[:, :], scalar=top8[:, 2:3], in1=pmask[:, own * 8:(own + 1) * 8], op0=ALU.is_ge, op1=ALU.mult),
                  (gm_b, pmask_b), (gm_b,))
                b3 = abank()
                V("pe", lambda h, b3=b3: h.transpose(out=pbank[b3][0:8, 0:128], in_=sel[:, :], identity=ident[:, :]), (gm_b, ident_b), (pb_b[b3],))
                V("act", lambda h, b3=b3, tt=tt: h.copy(out=selT[0:8, tt * 128:(tt + 1) * 128], in_=pbank[b3][0:8, 0:128]), (pb_b[b3],), (selT_b,))
            so_s = so[0] % 2
            so[0] += 1
            for qblk in range(8):
                ab = 4 + 2 * (qblk % 2)
                zb = ab + 1
                nkc = 2 * qblk + 2
                qc = slice(qblk * 256, (qblk + 1) * 256)
                ctx = {}
                mbs = {}

                def ph1(kc):
                    n = kc // 2
                    bk = sbank()
                    V("pe", lambda h: h.matmul(pbank[bk][:, 0:256], lhsT=kT[q][:, kc * 128:(kc + 1) * 128], rhs=qT[q][:, qc], start=True, stop=True),
                      (kT_b[q], qT_b[q]), (pb_b[bk],))
                    pj = pi[0] % NPT
                    pi[0] += 1
                    ctx[kc] = pj
                    if n < qblk and kc % 2 == 0:
                        mb = abank()
                        mbs[n] = mb
                        V("pe", lambda h: h.matmul(pbank[mb][:, 0:256], lhsT=esel[0:8, n, :], rhs=selT[0:8, qc], start=True, stop=True),
                          (esel_b, selT_b), (pb_b[mb],))
                    V("act", lambda h: h.activation(out=pT[pj][:, 0:256], in_=pbank[bk][:, 0:256], func=AF.Exp, scale=SC), (pb_b[bk],), (pT_b[pj],))
                    if n < qblk:
                        mb = mbs[n]
                        V("dve", lambda h: h.tensor_tensor(out=pT[pj][:, 0:256], in0=pT[pj][:, 0:256], in1=pbank[mb][:, 0:256], op=ALU.mult), (pT_b[pj], pb_b[mb]), (pT_b[pj],))
                    else:
                        j = kc % 2
                        V("pool", lambda h: h.tensor_tensor(out=pT[pj][:, 0:256], in0=pT[pj][:, 0:256], in1=cmask[:, (3 - j) * 128:(3 - j) * 128 + 256], op=ALU.mult), (pT_b[pj], cmask_b), (pT_b[pj],))

                def ph2(kc):
                    pj = ctx[kc]
                    V("pe", lambda h: h.matmul(pbank[ab][:, 0:256], lhsT=vv[q][:, kc, :], rhs=pT[pj][:, 0:256], start=(kc == 0), stop=(kc == nkc - 1)),
                      (vv_b[q], pT_b[pj]), (pb_b[ab],))
                    V("pe", lambda h: h.matmul(pbank[zb][:, 0:256], lhsT=ones_bf[:, :], rhs=pT[pj][:, 0:256], start=(kc == 0), stop=(kc == nkc - 1)),
                      (onesbf_b, pT_b[pj]), (pb_b[zb],))

                pipeline(list(range(nkc)), ph1, ph2)
                V("dve", lambda h, zb=zb: h.reciprocal(out=fb[:, 0:256], in_=pbank[zb][:, 0:256]), (pb_b[zb],), (fb_b,))
                V("dve", lambda h, ab=ab, qc=qc: h.tensor_tensor(out=stg[so_s][:, qc], in0=pbank[ab][:, 0:256], in1=fb[:, 0:256], op=ALU.mult), (pb_b[ab], fb_b), (stg_b[so_s],))
            store_head("m", hh, hd, so_s)

    if "gla" in phases:
        gq, gk = t2[0], t2[1]
        ggT = cx.sb([16, 512], F32, "ggT")
        grs = cx.sb([128, 2, 512], BF16, "grs")
        gq_b, gk_b, ggT_b, grs_b = t2_b[0], t2_b[1], Buf("ggT"), Buf("grs")
        ktm = cx.sb([128, 128], F32, "ktm")
        gv = cx.sb([128, 256], BF16, "gv")
        la = cx.sb([128, 128], F32, "la")
        ktm_b, gv_b, la_b = Buf("ktm"), Buf("gv"), Buf("la")
        Eq = cx.sb([128, 128], F32, "Eq")
        Ek = cx.sb([128, 128], F32, "Ek")
        Er = cx.sb([128, 128], F32, "Er")
        Eq_b, Ek_b, Er_b = Buf("Eq"), Buf("Ek"), Buf("Er")
        qtl = cx.sb([128, 128], BF16, "qtl")
        kt2 = cx.sb([128, 2, 128], BF16, "kt2")
        khat = cx.sb([128, 128], BF16, "khat")
        attm = [cx.sb([128, 128], BF16, f"attm{i}") for i in range(2)]
        qtl_b, kt2_b, khat_b = Buf("qtl"), Buf("kt2"), Buf("khat")
        attm_b = [Buf(f"attm{i}") for i in range(2)]
        Sst = cx.sb([128, 128], F32, "Sst")
        Sb2 = cx.sb([128, 2, 128], BF16, "Sb2")
        Sst_b, Sb2_b = Buf("Sst"), Buf("Sb2")
        og_b = [t1_b[0], t1_b[1]]
        for hh in HH:
            cur["hh"] = hh
            sa = load_piece(1152, 400)
            sb_ = load_piece(1552, 384)
            V("pool", lambda h: h.memset(kt2[:, :, :], 0.0), (), (kt2_b,))
            V("pool", lambda h: h.memset(Sst[:, :], 0.0), (), (Sst_b,))
            V("pool", lambda h: h.memset(Sb2[:, :, :], 0.0), (), (Sb2_b,))
            so0 = so[0] % 2
            so1 = (so[0] + 1) % 2
            so[0] += 2
            sos = (so0, so1)
            for tg in range(4):
                bk = gbank()
                proj_fm(bk, sa, 0, 128, tg)
                V("act", lambda h, bk=bk: h.copy(out=gq[:, :], in_=pbank[bk][:, :]), (pb_b[bk],), (gq_b,))
                bk = gbank()
                proj_fm(bk, sa, 128, 128, tg)
                V("dve", lambda h, bk=bk: h.tensor_copy(out=gk[:, :], in_=pbank[bk][:, :]), (pb_b[bk],), (gk_b,))
                bk = gbank()
                proj_fm(bk, sa, 256, 16, tg)
                V("act", lambda h, bk=bk: h.copy(out=ggT[:, :], in_=pbank[bk][0:16, :]), (pb_b[bk],), (ggT_b,))
                for hd in range(2):
                    bk = gbank()
                    proj_fm(bk, sb_, 128 + hd * 128, 128, tg)
                    V("act", lambda h, bk=bk, hd=hd: h.activation(out=grs[:, hd, :], in_=pbank[bk][:, :], func=AF.Silu), (pb_b[bk],), (grs_b,))
                for ci in range(4):
                    tt = tg * 4 + ci
                    cc = slice(ci * 128, (ci + 1) * 128)
                    bk = gbank()
                    proj_tm(bk, sa, 128, 128, tt)
                    V("act", lambda h, bk=bk: h.copy(out=ktm[:, :], in_=pbank[bk][:, 0:128]), (pb_b[bk],), (ktm_b,))
                    bk = gbank()
                    proj_tm(bk, sa, 272, 128, tt, 0)
                    proj_tm(bk, sb_, 0, 128, tt, 128)
                    V("dve", lambda h, bk=bk: h.tensor_copy(out=gv[:, :], in_=pbank[bk][:, 0:256]), (pb_b[bk],), (gv_b,))
                    bk = gbank()
                    V("pe", lambda h, bk=bk, cc=cc: h.matmul(pbank[bk][:, 0:128], lhsT=ggT[0:16, cc], rhs=ggut_l[hh][0:16, :], start=True, stop=True), (ggT_b, ggu_bl[hh]), (pb_b[bk],))
                    V("dve", lambda h, bk=bk: h.tensor_tensor(out=la[:, :], in0=pbank[bk][:, 0:128], in1=ggbt_l[hh][:, :], op=ALU.add), (pb_b[bk], ggb_bl[hh]), (la_b,))
                    V("act", lambda h: h.activation(out=la[:, :], in_=la[:, :], func=AF.Sigmoid), (la_b,), (la_b,))
                    V("act", lambda h: h.activation(out=la[:, :], in_=la[:, :], func=AF.Ln), (la_b,), (la_b,))
                    V("dve", lambda h: h.tensor_scalar(out=la[:, :], in0=la[:, :], scalar1=1.0 / 16.0, scalar2=None, op0=ALU.mult), (la_b,), (la_b,))
                    bc = gbank()
                    V("pe", lambda h, bc=bc: h.matmul(pbank[bc][:, 0:128], lhsT=la[:, :], rhs=gcst[:, 0, :], start=True, stop=True), (la_b, gcst_b), (pb_b[bc],))
                    V("act", lambda h, bc=bc: h.activation(out=Eq[:, :], in_=pbank[bc][:, 0:128], func=AF.Exp), (pb_b[bc],), (Eq_b,))
                    V("act", lambda h, bc=bc: h.activation(out=Ek[:, :], in_=pbank[bc][:, 0:128], func=AF.Exp, scale=-1.0), (pb_b[bc],), (Ek_b,))
                    br = gbank()
                    V("pe", lambda h, br=br: h.matmul(pbank[br][:, 0:128], lhsT=gcst[:, 1, :], rhs=la[:, :], start=True, stop=True), (la_b, gcst_b), (pb_b[br],))
                    V("act", lambda h, br=br: h.activation(out=Er[:, :], in_=pbank[br][:, 0:128], func=AF.Exp), (pb_b[br],), (Er_b,))
                    V("dve", lambda h, cc=cc: h.scalar_tensor_tensor(out=qtl[:, :], in0=gq[:, cc], scalar=0.125, in1=Eq[:, :], op0=ALU.mult, op1=ALU.mult), (gq_b, Eq_b), (qtl_b,))
                    for hd in range(2):
                        ps_ = slice(hd * 64, (hd + 1) * 64)
                        V("pool", lambda h, hd=hd, ps_=ps_, cc=cc: h.tensor_tensor(out=kt2[ps_, hd, :], in0=gk[ps_, cc], in1=Ek[ps_, :], op=ALU.mult), (gk_b, Ek_b), (kt2_b,))
                    V("pool", lambda h: h.tensor_tensor(out=khat[:, :], in0=ktm[:, :], in1=Er[:, :], op=ALU.mult), (ktm_b, Er_b), (khat_b,))
                    for hd in range(2):
                        bt = gbank()
                        V("pe", lambda h, bt=bt, hd=hd: h.matmul(pbank[bt][:, 0:128], lhsT=kt2[:, hd, :], rhs=qtl[:, :], start=True, stop=True), (kt2_b, qtl_b), (pb_b[bt],))
                        V("dve", lambda h, bt=bt, hd=hd: h.tensor_tensor(out=attm[hd][:, :], in0=pbank[bt][:, 0:128], in1=tri_bf, op=ALU.mult), (pb_b[bt], cmask_b), (attm_b[hd],))
                        bo = gbank()
                        V("pe", lambda h, bo=bo, hd=hd: h.matmul(pbank[bo][:, 0:128], lhsT=gv[:, hd * 128:(hd + 1) * 128], rhs=attm[hd][:, :], start=True, stop=False), (gv_b, attm_b[hd]), (pb_b[bo],))
                        V("pe", lambda h, bo=bo, hd=hd: h.matmul(pbank[bo][:, 0:128], lhsT=Sb2[:, hd, :], rhs=qtl[:, :], start=False, stop=True), (Sb2_b, qtl_b), (pb_b[bo],))
                        V("act", lambda h, bo=bo, hd=hd, cc=cc: h.copy(out=t1[hd][:, cc], in_=pbank[bo][:, 0:128]), (pb_b[bo],), (og_b[hd],))
                    bkv = gbank()
                    V("pe", lambda h, bkv=bkv: h.matmul(pbank[bkv][:, 0:256], lhsT=khat[:, :], rhs=gv[:, :], start=True, stop=True), (khat_b, gv_b), (pb_b[bkv],))
                    for hd in range(2):
                        ps_ = slice(hd * 64, (hd + 1) * 64)
                        V("dve", lambda h, hd=hd, ps_=ps_, bkv=bkv: h.scalar_tensor_tensor(out=Sst[ps_, :], in0=Sst[ps_, :], scalar=Eq[ps_, 127:128], in1=pbank[bkv][ps_, hd * 128:(hd + 1) * 128],
                                                                                       op0=ALU.mult, op1=ALU.add), (Sst_b, Eq_b, pb_b[bkv]), (Sst_b,))
                        V("act", lambda h, hd=hd, ps_=ps_: h.copy(out=Sb2[ps_, hd, :], in_=Sst[ps_, :]), (Sst_b,), (Sb2_b,))
                for hd in range(2):
                    rms_over_partitions(t1[hd][:, :], og_b[hd], 512)
                    V("dve", lambda h, hd=hd: h.scalar_tensor_tensor(out=fa[:, :], in0=t1[hd][:, :], scalar=gngt[:, 0:1], in1=fb[:, :], op0=ALU.mult, op1=ALU.mult),
                      (og_b[hd], fb_b, gng_b), (fa_b,))
                    V("pool", lambda h, hd=hd, tg=tg: h.tensor_tensor(out=stg[sos[hd]][:, tg * 512:(tg + 1) * 512], in0=fa[:, :], in1=grs[:, hd, :], op=ALU.mult),
                      (fa_b, grs_b), (stg_b[sos[hd]],))
            for hd in range(2):
                store_head("g", hh, hd, sos[hd])


_CACHE = {}

_IN_OFF = {"dq": 0, "dk": 768, "dv": 1536, "gq": 2304, "gk": 2560, "gv": 2816, "gr": 3328, "gg": 3840, "mq": 3856, "mk": 4624, "mv": 5392}


def _wsel_cols(hh):
    o = _IN_OFF
    cols = []
    for h in range(3 * hh, 3 * hh + 3):
        cols += list(range(o["dq"] + h * 128, o["dq"] + (h + 1) * 128))
        cols += list(range(o["dk"] + h * 128, o["dk"] + (h + 1) * 128))
        cols += list(range(o["dv"] + h * 128, o["dv"] + (h + 1) * 128))
    g0 = 2 * hh
    cols += list(range(o["gq"] + g0 * 64, o["gq"] + (g0 + 2) * 64))
    cols += list(range(o["gk"] + g0 * 64, o["gk"] + (g0 + 2) * 64))
    cols += list(range(o["gg"], o["gg"] + 16))
    cols += list(range(o["gv"] + g0 * 128, o["gv"] + (g0 + 1) * 128))
    cols += list(range(o["gv"] + (g0 + 1) * 128, o["gv"] + (g0 + 2) * 128))
    cols += list(range(o["gr"] + g0 * 128, o["gr"] + (g0 + 2) * 128))
    for h in range(3 * hh, 3 * hh + 3):
        cols += list(range(o["mq"] + h * 128, o["mq"] + (h + 1) * 128))
        cols += list(range(o["mk"] + h * 128, o["mk"] + (h + 1) * 128))
        cols += list(range(o["mv"] + h * 128, o["mv"] + (h + 1) * 128))
    assert len(cols) == NCOL
    return np.array(cols)


_CONST_SHAPES = {"ident": [128, 128], "rc": [128, 8], "perm": [2, 128, 128], "cmask": [128, 896], "gcst": [3, 128, 128],
                 "nmask": [1, 64], "pmask": [1, 64], "esel": [8, 8, 128]}


def build_fused():
    cx = Ctx()
    P = cx.P
    x = cx.din("x", [TOK, D])
    pos = cx.din("pos", [1, SEQ], I32)
    cst = {k: cx.din(k, shp) for k, shp in _CONST_SHAPES.items()}
    L = []
    for i in range(DEPTH):
        L.append({
            "p": cx.din(f"p{i}", [TOK, 256]), "wsel2": cx.din(f"wsel{i}", [1, D, NCOL]), "wo": cx.din(f"wo{i}", [D, D]),
            "f1g": cx.din(f"f1g{i}", [D, DFF]), "f1u": cx.din(f"f1u{i}", [D, DFF]), "f1d": cx.din(f"f1d{i}", [DFF, D]),
            "f2g": cx.din(f"f2g{i}", [D, DFF]), "f2u": cx.din(f"f2u{i}", [D, DFF]), "f2d": cx.din(f"f2d{i}", [DFF, D]),
            "wpe": cx.din(f"wpe{i}", [256, D]), "wpg": cx.din(f"wpg{i}", [D, D]),
            "lng": cx.din(f"lng{i}", [4, D]), "lnb": cx.din(f"lnb{i}", [4, D]),
            "dlam": cx.din(f"dlam{i}", [1, 256]), "lamc": cx.din(f"lamc{i}", [1, 2]), "dng": cx.din(f"dng{i}", [128, 1]),
            "ggu2": cx.din(f"ggu{i}", [1, 16, 128]), "ggb2": cx.din(f"ggb{i}", [1, 1, 128]), "gng": cx.din(f"gng{i}", [128, 1]),
        })
    y = cx.dout("y", [TOK, D])
    HB = D // 2
    x1loc = cx.dscratch("x1loc", [TOK, D])
    x1b = cx.dscratch("x1b", [TOK, D], BF16)
    xg = cx.dscratch("xg", [4, TOK, D], BF16)
    xsel = cx.dscratch("xsel", [SEQ, D], BF16)
    oTloc = cx.dscratch("oTloc", [2, HB, TOK], BF16)
    og = cx.dscratch("og", [4 * 4 * (HB // 2), TOK], BF16)
    oTsel = cx.dscratch("oTsel", [D, TOK], BF16)
    ident = cst["ident"]
    RG = [[0, 1, 2, 3], [4, 5, 6, 7]]
    dyn = {}

    def beta(h):
        return (h.partition_id() // 2) % 2

    def half(h):
        return h.partition_id() % 2

    def dval(h, name, fn, hi):
        if (name, id(h)) not in dyn:
            dyn[(name, id(h))] = h.snap(fn(h), min_val=0, max_val=hi)
        return dyn[(name, id(h))]

    def cc(src, dst):
        P.add("pool", lambda h: h.collective_compute("AllGather", ALU.bypass, replica_groups=RG, ins=[src], outs=[dst]),
              dma=True, key="cc", cc=True)

    def exchange_x():
        toks = []
        for k in range(4):
            gb = Buf(f"xg{k}")
            P.add("pool", lambda h, k=k: h.collective_compute("AllGather", ALU.bypass, replica_groups=RG, ins=[x1b[k * 256:(k + 1) * 256, :]], outs=[xg[k]]),
                  writes=(gb,), dma=True, key="cc", cc=True)
            pb = Buf(f"xsel{k}")
            P.add("sp", lambda h, k=k: [h.dma_start(out=xsel[a * TOK + k * 256:a * TOK + (k + 1) * 256, :],
                                                    in_=xg[k][bass.ds(dval(h, "xrow", lambda e: beta(e) * 512, 512), 512), :][a * 256:(a + 1) * 256, :])
                                        for a in range(2)], reads=(gb,), writes=(pb,), dma=True, key="pickx", ndma=2)
            toks.append(pb)
        return toks

    def exchange_o():
        toks = []
        for t in range(2):
            for f2 in range(2):
                k = t * 2 + f2
                gb = Buf(f"og{k}")
                toks.append(gb)
                P.add("pool", lambda h, t=t, f2=f2, k=k: h.collective_compute("AllGather", ALU.bypass, replica_groups=RG,
                                                                               ins=[oTloc[t, f2 * 512:(f2 + 1) * 512, :]], outs=[og[k * 2048:(k + 1) * 2048, :]]),
                      writes=(gb,), dma=True, key="cc", cc=True)
        out = []
        for hh in range(2):
            for f2 in range(2):
                pb = Buf(f"osel{hh}{f2}")
                P.add("act", lambda h, hh=hh, f2=f2: h.dma_start(
                    out=oTsel[hh * HB + f2 * 512:hh * HB + (f2 + 1) * 512, :],
                    in_=og[bass.ds(dval(h, "orow", lambda e: half(e) * 4096 + beta(e) * 1024, 5120), 3072), :][f2 * 2048 + hh * 512:f2 * 2048 + (hh + 1) * 512, :]),
                    reads=tuple(toks), writes=(pb,), dma=True, key="picko")
                out.append(pb)
        return out

    cx.begin_stage()
    emit_A(cx, x, x1loc, L[0]["f1g"], L[0]["f1u"], L[0]["f1d"], L[0]["lng"], L[0]["lnb"], ident, yb=x1b)
    cx.end_stage()
    for i in range(DEPTH):
        w = L[i]
        cx.begin_stage()
        A = {"x": xsel, "pos": pos, "out": oTloc, "nhh": 1, "out_split": True, "x_cast": True, "out_cast": True, "pre": exchange_x,
             "row_of": lambda kind, hh, j: {"d": j * 128, "g": 384 + j * 128, "m": 640 + j * 128}[kind]}
        A.update({k: w[k] for k in ("wsel2", "dlam", "lamc", "dng", "ggu2", "ggb2", "gng")})
        A.update(cst)
        emit_mix(cx, A)
        cx.end_stage()
        cx.begin_stage()
        if i + 1 < DEPTH:
            n = L[i + 1]
            nxt = (n["f1g"], n["f1u"], n["f1d"], n["lng"], n["lnb"])
            emit_C(cx, x1loc, oTsel, w["p"], w["wo"], w["f2g"], w["f2u"], w["f2d"], w["wpe"], w["wpg"], w["lng"], w["lnb"], ident, x1loc, nxt, yb=x1b, pre=exchange_o)
        else:
            emit_C(cx, x1loc, oTsel, w["p"], w["wo"], w["f2g"], w["f2u"], w["f2d"], w["wpe"], w["wpg"], w["lng"], w["lnb"], ident, y, None, pre=exchange_o)
        cx.end_stage()
    return cx.finish()


def _wo_rows():
    rows = []
    for hh in range(2):
        for j in range(3):
            rows += list(range((3 * hh + j) * 128, (3 * hh + j + 1) * 128))
        for j in range(2):
            rows += list(range(768 + (2 * hh + j) * 128, 768 + (2 * hh + j + 1) * 128))
        for j in range(3):
            rows += list(range(1280 + (3 * hh + j) * 128, 1280 + (3 * hh + j + 1) * 128))
    return np.array(rows)


def kernel(**inputs):
    f = lambda k: np.ascontiguousarray(np.asarray(inputs[k]), dtype=np.float32)
    x = f("x")
    p = f("p")
    positions = np.ascontiguousarray(np.asarray(inputs["positions"]).astype(np.int32))
    w_in, w_out = f("w_in"), f("w_out")
    dlam, dng = f("diff_lambda"), f("diff_norm_g")
    ggu, ggb, gng = f("gla_gate_up"), f("gla_gate_b"), f("gla_norm_g")
    f1g, f1u, f1d = f("ffn1_gate"), f("ffn1_up"), f("ffn1_down")
    f2g, f2u, f2d = f("ffn2_gate"), f("ffn2_up"), f("ffn2_down")
    wpe, wpg = f("w_pe"), f("w_pg")
    lng, lnb = f("ln_g"), f("ln_b")
    shared = dict(mix_consts())
    per_hh = [dict(), dict()]
    cols = [_wsel_cols(hh) for hh in range(2)]
    wor = _wo_rows()
    for i in range(DEPTH):
        lam_init = 0.8 - 0.6 * math.exp(-0.3 * i)
        shared.update({
            f"wo{i}": np.ascontiguousarray(w_out[i][wor, :]), f"f1g{i}": f1g[i], f"f1u{i}": f1u[i], f"f1d{i}": f1d[i],
            f"f2g{i}": f2g[i], f"f2u{i}": f2u[i], f"f2d{i}": f2d[i], f"wpe{i}": wpe[i], f"wpg{i}": wpg[i],
            f"lng{i}": lng[i], f"lnb{i}": lnb[i], f"dlam{i}": np.ascontiguousarray(dlam[i].reshape(1, 256)),
            f"lamc{i}": np.array([[lam_init, 1.0 - lam_init]], np.float32), f"dng{i}": np.ascontiguousarray(dng[i].reshape(128, 1)),
            f"gng{i}": np.ascontiguousarray(gng[i].reshape(128, 1)),
        })
        for hh in range(2):
            per_hh[hh].update({
                f"wsel{i}": np.ascontiguousarray(w_in[i][:, cols[hh]])[None],
                f"ggu{i}": np.ascontiguousarray(ggu[i][:, hh * 128:(hh + 1) * 128])[None],
                f"ggb{i}": np.ascontiguousarray(ggb[i][hh * 128:(hh + 1) * 128].reshape(1, 1, 128)),
            })
    in_maps = []
    for c in range(NCORES):
        b, h = c // 2, c % 2
        m = dict(shared)
        m.update(per_hh[h])
        m["x"] = np.ascontiguousarray(x[b, h * TOK:(h + 1) * TOK])
        m["pos"] = np.ascontiguousarray(positions[b].reshape(1, SEQ))
        for i in range(DEPTH):
            m[f"p{i}"] = np.ascontiguousarray(p[i, b, h * TOK:(h + 1) * TOK])
        in_maps.append(m)
    if "fused" not in _CACHE:
        _CACHE["fused"] = build_fused()
    res = run_bass_kernel_spmd(_CACHE["fused"], in_maps, core_ids=list(range(NCORES))).results
    out = np.concatenate([res[c]["y"] for c in range(NCORES)], axis=0)
    return out.reshape(NB, SEQ, D).astype(np.float32)
```

```python
import math
import types
from contextlib import ExitStack
import numpy as np
import concourse.bass as bass
import concourse.mybir as mybir
from concourse.bass_utils import run_bass_kernel_spmd

F32 = mybir.dt.float32
BF16 = mybir.dt.bfloat16
I32 = mybir.dt.int32
AF = mybir.ActivationFunctionType
ALU = mybir.AluOpType
AX = mybir.AxisListType

D = 2048
DFF = 5632
SEQ = 2048
NB = 4
DEPTH = 2
NCORES = 8
TOK = 1024
ALPHA = (2 * DEPTH) ** 0.25
EPS = 1e-5
DIN = 6160


class Buf:
    __slots__ = ("name", "lw", "rd", "excl")

    def __init__(self, name, excl=False):
        self.name = name
        self.lw = None
        self.rd = {}
        self.excl = excl


def _freeze(fn):
    if getattr(fn, "__closure__", None) is None:
        return fn
    cells = []
    for c in fn.__closure__:
        try:
            cells.append(types.CellType(c.cell_contents))
        except ValueError:
            cells.append(c)
    return types.FunctionType(fn.__code__, fn.__globals__, fn.__name__, fn.__defaults__, tuple(cells))


def _flat(x):
    for b in x:
        if isinstance(b, (tuple, list)):
            yield from _flat(b)
        else:
            yield b


class Op:
    __slots__ = ("eng", "fn", "deps", "sig", "cnt", "dma", "key", "ndma", "chan", "cc")


class Prog:
    ENGS = ("pe", "dve", "act", "pool", "sp")

    def __init__(self):
        self.ops = []
        self.bar = []

    def barrier(self):
        last = {}
        for op in self.ops:
            last[op.chan] = op
        self.bar = list(last.values())

    def add(self, eng, fn, reads=(), writes=(), dma=False, key=None, ndma=1, cc=False):
        op = Op()
        op.cc = cc
        op.eng = eng
        op.fn = _freeze(fn)
        op.sig = False
        op.cnt = 0
        op.dma = dma
        op.key = key
        op.ndma = ndma
        op.chan = ("dma", key) if dma else eng
        deps = {}

        def need(d):
            if d is None or d is op:
                return
            c = d.chan
            if c not in deps or deps[c][0] < d.cnt:
                deps[c] = (d.cnt, d)

        op.cnt = len(self.ops)
        reads = tuple(_flat(reads))
        writes = tuple(_flat(writes))
        for d in self.bar:
            need(d)
        writes = tuple(writes) + tuple(b for b in reads if b.excl)
        for b in reads:
            need(b.lw)
        for b in writes:
            need(b.lw)
            for r in b.rd.values():
                need(r)
        for b in reads:
            b.rd[op.chan] = op
        for b in writes:
            b.lw = op
            b.rd = {}
        op.deps = [d for (_, d) in deps.values()]
        if not dma and eng == "pe":
            op.deps = [d for d in op.deps if d.chan != "pe"]
        for d in op.deps:
            d.sig = True
        self.ops.append(op)
        return op

    def emit(self, nc, stack):
        cnt = {}
        for op in self.ops:
            if op.dma:
                cnt[op.chan] = cnt.get(op.chan, 0) + (1 if op.cc else 16 * op.ndma)
                op.cnt = cnt[op.chan]
            elif op.sig:
                cnt[op.chan] = cnt.get(op.chan, 0) + 1
                op.cnt = cnt[op.chan]
            else:
                op.cnt = None
        sems = {}
        for c in cnt:
            nm = ("s_" + (c if isinstance(c, str) else "d_" + str(c[1]))).replace(":", "_")
            sems[c] = stack.enter_context(nc.semaphore(nm))
        self.nsems = len(sems)
        self.final_cnt = dict(cnt)
        block = stack.enter_context(nc.Block())
        streams = {e: [op for op in self.ops if op.eng == e] for e in self.ENGS}

        def run(e, h):
            waited = {}
            for op in streams[e]:
                for d in op.deps:
                    if waited.get(d.chan, 0) < d.cnt:
                        h.wait_ge(sems[d.chan], d.cnt)
                        waited[d.chan] = d.cnt
                ins = op.fn(h)
                if op.cc:
                    ins.then_inc(sems[op.chan])
                elif op.dma:
                    if not isinstance(ins, (list, tuple)):
                        ins = [ins]
                    assert len(ins) == op.ndma
                    for i in ins:
                        i.then_inc(sems[op.chan], 16)
                elif op.sig:
                    ins.then_inc(sems[op.chan], 1)
            if e == "sp":
                for c, v in cnt.items():
                    if waited.get(c, 0) < v:
                        h.wait_ge(sems[c], v)

        @block.tensor
        def _(h):
            run("pe", h)

        @block.vector
        def _(h):
            run("dve", h)

        @block.scalar
        def _(h):
            run("act", h)

        @block.gpsimd
        def _(h):
            run("pool", h)

        @block.sync
        def _(h):
            run("sp", h)


class Ctx:
    def __init__(self):
        self.nc = bass.Bass("TRN2", target_bir_lowering=False)
        self.P = Prog()
        self.stack = ExitStack()
        self.stage = None
        self.n = 0
        self.nstage = 0

    def begin_stage(self):
        self.stage = ExitStack()
        self.nstage += 1

    def end_stage(self):
        self.stage.close()
        self.stage = None
        self.P.barrier()

    def sb(self, shape, dt, name=None):
        self.n += 1
        st = self.stage if self.stage is not None else self.stack
        return st.enter_context(self.nc.sbuf_tensor(f"sb{self.nstage}_" + (name or f"t{self.n}"), list(shape), dt))

    def ps(self, shape, dt=F32, name=None):
        self.n += 1
        st = self.stage if self.stage is not None else self.stack
        return st.enter_context(self.nc.psum_tensor(f"ps{self.nstage}_" + (name or f"p{self.n}"), list(shape), dt))

    def din(self, name, shape, dt=F32):
        return self.nc.dram_tensor(name, list(shape), dt, kind="ExternalInput").ap()

    def dout(self, name, shape, dt=F32):
        return self.nc.dram_tensor(name, list(shape), dt, kind="ExternalOutput").ap()

    def dscratch(self, name, shape, dt=F32):
        return self.nc.dram_tensor(name, list(shape), dt).ap()

    def finish(self):
        if self.stage is not None:
            self.end_stage()
        self.P.emit(self.nc, self.stack)
        self.stack.close()
        self.nc._prog = self.P
        return self.nc


def pipeline(units, ph1, ph2, lag=2, before_ph2=None):
    n = len(units)
    for i in range(n + lag):
        if i < n:
            ph1(units[i])
        if i >= lag:
            if i == lag and before_ph2 is not None:
                before_ph2()
            ph2(units[i - lag])
    if n == 0 and before_ph2 is not None:
        before_ph2()


class RowStage:
    def __init__(self, cx, ntok):
        self.cx = cx
        self.ntok = ntok
        self.NT = ntok // 128
        self.xacc = cx.sb([128, self.NT, D], F32, "xacc")
        self.xacc_cb = [[Buf(f"xacc{t}_{c}") for c in range(4)] for t in range(self.NT)]
        self.xacc_b = [tuple(self.xacc_cb[t]) for t in range(self.NT)]
        self.acc_tmp = [cx.sb([128, 512], F32, f"acct{k}") for k in range(2)]
        self.acc_tmp_b = [Buf(f"acct{k}") for k in range(2)]
        self.xT = cx.sb([128, 16, ntok], BF16, "xT")
        self.xT_b = [Buf(f"xT{t}") for t in range(self.NT)]
        self.ident = cx.sb([128, 128], F32, "ident")
        self.ident_b = Buf("ident")
        self.pbank = [cx.ps([128, 512], F32, f"bank{i}") for i in range(8)]
        self.pbank_b = [Buf(f"bank{i}", excl=True) for i in range(8)]
        self.rr = 0
        self.wbuf = [cx.sb([128, 8192], BF16, f"wbuf{s}") for s in range(2)]
        self.wbuf_b = [Buf(f"wbuf{s}") for s in range(2)]
        self.wdb = [cx.sb([128, 2, D], BF16, f"wdb{s}") for s in range(2)]
        self.wdb_b = [Buf(f"wdb{s}") for s in range(2)]
        self.actT = [cx.sb([128, 2, ntok], BF16, f"actT{s}") for s in range(2)]
        self.actT_b = [Buf(f"actT{s}") for s in range(2)]
        self.sg = [cx.sb([128, 512], F32, f"sg{s}") for s in range(2)]
        self.sg_b = [Buf(f"sg{s}") for s in range(2)]
        self.gbc = cx.sb([128, 2, D], F32, "gbc")
        self.gb_b = Buf("gbc")
        self.ln_stats = [(cx.sb([128, 4, 6], F32, f"stats{k}"), Buf(f"stats{k}")) for k in range(2)]
        self.ln_mv = cx.sb([128, self.NT, 2], F32, "ln_mv")
        self.ln_sc = cx.sb([128, self.NT], F32, "ln_sc")
        self.ln_nb = cx.sb([128, self.NT], F32, "ln_nb")
        self.ln_b = Buf("ln")
        self.xin = [cx.sb([128, D], F32, f"xin{s}") for s in range(2)]
        self.xin_b = [Buf(f"xin{s}") for s in range(2)]
        self.wslot = 0

    def load_ln(self, lng, lnb, idx):
        gbc = self.gbc
        self.cx.P.add("sp", lambda h: [h.dma_start(out=gbc[:, 0, :], in_=lng[idx:idx + 1, :].partition_broadcast(128)),
                                       h.dma_start(out=gbc[:, 1, :], in_=lnb[idx:idx + 1, :].partition_broadcast(128))],
                      writes=(self.gb_b,), dma=True, key="gbc", ndma=2)

    def ln_all(self, zscale, out_fn=None):
        P = self.cx.P
        NT = self.NT
        mv, sc, nb, lb = self.ln_mv, self.ln_sc, self.ln_nb, self.ln_b
        for tt in range(NT):
            stats, sb_ = self.ln_stats[tt % 2]
            P.add("dve", lambda h, tt=tt, stats=stats: [h.bn_stats(out=stats[:, c, :], in_=self.xacc[:, tt, c * 512:(c + 1) * 512]) for c in range(4)][-1],
                  reads=(self.xacc_b[tt],), writes=(sb_,))
            P.add("dve", lambda h, tt=tt, stats=stats: h.bn_aggr(out=mv[:, tt, :], in_=stats[:, :, :]), reads=(sb_,), writes=(lb,))
        P.add("dve", lambda h: h.tensor_scalar_add(out=sc[:, :], in0=mv[:, :, 1], scalar1=EPS / (zscale * zscale)), reads=(lb,), writes=(lb,))
        P.add("act", lambda h: h.sqrt(out=sc[:, :], in_=sc[:, :]), reads=(lb,), writes=(lb,))
        P.add("dve", lambda h: h.reciprocal(out=sc[:, :], in_=sc[:, :]), reads=(lb,), writes=(lb,))
        P.add("dve", lambda h: h.scalar_tensor_tensor(out=nb[:, :], in0=mv[:, :, 0], scalar=-1.0, in1=sc[:, :], op0=ALU.mult, op1=ALU.mult),
              reads=(lb,), writes=(lb,))
        for tt in range(NT):
            if out_fn is None:
                o, ob = self.xacc[:, tt, :], self.xacc_b[tt]
            else:
                o, ob = out_fn(tt)
            xa = self.xacc[:, tt, :]
            P.add("act", lambda h, xa=xa, tt=tt: h.activation(out=xa, in_=xa, func=AF.Identity, bias=nb[:, tt:tt + 1], scale=sc[:, tt:tt + 1]),
                  reads=(lb, self.xacc_b[tt]), writes=(self.xacc_b[tt],))
            P.add("pool", lambda h, xa=xa: h.tensor_tensor(out=xa, in0=xa, in1=self.gbc[:, 0, :], op=ALU.mult), reads=(self.xacc_b[tt], self.gb_b), writes=(self.xacc_b[tt],))
            P.add("dve", lambda h, xa=xa, o=o: h.tensor_tensor(out=o, in0=xa, in1=self.gbc[:, 1, :], op=ALU.add), reads=(self.xacc_b[tt], self.gb_b), writes=(ob,))
            if out_fn is not None:
                out_fn(tt, done=True)


def load_const(cx, dst, dst_b, src_ap, key):
    cx.P.add("sp", lambda h: h.dma_start(out=dst, in_=src_ap), reads=(), writes=(dst_b,), dma=True, key=key)


def build_xT(cx, st, tt, src, src_b, banks=(4, 5, 6, 7)):
    P = cx.P
    for g in range(4):
        bk = banks[(st.rr) % len(banks)]
        st.rr += 1
        pb = st.pbank[bk]
        for j in range(4):
            kc = g * 4 + j
            P.add("pe", lambda h, pb=pb, j=j, kc=kc: h.transpose(out=pb[:, j * 128:(j + 1) * 128], in_=src[:, kc * 128:(kc + 1) * 128], identity=st.ident[:, :]),
                  reads=(src_b, st.ident_b), writes=(st.pbank_b[bk],))
        eng = "act" if g % 2 == 0 else "dve"
        o = st.xT[:, g * 4:(g + 1) * 4, tt * 128:(tt + 1) * 128]
        i = pb[:, :].rearrange("p (j t) -> p j t", j=4)
        if eng == "act":
            P.add("act", lambda h, o=o, i=i: h.copy(out=o, in_=i), reads=(st.pbank_b[bk],), writes=(st.xT_b[tt],))
        else:
            P.add("dve", lambda h, o=o, i=i: h.tensor_copy(out=o, in_=i), reads=(st.pbank_b[bk],), writes=(st.xT_b[tt],))


def ln_stats(cx, st, tt, zscale, tmp):
    P = cx.P
    stats, mv, sc, nb, stat_b = tmp
    P.add("dve", lambda h: [h.bn_stats(out=stats[:, c, :], in_=st.xacc[:, tt, c * 512:(c + 1) * 512]) for c in range(4)][-1],
          reads=(st.xacc_b[tt],), writes=(stat_b,))
    P.add("dve", lambda h: h.bn_aggr(out=mv[:, :], in_=stats[:, :, :]), reads=(stat_b,), writes=(stat_b,))
    P.add("dve", lambda h: h.tensor_scalar_add(out=sc[:, :], in0=mv[:, 1:2], scalar1=EPS / (zscale * zscale)),
          reads=(stat_b,), writes=(stat_b,))
    P.add("act", lambda h: h.sqrt(out=sc[:, :], in_=sc[:, :]), reads=(stat_b,), writes=(stat_b,))
    P.add("dve", lambda h: h.reciprocal(out=sc[:, :], in_=sc[:, :]), reads=(stat_b,), writes=(stat_b,))
    P.add("dve", lambda h: h.scalar_tensor_tensor(out=nb[:, :], in0=mv[:, 0:1], scalar=-1.0, in1=sc[:, :], op0=ALU.mult, op1=ALU.mult),
          reads=(stat_b,), writes=(stat_b,))


def ln_apply(cx, st, tt, g_bc, b_bc, gb_b, out_ap, out_b, tmp):
    P = cx.P
    xa = st.xacc[:, tt, :]
    stats, mv, sc, nb, stat_b = tmp
    P.add("act", lambda h: h.activation(out=xa, in_=xa, func=AF.Identity, bias=nb[:, 0:1], scale=sc[:, 0:1]),
          reads=(stat_b, st.xacc_b[tt]), writes=(st.xacc_b[tt],))
    P.add("pool", lambda h: h.tensor_tensor(out=xa, in0=xa, in1=g_bc, op=ALU.mult), reads=(st.xacc_b[tt], gb_b), writes=(st.xacc_b[tt],))
    P.add("dve", lambda h: h.tensor_tensor(out=out_ap, in0=xa, in1=b_bc, op=ALU.add), reads=(st.xacc_b[tt], gb_b), writes=(out_b,))


def ffn_stage(cx, st, wg, wu, wd, FB=256):
    P = cx.P
    nblk = DFF // FB
    CPB = FB // 128
    wgu = [w[:, :].rearrange("p (w k f) -> p w k f", w=2, k=16) for w in st.wbuf]
    wgu_b = st.wbuf_b
    wdb, wdb_b = st.wdb, st.wdb_b
    actT, actT_b = st.actT, st.actT_b
    sg, sg_b = st.sg, st.sg_b
    wgv = wg.rearrange("(kc p) f -> p kc f", p=128)
    wuv = wu.rearrange("(kc p) f -> p kc f", p=128)
    wdv = wd.rearrange("(c p) n -> p c n", p=128)
    NTG = st.ntok // 512
    allxT = tuple(st.xT_b)
    gi = 0
    di = 0

    def load_gu(b):
        s = b % 2
        P.add("pool", lambda h: [h.dma_start(out=wgu[s][:, 0, :, :], in_=wgv[:, :, b * FB:(b + 1) * FB]),
                                 h.dma_start(out=wgu[s][:, 1, :, :], in_=wuv[:, :, b * FB:(b + 1) * FB])],
              writes=(wgu_b[s],), dma=True, key=f"wbuf{s}", ndma=2)

    def load_d(b):
        s = b % 2
        P.add("pool", lambda h: h.dma_start(out=wdb[s][:, :, :], in_=wdv[:, b * CPB:(b + 1) * CPB, :]),
              writes=(wdb_b[s],), dma=True, key=f"wdb{s}")

    def gateup(b):
        nonlocal gi
        s = b % 2
        for c in range(CPB):
            for tg in range(NTG):
                bg = gi % 2
                bu = 2 + gi % 2
                gi += 1
                for (w, bk) in ((0, bg), (1, bu)):
                    for kc in range(16):
                        P.add("pe", lambda h, w=w, bk=bk, kc=kc, c=c, tg=tg: h.matmul(
                            st.pbank[bk][:, :], lhsT=wgu[s][:, w, kc, c * 128:(c + 1) * 128],
                            rhs=st.xT[:, kc, tg * 512:(tg + 1) * 512], start=(kc == 0), stop=(kc == 15)),
                            reads=(wgu_b[s],) + allxT, writes=(st.pbank_b[bk],))
                sgi = gi % 2
                P.add("act", lambda h, bg=bg, sgi=sgi: h.activation(out=sg[sgi][:, :], in_=st.pbank[bg][:, :], func=AF.Silu),
                      reads=(st.pbank_b[bg],), writes=(sg_b[sgi],))
                P.add("dve", lambda h, bu=bu, sgi=sgi, c=c, tg=tg: h.tensor_tensor(
                    out=actT[s][:, c, tg * 512:(tg + 1) * 512], in0=sg[sgi][:, :], in1=st.pbank[bu][:, :], op=ALU.mult),
                    reads=(sg_b[sgi], st.pbank_b[bu]), writes=(actT_b[s],))

    def down(b):
        nonlocal di
        s = b % 2
        for tt in range(st.NT):
            for cg in range(4):
                bk = 4 + di % 4
                di += 1
                for c in range(CPB):
                    P.add("pe", lambda h, bk=bk, c=c, tt=tt, cg=cg: h.matmul(
                        st.pbank[bk][:, :], lhsT=actT[s][:, c, tt * 128:(tt + 1) * 128],
                        rhs=wdb[s][:, c, cg * 512:(cg + 1) * 512], start=(c == 0), stop=(c == CPB - 1)),
                        reads=(actT_b[s], wdb_b[s]), writes=(st.pbank_b[bk],))
                xa = st.xacc[:, tt, cg * 512:(cg + 1) * 512]
                xb = st.xacc_cb[tt][cg]
                if di % 3 == 0:
                    k = (di // 3) % 2
                    P.add("act", lambda h, bk=bk, k=k: h.copy(out=st.acc_tmp[k][:, :], in_=st.pbank[bk][:, :]),
                          reads=(st.pbank_b[bk],), writes=(st.acc_tmp_b[k],))
                    P.add("pool", lambda h, xa=xa, k=k: h.tensor_tensor(out=xa, in0=xa, in1=st.acc_tmp[k][:, :], op=ALU.add),
                          reads=(st.acc_tmp_b[k], xb), writes=(xb,))
                else:
                    P.add("dve", lambda h, xa=xa, bk=bk: h.tensor_tensor(out=xa, in0=xa, in1=st.pbank[bk][:, :], op=ALU.add),
                          reads=(st.pbank_b[bk], xb), writes=(xb,))

    load_gu(0)
    load_gu(1)
    load_d(0)
    for b in range(nblk):
        gateup(b)
        if b + 2 < nblk:
            load_gu(b + 2)
        if b >= 1:
            down(b - 1)
        if b + 1 < nblk:
            load_d(b + 1)
    down(nblk - 1)


def load_x_tiles(cx, st, x, scale, with_xT=True):
    P = cx.P
    for tt in range(st.NT):
        s = tt % 2
        P.add("sp", lambda h, s=s, tt=tt: h.dma_start(out=st.xin[s][:, :], in_=x[tt * 128:(tt + 1) * 128, :]),
              writes=(st.xin_b[s],), dma=True, key=f"xin{s}")
        if with_xT:
            build_xT(cx, st, tt, st.xin[s], st.xin_b[s])
        P.add("act", lambda h, s=s, tt=tt: h.mul(out=st.xacc[:, tt, :], in_=st.xin[s][:, :], mul=scale),
              reads=(st.xin_b[s],), writes=(st.xacc_b[tt],))


def refresh_xT(cx, st, scale):
    for tt in range(st.NT):
        build_xT(cx, st, tt, st.xacc[:, tt, :], st.xacc_b[tt])
        if scale != 1.0:
            cx.P.add("act", lambda h, tt=tt: h.mul(out=st.xacc[:, tt, :], in_=st.xacc[:, tt, :], mul=scale),
                     reads=(st.xacc_b[tt],), writes=(st.xacc_b[tt],))


def store_out(cx, st, y, yb=None):
    def out_fn(tt, done=False):
        s = tt % 2
        if done:
            cx.P.add("sp", lambda h: h.dma_start(out=y[tt * 128:(tt + 1) * 128, :], in_=st.xin[s][:, :]),
                     reads=(st.xin_b[s],), dma=True, key=f"yo{s}")
            if yb is not None:
                cx.P.add("pool", lambda h: h.dma_start(out=yb[tt * 128:(tt + 1) * 128, :], in_=st.xin[s][:, :]),
                         reads=(st.xin_b[s],), dma=True, key=f"yb{s}")
            return None
        return st.xin[s][:, :], st.xin_b[s]
    return out_fn


def dense_acc(cx, st, w, lhs_fn, nk, lhs_bufs, post):
    P = cx.P
    wv = w.rearrange("(kc p) n -> p kc n", p=128)
    for cg in range(4):
        s = st.wslot % 2
        st.wslot += 1
        wt = st.wbuf[s][:, 0:nk * 512].rearrange("p (k f) -> p k f", k=nk)
        P.add("pool", lambda h, wt=wt, cg=cg: h.dma_start(out=wt, in_=wv[:, :, cg * 512:(cg + 1) * 512]),
              writes=(st.wbuf_b[s],), dma=True, key=f"wbuf{s}")
        for tt in range(st.NT):
            bk = 4 + st.rr % 4
            st.rr += 1
            for kc in range(nk):
                P.add("pe", lambda h, bk=bk, kc=kc, tt=tt, wt=wt: h.matmul(st.pbank[bk][:, :], lhsT=lhs_fn(kc, tt), rhs=wt[:, kc, :],
                                                                        start=(kc == 0), stop=(kc == nk - 1)),
                      reads=(st.wbuf_b[s],) + tuple(lhs_bufs), writes=(st.pbank_b[bk],))
            post(tt, cg, bk)


def emit_A(cx, x, y, wg, wu, wd, lng, lnb, identd, ntok=TOK, yb=None):
    st = RowStage(cx, ntok)
    load_const(cx, st.ident[:, :], st.ident_b, identd[:, :], "ident")
    st.load_ln(lng, lnb, 0)
    load_x_tiles(cx, st, x, 2.0 * ALPHA)
    ffn_stage(cx, st, wg, wu, wd)
    st.ln_all(0.5, store_out(cx, st, y, yb))


def emit_C(cx, x, oT, pp, wo, wg, wu, wd, wpe, wpg, lng, lnb, identd, y, nxt=None, ntok=TOK, yb=None, pre=None):
    P = cx.P
    with_next = nxt is not None
    odeps = tuple(pre()) if pre else ()
    st = RowStage(cx, ntok)
    load_const(cx, st.ident[:, :], st.ident_b, identd[:, :], "ident")
    st.load_ln(lng, lnb, 1)
    load_x_tiles(cx, st, x, ALPHA, with_xT=False)
    if callable(oT):
        for kc in range(16):
            sl = kc % 2
            P.add("sp", lambda h, kc=kc, sl=sl: h.dma_start(out=st.xin[sl][:, 0:ntok], in_=oT(h, kc)), writes=(st.xin_b[sl],), dma=True, key=f"xin{sl}")
            if kc % 2 == 0:
                P.add("act", lambda h, kc=kc, sl=sl: h.copy(out=st.xT[:, kc, :], in_=st.xin[sl][:, 0:ntok]), reads=(st.xin_b[sl],), writes=tuple(st.xT_b))
            else:
                P.add("dve", lambda h, kc=kc, sl=sl: h.tensor_copy(out=st.xT[:, kc, :], in_=st.xin[sl][:, 0:ntok]), reads=(st.xin_b[sl],), writes=tuple(st.xT_b))
    else:
        oTv = oT.rearrange("(kc p) t -> p kc t", p=128)
        P.add("pool", lambda h: h.dma_start(out=st.xT[:, :, :], in_=oTv), reads=odeps, writes=tuple(st.xT_b), dma=True, key="oT")

    def post_add(tt, cg, bk):
        xa = st.xacc[:, tt, cg * 512:(cg + 1) * 512]
        P.add("dve", lambda h: h.tensor_tensor(out=xa, in0=xa, in1=st.pbank[bk][:, :], op=ALU.add),
              reads=(st.pbank_b[bk], st.xacc_cb[tt][cg]), writes=(st.xacc_cb[tt][cg],))

    dense_acc(cx, st, wo, lambda kc, tt: st.xT[:, kc, tt * 128:(tt + 1) * 128], 16, st.xT_b, post_add)
    st.ln_all(1.0)
    st.load_ln(lng, lnb, 2)
    refresh_xT(cx, st, 2.0 * ALPHA)
    ffn_stage(cx, st, wg, wu, wd)
    st.ln_all(0.5)
    st.load_ln(lng, lnb, 3)
    refresh_xT(cx, st, 1.0)
    pT = cx.sb([128, 2, ntok], BF16, "pT")
    pT_b = Buf("pT")
    pin = cx.sb([128, 256], F32, "pin")
    pin_b = Buf("pin")
    for tt in range(st.NT):
        P.add("sp", lambda h, tt=tt: h.dma_start(out=pin[:, :], in_=pp[tt * 128:(tt + 1) * 128, :]), writes=(pin_b,), dma=True, key="pin")
        bk = 4 + st.rr % 4
        st.rr += 1
        for j in range(2):
            P.add("pe", lambda h, j=j, bk=bk: h.transpose(out=st.pbank[bk][:, j * 128:(j + 1) * 128], in_=pin[:, j * 128:(j + 1) * 128], identity=st.ident[:, :]),
                  reads=(pin_b, st.ident_b), writes=(st.pbank_b[bk],))
        P.add("act", lambda h, tt=tt, bk=bk: h.copy(out=pT[:, :, tt * 128:(tt + 1) * 128], in_=st.pbank[bk][:, 0:256].rearrange("p (j t) -> p j t", j=2)),
              reads=(st.pbank_b[bk],), writes=(pT_b,))
    wpet = cx.sb([128, 2, D], BF16, "wpe")
    wpe_b = Buf("wpe")
    P.add("pool", lambda h: h.dma_start(out=wpet[:, :, :], in_=wpe.rearrange("(c p) n -> p c n", p=128)), writes=(wpe_b,), dma=True, key="wpe")
    et, et_b = st.acc_tmp, st.acc_tmp_b
    ei = [0]

    def post_ple(tt, cg, bk):
        be = ei[0] % 2
        s = ei[0] % 2
        ei[0] += 1
        for c in range(2):
            P.add("pe", lambda h, c=c: h.matmul(st.pbank[be][:, :], lhsT=pT[:, c, tt * 128:(tt + 1) * 128], rhs=wpet[:, c, cg * 512:(cg + 1) * 512],
                                               start=(c == 0), stop=(c == 1)),
                  reads=(pT_b, wpe_b), writes=(st.pbank_b[be],))
        P.add("act", lambda h: h.activation(out=st.sg[s][:, :], in_=st.pbank[bk][:, :], func=AF.Sigmoid),
              reads=(st.pbank_b[bk],), writes=(st.sg_b[s],))
        P.add("dve", lambda h: h.tensor_tensor(out=et[s][:, :], in0=st.sg[s][:, :], in1=st.pbank[be][:, :], op=ALU.mult),
              reads=(st.sg_b[s], st.pbank_b[be]), writes=(et_b[s],))
        xa = st.xacc[:, tt, cg * 512:(cg + 1) * 512]
        P.add("dve", lambda h: h.scalar_tensor_tensor(out=xa, in0=xa, scalar=ALPHA, in1=et[s][:, :], op0=ALU.mult, op1=ALU.add),
              reads=(et_b[s], st.xacc_cb[tt][cg]), writes=(st.xacc_cb[tt][cg],))

    dense_acc(cx, st, wpg, lambda kc, tt: st.xT[:, kc, tt * 128:(tt + 1) * 128], 16, st.xT_b, post_ple)
    if not with_next:
        st.ln_all(1.0, store_out(cx, st, y))
        return
    wg2, wu2, wd2, lng2, lnb2 = nxt
    st.ln_all(1.0)
    st.load_ln(lng2, lnb2, 0)
    refresh_xT(cx, st, 2.0 * ALPHA)
    ffn_stage(cx, st, wg2, wu2, wd2)
    st.ln_all(0.5, store_out(cx, st, y, yb))


NCOL = 3088
PW = 400


def mix_consts():
    c = {}
    c["ident"] = np.eye(128, dtype=np.float32)
    rc = np.zeros((128, 8), np.float32)
    p = np.arange(128)
    rc[:, 0] = 10000.0 ** (-(2.0 * (p % 32)) / 64.0)
    rc[:, 1] = 10000.0 ** (-(2.0 * (p % 64)) / 128.0)
    rc[:, 2] = np.where((p % 64) < 32, -2.0, 2.0)
    rc[:, 3] = np.where(p < 64, -2.0, 2.0)
    rc[:, 6] = math.pi / 2.0
    rc[:, 4] = -math.pi
    rc[:, 5] = EPS
    c["rc"] = rc
    perm = np.zeros((2, 128, 128), np.float32)
    for m in range(128):
        perm[0, (m // 64) * 64 + ((m % 64) + 32) % 64, m] = 1.0
        perm[1, (m + 64) % 128, m] = 1.0
    c["perm"] = perm
    k = np.arange(128)[:, None]
    q = np.arange(512)[None, :]
    c["cmask"] = (np.arange(128)[:, None] <= np.arange(896)[None, :] - 384).astype(np.float32)
    tri = (np.arange(128)[:, None] <= np.arange(128)[None, :]).astype(np.float32)
    c["gcst"] = np.stack([tri, 1.0 - tri, np.ones((128, 128), np.float32)])
    own = np.arange(8)[:, None]
    n = np.arange(8)[None, :]
    c["nmask"] = np.where(n < own, 0.0, -1e30).astype(np.float32).reshape(1, 64)
    c["pmask"] = (n < own).astype(np.float32).reshape(1, 64)
    es = np.zeros((8, 8, 128), np.float32)
    for j in range(8):
        es[j, j, :] = 1.0
    c["esel"] = es
    return c


def emit_mix(cx, A):
    P = cx.P
    S = SEQ
    NT = S // 128
    x, pos, wsel2, dlam, lamc, dng, ggu2, ggb2, gng = (A[k] for k in ("x", "pos", "wsel2", "dlam", "lamc", "dng", "ggu2", "ggb2", "gng"))
    d_ident, d_rc, d_perm, d_cmask, d_gcst, d_nmask, d_pmask, d_esel = (A[k] for k in ("ident", "rc", "perm", "cmask", "gcst", "nmask", "pmask", "esel"))
    out = A["out"]
    phases = ("diff", "gla", "moba")
    cur = {"hh": 0}
    xdeps = tuple(A["pre"]()) if A.get("pre") else ()
    HH = tuple(range(A.get("nhh", 2)))
    xsrc = A.get("x_dyn") or (lambda h, tt: x[tt * 128:(tt + 1) * 128, :])
    row_of = A.get("row_of") or (lambda kind, hh, j: {"d": (3 * hh + j) * 128, "g": 768 + (2 * hh + j) * 128, "m": 1280 + (3 * hh + j) * 128}[kind])

    def V(eng, fn, r=(), w=()):
        return P.add(eng, fn, reads=r, writes=w)

    def LD(dst, src, name, cast=False, n=1):
        b = Buf(name)
        P.add("pool" if cast else "sp", lambda h: h.dma_start(out=dst, in_=src), writes=(b,), dma=True, key=name)
        return b

    ident = cx.sb([128, 128], F32, "ident")
    ident_b = LD(ident[:, :], d_ident[:, :], "ident")
    rc = cx.sb([128, 8], F32, "rc")
    rc_b = LD(rc[:, :], d_rc[:, :], "rc")
    perm = cx.sb([128, 2, 128], BF16, "perm")
    perm_b = LD(perm[:, :, :], d_perm.rearrange("a p m -> p a m"), "perm", cast=True)
    cmask = cx.sb([128, 896], BF16, "cmask")
    cmask_b = LD(cmask[:, :], d_cmask[:, :], "cmask", cast=True)
    gcst = cx.sb([128, 3, 128], F32, "gcst")
    gcst_b = LD(gcst[:, :, :], d_gcst.rearrange("a p m -> p a m"), "gcst")
    ones_bf = cx.sb([128, 128], BF16, "ones_bf")
    onesbf_b = LD(ones_bf[:, :], d_gcst[2, :, :], "ones_bf", cast=True)
    tri_bf = cmask[:, 384:512]
    nmask = cx.sb([128, 64], F32, "nmask")
    nmask_b = LD(nmask[:, :], d_nmask[0:1, :].partition_broadcast(128), "nmask")
    pmask = cx.sb([128, 64], F32, "pmask")
    pmask_b = LD(pmask[:, :], d_pmask[0:1, :].partition_broadcast(128), "pmask")
    esel = cx.sb([8, 8, 128], BF16, "esel")
    esel_b = LD(esel[:, :, :], d_esel[:, :, :], "esel", cast=True)
    dl = cx.sb([128, 256], F32, "dl")
    dl_b = LD(dl[:, :], dlam[0:1, :].partition_broadcast(128), "dl")
    lc = cx.sb([128, 2], F32, "lc")
    lc_b = LD(lc[:, :], lamc[0:1, :].partition_broadcast(128), "lc")
    dngt = cx.sb([128, 1], F32, "dngt")
    dng_b = LD(dngt[:, :], dng[:, :], "dng")
    gngt = cx.sb([128, 1], F32, "gngt")
    gng_b = LD(gngt[:, :], gng[:, :], "gng")
    ggut_l = [cx.sb([16, 128], F32, f"ggut{k}") for k in HH]
    ggu_bl = [LD(ggut_l[k][:, :], ggu2[k, :, :], f"ggu{k}") for k in HH]
    ggbt_l = [cx.sb([128, 128], F32, f"ggbt{k}") for k in HH]
    ggb_bl = [LD(ggbt_l[k][:, :], ggb2[k, 0:1, :].partition_broadcast(128), f"ggb{k}") for k in HH]

    pbank = [cx.ps([128, 512], F32, f"bank{i}") for i in range(8)]
    pb_b = [Buf(f"bank{i}", excl=True) for i in range(8)]
    rr = {"s": 0, "a": 0, "g": 0}

    def sbank():
        rr["s"] += 1
        return rr["s"] % 2

    def abank():
        rr["a"] += 1
        return 2 + rr["a"] % 2

    def gbank():
        rr["g"] += 1
        return rr["g"] % 8

    xT = cx.sb([128, 16, S], BF16, "xT")
    xT_b = [Buf(f"xT{t}") for t in range(NT)]
    stg = [cx.sb([128, S], F32, f"stg{s}") for s in range(2)]
    stg_b = [Buf(f"stg{s}") for s in range(2)]
    k4 = 0
    for tt in range(NT):
        s = tt % 2
        P.add("pool" if A.get("x_cast") else "sp", lambda h, s=s, tt=tt: h.dma_start(out=stg[s][:, :], in_=xsrc(h, tt)),
              reads=xdeps, writes=(stg_b[s],), dma=True, key=f"stg{s}")
        for g in range(4):
            bk = k4 % 4
            k4 += 1
            for j in range(4):
                kc = g * 4 + j
                V("pe", lambda h, bk=bk, j=j, kc=kc, s=s: h.transpose(out=pbank[bk][:, j * 128:(j + 1) * 128], in_=stg[s][:, kc * 128:(kc + 1) * 128], identity=ident[:, :]),
                  (stg_b[s], ident_b), (pb_b[bk],))
            o = xT[:, g * 4:(g + 1) * 4, tt * 128:(tt + 1) * 128]
            i = pbank[bk][:, :].rearrange("p (j t) -> p j t", j=4)
            if g % 2 == 0:
                V("act", lambda h, o=o, i=i: h.copy(out=o, in_=i), (pb_b[bk],), (xT_b[tt],))
            else:
                V("dve", lambda h, o=o, i=i: h.tensor_copy(out=o, in_=i), (pb_b[bk],), (xT_b[tt],))
    allxT = tuple(xT_b)

    wp = [cx.sb([128, 16, PW], BF16, f"wp{s}") for s in range(2)]
    wp_b = [Buf(f"wp{s}") for s in range(2)]
    wvs = [wsel2[k].rearrange("(kc p) f -> p kc f", p=128) for k in HH]
    wi = [0]

    def load_piece(c0, n):
        s = wi[0] % 2
        wi[0] += 1
        wv = wvs[cur["hh"]]
        P.add("pool", lambda h: h.dma_start(out=wp[s][:, :, 0:n], in_=wv[:, :, c0:c0 + n]), writes=(wp_b[s],), dma=True, key=f"wp{s}")
        return s

    def proj_fm(bk, s, c0, m, tg):
        for kc in range(16):
            V("pe", lambda h, kc=kc: h.matmul(pbank[bk][0:m, :], lhsT=wp[s][:, kc, c0:c0 + m], rhs=xT[:, kc, tg * 512:(tg + 1) * 512],
                                             start=(kc == 0), stop=(kc == 15)), (wp_b[s],) + allxT, (pb_b[bk],))

    def proj_tm(bk, s, c0, n, tt, o0=0):
        for kc in range(16):
            V("pe", lambda h, kc=kc: h.matmul(pbank[bk][:, o0:o0 + n], lhsT=xT[:, kc, tt * 128:(tt + 1) * 128], rhs=wp[s][:, kc, c0:c0 + n],
                                             start=(kc == 0), stop=(kc == 15)), (wp_b[s],) + allxT, (pb_b[bk],))

    posf = cx.sb([128, S], F32, "posf")
    posf_b = Buf("posf")
    posi = cx.sb([128, 512], I32, "posi")
    posi_b = Buf("posi")
    for c4 in range(4):
        P.add("sp", lambda h, c4=c4: h.dma_start(out=posi[:, :], in_=pos[0:1, c4 * 512:(c4 + 1) * 512].partition_broadcast(128)), writes=(posi_b,), dma=True, key="posi")
        V("dve", lambda h, c4=c4: h.tensor_copy(out=posf[:, c4 * 512:(c4 + 1) * 512], in_=posi[:, :]), (posi_b,), (posf_b,))
    ctab = cx.sb([128, S], F32, "ctab")
    stab = cx.sb([128, S], F32, "stab")
    tab_b = Buf("tab")

    def make_tables(kind):
        tb = (tab_b, stg_b[0], stg_b[1], posi_b)
        V("dve", lambda h: h.tensor_scalar(out=stg[0][:, :], in0=posf[:, :], scalar1=rc[:, kind:kind + 1], scalar2=None, op0=ALU.mult), (posf_b, rc_b), (stg_b[0],))
        for c4 in range(4):
            cs = slice(c4 * 512, (c4 + 1) * 512)
            V("dve", lambda h, cs=cs: h.tensor_scalar(out=posi[:, :], in0=stg[0][:, cs], scalar1=1.0 / (2.0 * math.pi), scalar2=None, op0=ALU.mult), (stg_b[0],), (posi_b,))
            V("dve", lambda h, cs=cs: h.tensor_copy(out=stg[1][:, cs], in_=posi[:, :]), (posi_b, stg_b[1]), (stg_b[1],))
        V("dve", lambda h: h.scalar_tensor_tensor(out=stg[0][:, :], in0=stg[1][:, :], scalar=-2.0 * math.pi, in1=stg[0][:, :], op0=ALU.mult, op1=ALU.add), (stg_b[0], stg_b[1]), (stg_b[0],))
        V("act", lambda h: h.activation(out=stg[1][:, :], in_=stg[0][:, :], func=AF.Sin, scale=0.5), (stg_b[0],), (stg_b[1],))
        V("act", lambda h: h.activation(out=ctab[:, :], in_=stg[0][:, :], func=AF.Sin, bias=rc[:, 6:7], scale=-0.5), (stg_b[0], rc_b), (tab_b,))
        V("dve", lambda h: h.scalar_tensor_tensor(out=stab[:, :], in0=stg[1][:, :], scalar=rc[:, 2 + kind:3 + kind], in1=ctab[:, :], op0=ALU.mult, op1=ALU.mult), (stg_b[1], rc_b, tab_b), (tab_b,))
        V("dve", lambda h: h.tensor_tensor(out=ctab[:, :], in0=stg[1][:, :], in1=stg[1][:, :], op=ALU.mult), (stg_b[1], tab_b), (tab_b,))
        V("dve", lambda h: h.tensor_scalar(out=ctab[:, :], in0=ctab[:, :], scalar1=-2.0, scalar2=1.0, op0=ALU.mult, op1=ALU.add), (tab_b,), (tab_b,))

    qT2 = [cx.sb([128, 2, S], BF16, f"qT{s}") for s in range(2)]
    qT = [t[:, 0, :] for t in qT2]
    kT = [cx.sb([128, S], BF16, f"kT{s}") for s in range(2)]
    vv = [cx.sb([128, NT, 128], BF16, f"vv{s}") for s in range(2)]
    qT_b = [Buf(f"qT{s}") for s in range(2)]
    kT_b = [Buf(f"kT{s}") for s in range(2)]
    vv_b = [Buf(f"vv{s}") for s in range(2)]
    qb = [cx.sb([128, 512], BF16, f"qb{s}") for s in range(2)]
    qb_b = [Buf(f"qb{s}") for s in range(2)]
    t1 = [cx.sb([128, 512], F32, f"t1{s}") for s in range(2)]
    t1_b = [Buf(f"t1{s}") for s in range(2)]
    t2 = [cx.sb([128, 512], F32, f"t2{s}") for s in range(2)]
    t2_b = [Buf(f"t2{s}") for s in range(2)]
    NPT = 4
    pT = [cx.sb([128, 512], BF16, f"pT{s}") for s in range(NPT)]
    pT_b = [Buf(f"pT{s}") for s in range(NPT)]
    fa = cx.sb([128, 512], F32, "fa")
    fb = cx.sb([128, 512], F32, "fb")
    fc = cx.sb([128, 512], F32, "fc")
    fa_b, fb_b, fc_b = Buf("fa"), Buf("fb"), Buf("fc")
    kms = cx.sb([128, 8], F32, "kms")
    kms_b = Buf("kms")
    kmb = cx.sb([128, 8], BF16, "kmb")
    kmb_b = Buf("kmb")
    ri = [0]
    pi = [0]

    def rope_proj(s, c0, dst, dst_b, pk, want_kms=False, split=None):
        ctx = {}

        def ph1(tg):
            bk = sbank()
            proj_fm(bk, s, c0, 128, tg)
            i = ri[0] % 2
            ri[0] += 1
            ctx[tg] = i
            cols = slice(tg * 512, (tg + 1) * 512)
            V("act", lambda h: h.copy(out=qb[i][:, :], in_=pbank[bk][:, :]), (pb_b[bk],), (qb_b[i],))
            V("dve", lambda h: h.tensor_tensor(out=t1[i][:, :], in0=pbank[bk][:, :], in1=ctab[:, cols], op=ALU.mult), (pb_b[bk], tab_b), (t1_b[i],))

        def ph2(tg):
            i = ctx[tg]
            cols = slice(tg * 512, (tg + 1) * 512)
            b2 = abank()
            V("pe", lambda h: h.matmul(pbank[b2][:, :], lhsT=perm[:, pk, :], rhs=qb[i][:, :], start=True, stop=True), (perm_b, qb_b[i]), (pb_b[b2],))
            V("dve", lambda h: h.tensor_tensor(out=t2[i][:, :], in0=pbank[b2][:, :], in1=stab[:, cols], op=ALU.mult), (pb_b[b2], tab_b), (t2_b[i],))
            V("pool", lambda h: h.tensor_tensor(out=t1[i][:, :], in0=t1[i][:, :], in1=t2[i][:, :], op=ALU.add), (t1_b[i], t2_b[i]), (t1_b[i],))
            if split is None:
                V("act", lambda h: h.copy(out=dst[:, cols], in_=t1[i][:, :]), (t1_b[i],), (dst_b,))
            else:
                for m in range(2):
                    V("act", lambda h, m=m: h.copy(out=split[m * 64:(m + 1) * 64, m, cols], in_=t1[i][m * 64:(m + 1) * 64, :]), (t1_b[i],), (dst_b,))
            if want_kms:
                V("dve", lambda h: h.tensor_reduce(out=kms[:, 2 * tg:2 * tg + 2], in_=t1[i][:, :].rearrange("p (b k) -> p b k", b=2), axis=AX.X, op=ALU.add),
                  (t1_b[i],), (kms_b,))

        pipeline(list(range(4)), ph1, ph2, lag=1)

    def v_proj(s, c0, dst, dst_b):
        for tt in range(NT):
            bk = sbank()
            proj_tm(bk, s, c0, 128, tt)
            if tt % 2 == 0:
                V("act", lambda h, bk=bk, tt=tt: h.copy(out=dst[:, tt, :], in_=pbank[bk][:, 0:128]), (pb_b[bk],), (dst_b,))
            else:
                V("dve", lambda h, bk=bk, tt=tt: h.tensor_copy(out=dst[:, tt, :], in_=pbank[bk][:, 0:128]), (pb_b[bk],), (dst_b,))

    def store_head(kind, hh, j, s):
        r0 = row_of(kind, hh, j)
        if A.get("out_split"):
            P.add("pool" if A.get("out_cast") else "sp", lambda h: [h.dma_start(out=out[t, r0:r0 + 128, :], in_=stg[s][:, t * (S // 2):(t + 1) * (S // 2)]) for t in range(2)],
                  reads=(stg_b[s],), dma=True, key=f"so{s}", ndma=2)
        else:
            P.add("sp", lambda h: h.dma_start(out=out[r0:r0 + 128, :], in_=stg[s][:, :]), reads=(stg_b[s],), dma=True, key=f"so{s}")

    def rms_over_partitions(src, src_b, n):
        V("act", lambda h: h.activation(out=fc[:, 0:n], in_=src, func=AF.Square), (src_b,), (fc_b,))
        b2 = abank()
        V("pe", lambda h: h.matmul(pbank[b2][:, 0:n], lhsT=gcst[:, 2, :], rhs=fc[:, 0:n], start=True, stop=True), (gcst_b, fc_b), (pb_b[b2],))
        V("act", lambda h: h.activation(out=fb[:, 0:n], in_=pbank[b2][:, 0:n], func=AF.Sqrt, bias=rc[:, 5:6], scale=1.0 / 128.0), (pb_b[b2], rc_b), (fb_b,))
        V("dve", lambda h: h.reciprocal(out=fb[:, 0:n], in_=fb[:, 0:n]), (fb_b,), (fb_b,))

    hs = [0]
    so = [0]

    if "diff" in phases:
        make_tables(0)
        lam = cx.sb([128, 4], F32, "lam")
        lam_b = Buf("lam")
        V("dve", lambda h: h.tensor_tensor(out=fa[:, 0:64], in0=dl[:, 0:64], in1=dl[:, 64:128], op=ALU.mult), (dl_b,), (fa_b,))
        V("dve", lambda h: h.tensor_tensor(out=fa[:, 64:128], in0=dl[:, 128:192], in1=dl[:, 192:256], op=ALU.mult), (dl_b, fa_b), (fa_b,))
        V("dve", lambda h: h.tensor_reduce(out=lam[:, 0:2], in_=fa[:, 0:128].rearrange("p (a k) -> p a k", a=2), axis=AX.X, op=ALU.add), (fa_b,), (lam_b,))
        V("act", lambda h: h.activation(out=lam[:, 0:2], in_=lam[:, 0:2], func=AF.Exp), (lam_b,), (lam_b,))
        V("dve", lambda h: h.tensor_tensor(out=lam[:, 2:3], in0=lam[:, 1:2], in1=lam[:, 0:1], op=ALU.subtract), (lam_b,), (lam_b,))
        V("dve", lambda h: h.tensor_tensor(out=lam[:, 2:3], in0=lam[:, 2:3], in1=lc[:, 0:1], op=ALU.subtract), (lam_b, lc_b), (lam_b,))
        V("dve", lambda h: h.tensor_tensor(out=lam[:, 3:4], in0=dngt[:, 0:1], in1=lc[:, 1:2], op=ALU.mult), (dng_b, lc_b, lam_b), (lam_b,))
        for k2 in range(2):
            V("pool", lambda h, k2=k2: h.memset(qT2[k2][:, :, :], 0.0), (), (qT_b[k2],))
        for hh, hd in [(u, v) for u in HH for v in range(3)]:
            cur["hh"] = hh
            s = load_piece(hd * 384, 384)
            q = hs[0] % 2
            hs[0] += 1
            rope_proj(s, 0, None, qT_b[q], 0, split=qT2[q])
            rope_proj(s, 128, kT[q], kT_b[q], 0)
            v_proj(s, 256, vv[q], vv_b[q])
            so_s = so[0] % 2
            so[0] += 1
            pend = []
            for qg in range(4):
                nkc = 4 * qg + 4
                units = [(kc, m) for kc in range(nkc) for m in range(2)]
                ctx = {}

                def ph1(u):
                    kc, m = u
                    bk = sbank()
                    V("pe", lambda h: h.matmul(pbank[bk][:, :], lhsT=kT[q][:, kc * 128:(kc + 1) * 128],
                                               rhs=qT2[q][:, m, qg * 512:(qg + 1) * 512], start=True, stop=True),
                      (kT_b[q], qT_b[q]), (pb_b[bk],))
                    pj = pi[0] % NPT
                    pi[0] += 1
                    ctx[u] = pj
                    V("act", lambda h: h.activation(out=pT[pj][:, :], in_=pbank[bk][:, :], func=AF.Exp, scale=0.125), (pb_b[bk],), (pT_b[pj],))
                    if kc >= 4 * qg:
                        j = kc - 4 * qg
                        V("pool", lambda h: h.tensor_tensor(out=pT[pj][:, :], in0=pT[pj][:, :], in1=cmask[:, (3 - j) * 128:(3 - j) * 128 + 512], op=ALU.mult), (pT_b[pj], cmask_b), (pT_b[pj],))

                def ph2(u):
                    kc, m = u
                    pj = ctx[u]
                    V("pe", lambda h: h.matmul(pbank[4 + m][:, :], lhsT=vv[q][:, kc, :], rhs=pT[pj][:, :], start=(kc == 0), stop=(kc == nkc - 1)),
                      (vv_b[q], pT_b[pj]), (pb_b[4 + m],))
                    V("pe", lambda h: h.matmul(pbank[6 + m][:, :], lhsT=ones_bf[:, :], rhs=pT[pj][:, :], start=(kc == 0), stop=(kc == nkc - 1)),
                      (onesbf_b, pT_b[pj]), (pb_b[6 + m],))

                pipeline(units, ph1, ph2, before_ph2=pend.pop() if pend else None)
                def finalize(qg=qg, so_s=so_s):
                    V("dve", lambda h: h.reciprocal(out=fb[:, :], in_=pbank[6][:, :]), (pb_b[6],), (fb_b,))
                    V("dve", lambda h: h.tensor_tensor(out=fa[:, :], in0=pbank[4][:, :], in1=fb[:, :], op=ALU.mult), (pb_b[4], fb_b), (fa_b,))
                    V("dve", lambda h: h.reciprocal(out=fb[:, :], in_=pbank[7][:, :]), (pb_b[7], fb_b), (fb_b,))
                    V("dve", lambda h: h.tensor_tensor(out=fc[:, :], in0=pbank[5][:, :], in1=fb[:, :], op=ALU.mult), (pb_b[5], fb_b), (fc_b,))
                    V("dve", lambda h: h.scalar_tensor_tensor(out=fa[:, :], in0=fc[:, :], scalar=lam[:, 2:3], in1=fa[:, :], op0=ALU.mult, op1=ALU.add), (fc_b, fa_b, lam_b), (fa_b,))
                    rms_over_partitions(fa[:, :], fa_b, 512)
                    V("dve", lambda h, qg=qg: h.scalar_tensor_tensor(out=stg[so_s][:, qg * 512:(qg + 1) * 512], in0=fa[:, :], scalar=lam[:, 3:4], in1=fb[:, :], op0=ALU.mult, op1=ALU.mult),
                      (fa_b, fb_b, lam_b), (stg_b[so_s],))
                pend.append(finalize)
            pend.pop()()
            store_head("d", hh, hd, so_s)

    if "moba" in phases:
        make_tables(1)
        selT = cx.sb([8, S], BF16, "selT")
        selT_b = Buf("selT")
        gm = cx.sb([128, 8], F32, "gm")
        top8 = cx.sb([128, 8], F32, "top8")
        sel = cx.sb([128, 8], F32, "sel")
        gm_b = Buf("gm")
        SC = 128.0 ** -0.5
        for hh, hd in [(u, v) for u in HH for v in range(3)]:
            cur["hh"] = hh
            s = load_piece(1936 + hd * 384, 384)
            q = hs[0] % 2
            hs[0] += 1
            rope_proj(s, 0, qT[q], qT_b[q], 1)
            rope_proj(s, 128, kT[q], kT_b[q], 1, want_kms=True)
            v_proj(s, 256, vv[q], vv_b[q])
            V("act", lambda h: h.mul(out=kmb[:, :], in_=kms[:, :], mul=1.0 / 256.0), (kms_b,), (kmb_b,))
            for tt in range(NT):
                own = tt // 2
                b2 = abank()
                V("pe", lambda h, b2=b2, tt=tt: h.matmul(pbank[b2][:, 0:8], lhsT=qT[q][:, tt * 128:(tt + 1) * 128], rhs=kmb[:, :], start=True, stop=True),
                  (qT_b[q], kmb_b), (pb_b[b2],))
                V("dve", lambda h, b2=b2, own=own: h.tensor_tensor(out=gm[:, :], in0=pbank[b2][:, 0:8], in1=nmask[:, own * 8:(own + 1) * 8], op=ALU.add), (pb_b[b2], nmask_b), (gm_b,))
                V("dve", lambda h: h.max(out=top8[:, :], in_=gm[:, :]), (gm_b,), (gm_b,))
                V("dve", lambda h, own=own: h.scalar_tensor_tensor(out=sel[:, :], in0=gm[:, :], scalar=top8[:, 2:3], in1=pmask[:, own * 8:(own + 1) * 8], op0=ALU.is_ge, op1=ALU.mult),
                  (gm_b, pmask_b), (gm_b,))
                b3 = abank()
                V("pe", lambda h, b3=b3: h.transpose(out=pbank[b3][0:8, 0:128], in_=sel[:, :], identity=ident[:, :]), (gm_b, ident_b), (pb_b[b3],))
                V("act", lambda h, b3=b3, tt=tt: h.copy(out=selT[0:8, tt * 128:(tt + 1) * 128], in_=pbank[b3][0:8, 0:128]), (pb_b[b3],), (selT_b,))
            so_s = so[0] % 2
            so[0] += 1
            for qblk in range(8):
                ab = 4 + 2 * (qblk % 2)
                zb = ab + 1
                nkc = 2 * qblk + 2
                qc = slice(qblk * 256, (qblk + 1) * 256)
                ctx = {}
                mbs = {}

                def ph1(kc):
                    n = kc // 2
                    bk = sbank()
                    V("pe", lambda h: h.matmul(pbank[bk][:, 0:256], lhsT=kT[q][:, kc * 128:(kc + 1) * 128], rhs=qT[q][:, qc], start=True, stop=True),
                      (kT_b[q], qT_b[q]), (pb_b[bk],))
                    pj = pi[0] % NPT
                    pi[0] += 1
                    ctx[kc] = pj
                    if n < qblk and kc % 2 == 0:
                        mb = abank()
                        mbs[n] = mb
                        V("pe", lambda h: h.matmul(pbank[mb][:, 0:256], lhsT=esel[0:8, n, :], rhs=selT[0:8, qc], start=True, stop=True),
                          (esel_b, selT_b), (pb_b[mb],))
                    V("act", lambda h: h.activation(out=pT[pj][:, 0:256], in_=pbank[bk][:, 0:256], func=AF.Exp, scale=SC), (pb_b[bk],), (pT_b[pj],))
                    if n < qblk:
                        mb = mbs[n]
                        V("dve", lambda h: h.tensor_tensor(out=pT[pj][:, 0:256], in0=pT[pj][:, 0:256], in1=pbank[mb][:, 0:256], op=ALU.mult), (pT_b[pj], pb_b[mb]), (pT_b[pj],))
                    else:
                        j = kc % 2
                        V("pool", lambda h: h.tensor_tensor(out=pT[pj][:, 0:256], in0=pT[pj][:, 0:256], in1=cmask[:, (3 - j) * 128:(3 - j) * 128 + 256], op=ALU.mult), (pT_b[pj], cmask_b), (pT_b[pj],))

                def ph2(kc):
                    pj = ctx[kc]
                    V("pe", lambda h: h.matmul(pbank[ab][:, 0:256], lhsT=vv[q][:, kc, :], rhs=pT[pj][:, 0:256], start=(kc == 0), stop=(kc == nkc - 1)),
                      (vv_b[q], pT_b[pj]), (pb_b[ab],))
                    V("pe", lambda h: h.matmul(pbank[zb][:, 0:256], lhsT=ones_bf[:, :], rhs=pT[pj][:, 0:256], start=(kc == 0), stop=(kc == nkc - 1)),
                      (onesbf_b, pT_b[pj]), (pb_b[zb],))

                pipeline(list(range(nkc)), ph1, ph2)
                V("dve", lambda h, zb=zb: h.reciprocal(out=fb[:, 0:256], in_=pbank[zb][:, 0:256]), (pb_b[zb],), (fb_b,))
                V("dve", lambda h, ab=ab, qc=qc: h.tensor_tensor(out=stg[so_s][:, qc], in0=pbank[ab][:, 0:256], in1=fb[:, 0:256], op=ALU.mult), (pb_b[ab], fb_b), (stg_b[so_s],))
            store_head("m", hh, hd, so_s)

    if "gla" in phases:
        gq, gk = t2[0], t2[1]
        ggT = cx.sb([16, 512], F32, "ggT")
        grs = cx.sb([128, 2, 512], BF16, "grs")
        gq_b, gk_b, ggT_b, grs_b = t2_b[0], t2_b[1], Buf("ggT"), Buf("grs")
        ktm = cx.sb([128, 128], F32, "ktm")
        gv = cx.sb([128, 256], BF16, "gv")
        la = cx.sb([128, 128], F32, "la")
        ktm_b, gv_b, la_b = Buf("ktm"), Buf("gv"), Buf("la")
        Eq = cx.sb([128, 128], F32, "Eq")
        Ek = cx.sb([128, 128], F32, "Ek")
        Er = cx.sb([128, 128], F32, "Er")
        Eq_b, Ek_b, Er_b = Buf("Eq"), Buf("Ek"), Buf("Er")
        qtl = cx.sb([128, 128], BF16, "qtl")
        kt2 = cx.sb([128, 2, 128], BF16, "kt2")
        khat = cx.sb([128, 128], BF16, "khat")
        attm = [cx.sb([128, 128], BF16, f"attm{i}") for i in range(2)]
        qtl_b, kt2_b, khat_b = Buf("qtl"), Buf("kt2"), Buf("khat")
        attm_b = [Buf(f"attm{i}") for i in range(2)]
        Sst = cx.sb([128, 128], F32, "Sst")
        Sb2 = cx.sb([128, 2, 128], BF16, "Sb2")
        Sst_b, Sb2_b = Buf("Sst"), Buf("Sb2")
        og_b = [t1_b[0], t1_b[1]]
        for hh in HH:
            cur["hh"] = hh
            sa = load_piece(1152, 400)
            sb_ = load_piece(1552, 384)
            V("pool", lambda h: h.memset(kt2[:, :, :], 0.0), (), (kt2_b,))
            V("pool", lambda h: h.memset(Sst[:, :], 0.0), (), (Sst_b,))
            V("pool", lambda h: h.memset(Sb2[:, :, :], 0.0), (), (Sb2_b,))
            so0 = so[0] % 2
            so1 = (so[0] + 1) % 2
            so[0] += 2
            sos = (so0, so1)
            for tg in range(4):
                bk = gbank()
                proj_fm(bk, sa, 0, 128, tg)
                V("act", lambda h, bk=bk: h.copy(out=gq[:, :], in_=pbank[bk][:, :]), (pb_b[bk],), (gq_b,))
                bk = gbank()
                proj_fm(bk, sa, 128, 128, tg)
                V("dve", lambda h, bk=bk: h.tensor_copy(out=gk[:, :], in_=pbank[bk][:, :]), (pb_b[bk],), (gk_b,))
                bk = gbank()
                proj_fm(bk, sa, 256, 16, tg)
                V("act", lambda h, bk=bk: h.copy(out=ggT[:, :], in_=pbank[bk][0:16, :]), (pb_b[bk],), (ggT_b,))
                for hd in range(2):
                    bk = gbank()
                    proj_fm(bk, sb_, 128 + hd * 128, 128, tg)
                    V("act", lambda h, bk=bk, hd=hd: h.activation(out=grs[:, hd, :], in_=pbank[bk][:, :], func=AF.Silu), (pb_b[bk],), (grs_b,))
                for ci in range(4):
                    tt = tg * 4 + ci
                    cc = slice(ci * 128, (ci + 1) * 128)
                    bk = gbank()
                    proj_tm(bk, sa, 128, 128, tt)
                    V("act", lambda h, bk=bk: h.copy(out=ktm[:, :], in_=pbank[bk][:, 0:128]), (pb_b[bk],), (ktm_b,))
                    bk = gbank()
                    proj_tm(bk, sa, 272, 128, tt, 0)
                    proj_tm(bk, sb_, 0, 128, tt, 128)
                    V("dve", lambda h, bk=bk: h.tensor_copy(out=gv[:, :], in_=pbank[bk][:, 0:256]), (pb_b[bk],), (gv_b,))
                    bk = gbank()
                    V("pe", lambda h, bk=bk, cc=cc: h.matmul(pbank[bk][:, 0:128], lhsT=ggT[0:16, cc], rhs=ggut_l[hh][0:16, :], start=True, stop=True), (ggT_b, ggu_bl[hh]), (pb_b[bk],))
                    V("dve", lambda h, bk=bk: h.tensor_tensor(out=la[:, :], in0=pbank[bk][:, 0:128], in1=ggbt_l[hh][:, :], op=ALU.add), (pb_b[bk], ggb_bl[hh]), (la_b,))
                    V("act", lambda h: h.activation(out=la[:, :], in_=la[:, :], func=AF.Sigmoid), (la_b,), (la_b,))
                    V("act", lambda h: h.activation(out=la[:, :], in_=la[:, :], func=AF.Ln), (la_b,), (la_b,))
                    V("dve", lambda h: h.tensor_scalar(out=la[:, :], in0=la[:, :], scalar1=1.0 / 16.0, scalar2=None, op0=ALU.mult), (la_b,), (la_b,))
                    bc = gbank()
                    V("pe", lambda h, bc=bc: h.matmul(pbank[bc][:, 0:128], lhsT=la[:, :], rhs=gcst[:, 0, :], start=True, stop=True), (la_b, gcst_b), (pb_b[bc],))
                    V("act", lambda h, bc=bc: h.activation(out=Eq[:, :], in_=pbank[bc][:, 0:128], func=AF.Exp), (pb_b[bc],), (Eq_b,))
                    V("act", lambda h, bc=bc: h.activation(out=Ek[:, :], in_=pbank[bc][:, 0:128], func=AF.Exp, scale=-1.0), (pb_b[bc],), (Ek_b,))
                    br = gbank()
                    V("pe", lambda h, br=br: h.matmul(pbank[br][:, 0:128], lhsT=gcst[:, 1, :], rhs=la[:, :], start=True, stop=True), (la_b, gcst_b), (pb_b[br],))
                    V("act", lambda h, br=br: h.activation(out=Er[:, :], in_=pbank[br][:, 0:128], func=AF.Exp), (pb_b[br],), (Er_b,))
                    V("dve", lambda h, cc=cc: h.scalar_tensor_tensor(out=qtl[:, :], in0=gq[:, cc], scalar=0.125, in1=Eq[:, :], op0=ALU.mult, op1=ALU.mult), (gq_b, Eq_b), (qtl_b,))
                    for hd in range(2):
                        ps_ = slice(hd * 64, (hd + 1) * 64)
                        V("pool", lambda h, hd=hd, ps_=ps_, cc=cc: h.tensor_tensor(out=kt2[ps_, hd, :], in0=gk[ps_, cc], in1=Ek[ps_, :], op=ALU.mult), (gk_b, Ek_b), (kt2_b,))
                    V("pool", lambda h: h.tensor_tensor(out=khat[:, :], in0=ktm[:, :], in1=Er[:, :], op=ALU.mult), (ktm_b, Er_b), (khat_b,))
                    for hd in range(2):
                        bt = gbank()
                        V("pe", lambda h, bt=bt, hd=hd: h.matmul(pbank[bt][:, 0:128], lhsT=kt2[:, hd, :], rhs=qtl[:, :], start=True, stop=True), (kt2_b, qtl_b), (pb_b[bt],))
                        V("dve", lambda h, bt=bt, hd=hd: h.tensor_tensor(out=attm[hd][:, :], in0=pbank[bt][:, 0:128], in1=tri_bf, op=ALU.mult), (pb_b[bt], cmask_b), (attm_b[hd],))
                        bo = gbank()
                        V("pe", lambda h, bo=bo, hd=hd: h.matmul(pbank[bo][:, 0:128], lhsT=gv[:, hd * 128:(hd + 1) * 128], rhs=attm[hd][:, :], start=True, stop=False), (gv_b, attm_b[hd]), (pb_b[bo],))
                        V("pe", lambda h, bo=bo, hd=hd: h.matmul(pbank[bo][:, 0:128], lhsT=Sb2[:, hd, :], rhs=qtl[:, :], start=False, stop=True), (Sb2_b, qtl_b), (pb_b[bo],))
                        V("act", lambda h, bo=bo, hd=hd, cc=cc: h.copy(out=t1[hd][:, cc], in_=pbank[bo][:, 0:128]), (pb_b[bo],), (og_b[hd],))
                    bkv = gbank()
                    V("pe", lambda h, bkv=bkv: h.matmul(pbank[bkv][:, 0:256], lhsT=khat[:, :], rhs=gv[:, :], start=True, stop=True), (khat_b, gv_b), (pb_b[bkv],))
                    for hd in range(2):
                        ps_ = slice(hd * 64, (hd + 1) * 64)
                        V("dve", lambda h, hd=hd, ps_=ps_, bkv=bkv: h.scalar_tensor_tensor(out=Sst[ps_, :], in0=Sst[ps_, :], scalar=Eq[ps_, 127:128], in1=pbank[bkv][ps_, hd * 128:(hd + 1) * 128],
                                                                                       op0=ALU.mult, op1=ALU.add), (Sst_b, Eq_b, pb_b[bkv]), (Sst_b,))
                        V("act", lambda h, hd=hd, ps_=ps_: h.copy(out=Sb2[ps_, hd, :], in_=Sst[ps_, :]), (Sst_b,), (Sb2_b,))
                for hd in range(2):
                    rms_over_partitions(t1[hd][:, :], og_b[hd], 512)
                    V("dve", lambda h, hd=hd: h.scalar_tensor_tensor(out=fa[:, :], in0=t1[hd][:, :], scalar=gngt[:, 0:1], in1=fb[:, :], op0=ALU.mult, op1=ALU.mult),
                      (og_b[hd], fb_b, gng_b), (fa_b,))
                    V("pool", lambda h, hd=hd, tg=tg: h.tensor_tensor(out=stg[sos[hd]][:, tg * 512:(tg + 1) * 512], in0=fa[:, :], in1=grs[:, hd, :], op=ALU.mult),
                      (fa_b, grs_b), (stg_b[sos[hd]],))
            for hd in range(2):
                store_head("g", hh, hd, sos[hd])


_CACHE = {}

_IN_OFF = {"dq": 0, "dk": 768, "dv": 1536, "gq": 2304, "gk": 2560, "gv": 2816, "gr": 3328, "gg": 3840, "mq": 3856, "mk": 4624, "mv": 5392}


def _wsel_cols(hh):
    o = _IN_OFF
    cols = []
    for h in range(3 * hh, 3 * hh + 3):
        cols += list(range(o["dq"] + h * 128, o["dq"] + (h + 1) * 128))
        cols += list(range(o["dk"] + h * 128, o["dk"] + (h + 1) * 128))
        cols += list(range(o["dv"] + h * 128, o["dv"] + (h + 1) * 128))
    g0 = 2 * hh
    cols += list(range(o["gq"] + g0 * 64, o["gq"] + (g0 + 2) * 64))
    cols += list(range(o["gk"] + g0 * 64, o["gk"] + (g0 + 2) * 64))
    cols += list(range(o["gg"], o["gg"] + 16))
    cols += list(range(o["gv"] + g0 * 128, o["gv"] + (g0 + 1) * 128))
    cols += list(range(o["gv"] + (g0 + 1) * 128, o["gv"] + (g0 + 2) * 128))
    cols += list(range(o["gr"] + g0 * 128, o["gr"] + (g0 + 2) * 128))
    for h in range(3 * hh, 3 * hh + 3):
        cols += list(range(o["mq"] + h * 128, o["mq"] + (h + 1) * 128))
        cols += list(range(o["mk"] + h * 128, o["mk"] + (h + 1) * 128))
        cols += list(range(o["mv"] + h * 128, o["mv"] + (h + 1) * 128))
    assert len(cols) == NCOL
    return np.array(cols)


_CONST_SHAPES = {"ident": [128, 128], "rc": [128, 8], "perm": [2, 128, 128], "cmask": [128, 896], "gcst": [3, 128, 128],
                 "nmask": [1, 64], "pmask": [1, 64], "esel": [8, 8, 128]}


def build_fused():
    cx = Ctx()
    P = cx.P
    x = cx.din("x", [TOK, D])
    pos = cx.din("pos", [1, SEQ], I32)
    cst = {k: cx.din(k, shp) for k, shp in _CONST_SHAPES.items()}
    L = []
    for i in range(DEPTH):
        L.append({
            "p": cx.din(f"p{i}", [TOK, 256]), "wsel2": cx.din(f"wsel{i}", [1, D, NCOL]), "wo": cx.din(f"wo{i}", [D, D]),
            "f1g": cx.din(f"f1g{i}", [D, DFF]), "f1u": cx.din(f"f1u{i}", [D, DFF]), "f1d": cx.din(f"f1d{i}", [DFF, D]),
            "f2g": cx.din(f"f2g{i}", [D, DFF]), "f2u": cx.din(f"f2u{i}", [D, DFF]), "f2d": cx.din(f"f2d{i}", [DFF, D]),
            "wpe": cx.din(f"wpe{i}", [256, D]), "wpg": cx.din(f"wpg{i}", [D, D]),
            "lng": cx.din(f"lng{i}", [4, D]), "lnb": cx.din(f"lnb{i}", [4, D]),
            "dlam": cx.din(f"dlam{i}", [1, 256]), "lamc": cx.din(f"lamc{i}", [1, 2]), "dng": cx.din(f"dng{i}", [128, 1]),
            "ggu2": cx.din(f"ggu{i}", [1, 16, 128]), "ggb2": cx.din(f"ggb{i}", [1, 1, 128]), "gng": cx.din(f"gng{i}", [128, 1]),
        })
    y = cx.dout("y", [TOK, D])
    HB = D // 2
    x1loc = cx.dscratch("x1loc", [TOK, D])
    x1b = cx.dscratch("x1b", [TOK, D], BF16)
    xg = cx.dscratch("xg", [4, TOK, D], BF16)
    xsel = cx.dscratch("xsel", [SEQ, D], BF16)
    oTloc = cx.dscratch("oTloc", [2, HB, TOK], BF16)
    og = cx.dscratch("og", [4 * 4 * (HB // 2), TOK], BF16)
    oTsel = cx.dscratch("oTsel", [D, TOK], BF16)
    ident = cst["ident"]
    RG = [[0, 1, 2, 3], [4, 5, 6, 7]]
    dyn = {}

    def beta(h):
        return (h.partition_id() // 2) % 2

    def half(h):
        return h.partition_id() % 2

    def dval(h, name, fn, hi):
        if (name, id(h)) not in dyn:
            dyn[(name, id(h))] = h.snap(fn(h), min_val=0, max_val=hi)
        return dyn[(name, id(h))]

    def cc(src, dst):
        P.add("pool", lambda h: h.collective_compute("AllGather", ALU.bypass, replica_groups=RG, ins=[src], outs=[dst]),
              dma=True, key="cc", cc=True)

    def exchange_x():
        toks = []
        for k in range(4):
            gb = Buf(f"xg{k}")
            P.add("pool", lambda h, k=k: h.collective_compute("AllGather", ALU.bypass, replica_groups=RG, ins=[x1b[k * 256:(k + 1) * 256, :]], outs=[xg[k]]),
                  writes=(gb,), dma=True, key="cc", cc=True)
            pb = Buf(f"xsel{k}")
            P.add("sp", lambda h, k=k: [h.dma_start(out=xsel[a * TOK + k * 256:a * TOK + (k + 1) * 256, :],
                                                    in_=xg[k][bass.ds(dval(h, "xrow", lambda e: beta(e) * 512, 512), 512), :][a * 256:(a + 1) * 256, :])
                                        for a in range(2)], reads=(gb,), writes=(pb,), dma=True, key="pickx", ndma=2)
            toks.append(pb)
        return toks

    def exchange_o():
        toks = []
        for t in range(2):
            for f2 in range(2):
                k = t * 2 + f2
                gb = Buf(f"og{k}")
                toks.append(gb)
                P.add("pool", lambda h, t=t, f2=f2, k=k: h.collective_compute("AllGather", ALU.bypass, replica_groups=RG,
                                                                               ins=[oTloc[t, f2 * 512:(f2 + 1) * 512, :]], outs=[og[k * 2048:(k + 1) * 2048, :]]),
                      writes=(gb,), dma=True, key="cc", cc=True)
        out = []
        for hh in range(2):
            for f2 in range(2):
                pb = Buf(f"osel{hh}{f2}")
                P.add("act", lambda h, hh=hh, f2=f2: h.dma_start(
                    out=oTsel[hh * HB + f2 * 512:hh * HB + (f2 + 1) * 512, :],
                    in_=og[bass.ds(dval(h, "orow", lambda e: half(e) * 4096 + beta(e) * 1024, 5120), 3072), :][f2 * 2048 + hh * 512:f2 * 2048 + (hh + 1) * 512, :]),
                    reads=tuple(toks), writes=(pb,), dma=True, key="picko")
                out.append(pb)
        return out

    cx.begin_stage()
    emit_A(cx, x, x1loc, L[0]["f1g"], L[0]["f1u"], L[0]["f1d"], L[0]["lng"], L[0]["lnb"], ident, yb=x1b)
    cx.end_stage()
    for i in range(DEPTH):
        w = L[i]
        cx.begin_stage()
        A = {"x": xsel, "pos": pos, "out": oTloc, "nhh": 1, "out_split": True, "x_cast": True, "out_cast": True, "pre": exchange_x,
             "row_of": lambda kind, hh, j: {"d": j * 128, "g": 384 + j * 128, "m": 640 + j * 128}[kind]}
        A.update({k: w[k] for k in ("wsel2", "dlam", "lamc", "dng", "ggu2", "ggb2", "gng")})
        A.update(cst)
        emit_mix(cx, A)
        cx.end_stage()
        cx.begin_stage()
        if i + 1 < DEPTH:
            n = L[i + 1]
            nxt = (n["f1g"], n["f1u"], n["f1d"], n["lng"], n["lnb"])
            emit_C(cx, x1loc, oTsel, w["p"], w["wo"], w["f2g"], w["f2u"], w["f2d"], w["wpe"], w["wpg"], w["lng"], w["lnb"], ident, x1loc, nxt, yb=x1b, pre=exchange_o)
        else:
            emit_C(cx, x1loc, oTsel, w["p"], w["wo"], w["f2g"], w["f2u"], w["f2d"], w["wpe"], w["wpg"], w["lng"], w["lnb"], ident, y, None, pre=exchange_o)
        cx.end_stage()
    return cx.finish()


def _wo_rows():
    rows = []
    for hh in range(2):
        for j in range(3):
            rows += list(range((3 * hh + j) * 128, (3 * hh + j + 1) * 128))
        for j in range(2):
            rows += list(range(768 + (2 * hh + j) * 128, 768 + (2 * hh + j + 1) * 128))
        for j in range(3):
            rows += list(range(1280 + (3 * hh + j) * 128, 1280 + (3 * hh + j + 1) * 128))
    return np.array(rows)


def kernel(**inputs):
    f = lambda k: np.ascontiguousarray(np.asarray(inputs[k]), dtype=np.float32)
    x = f("x")
    p = f("p")
    positions = np.ascontiguousarray(np.asarray(inputs["positions"]).astype(np.int32))
    w_in, w_out = f("w_in"), f("w_out")
    dlam, dng = f("diff_lambda"), f("diff_norm_g")
    ggu, ggb, gng = f("gla_gate_up"), f("gla_gate_b"), f("gla_norm_g")
    f1g, f1u, f1d = f("ffn1_gate"), f("ffn1_up"), f("ffn1_down")
    f2g, f2u, f2d = f("ffn2_gate"), f("ffn2_up"), f("ffn2_down")
    wpe, wpg = f("w_pe"), f("w_pg")
    lng, lnb = f("ln_g"), f("ln_b")
    shared = dict(mix_consts())
    per_hh = [dict(), dict()]
    cols = [_wsel_cols(hh) for hh in range(2)]
    wor = _wo_rows()
    for i in range(DEPTH):
        lam_init = 0.8 - 0.6 * math.exp(-0.3 * i)
        shared.update({
            f"wo{i}": np.ascontiguousarray(w_out[i][wor, :]), f"f1g{i}": f1g[i], f"f1u{i}": f1u[i], f"f1d{i}": f1d[i],
            f"f2g{i}": f2g[i], f"f2u{i}": f2u[i], f"f2d{i}": f2d[i], f"wpe{i}": wpe[i], f"wpg{i}": wpg[i],
            f"lng{i}": lng[i], f"lnb{i}": lnb[i], f"dlam{i}": np.ascontiguousarray(dlam[i].reshape(1, 256)),
            f"lamc{i}": np.array([[lam_init, 1.0 - lam_init]], np.float32), f"dng{i}": np.ascontiguousarray(dng[i].reshape(128, 1)),
            f"gng{i}": np.ascontiguousarray(gng[i].reshape(128, 1)),
        })
        for hh in range(2):
            per_hh[hh].update({
                f"wsel{i}": np.ascontiguousarray(w_in[i][:, cols[hh]])[None],
                f"ggu{i}": np.ascontiguousarray(ggu[i][:, hh * 128:(hh + 1) * 128])[None],
                f"ggb{i}": np.ascontiguousarray(ggb[i][hh * 128:(hh + 1) * 128].reshape(1, 1, 128)),
            })
    in_maps = []
    for c in range(NCORES):
        b, h = c // 2, c % 2
        m = dict(shared)
        m.update(per_hh[h])
        m["x"] = np.ascontiguousarray(x[b, h * TOK:(h + 1) * TOK])
        m["pos"] = np.ascontiguousarray(positions[b].reshape(1, SEQ))
        for i in range(DEPTH):
            m[f"p{i}"] = np.ascontiguousarray(p[i, b, h * TOK:(h + 1) * TOK])
        in_maps.append(m)
    if "fused" not in _CACHE:
        _CACHE["fused"] = build_fused()
    res = run_bass_kernel_spmd(_CACHE["fused"], in_maps, core_ids=list(range(NCORES))).results
    out = np.concatenate([res[c]["y"] for c in range(NCORES)], axis=0)
    return out.reshape(NB, SEQ, D).astype(np.float32)
```
